# Optimizing a Trainium2 kernel written in Bass

```python
import math
import jax
import jax.numpy as jnp
from jax import lax
import numpy as np

D_MODEL = 1024
BATCH = 8
SEQ = 4096
DEPTH = 2

CHUNK = 64
D_MIX = D_MODEL
N_MIXERS = 4
GROUP_W = D_MIX // N_MIXERS

S5_CH = 16
S5_G = GROUP_W // S5_CH
S5_P = 64

RW_HD = 64
RW_H = GROUP_W // RW_HD
RW_W_LORA = 64
RW_A_LORA = 64
RW_G_LORA = 128
RW_IN = 3 * GROUP_W + RW_W_LORA + RW_A_LORA + RW_G_LORA
RW_GN_EPS = 64e-5

SB_HD = 64
SB_H = GROUP_W // SB_HD
SB_QBLOCK = 128

ML_HD = 64
ML_H = GROUP_W // ML_HD
ML_CONV = 4

D_FF = 2816
FFN_CONV = 3

LN_EPS = 1e-5
DN_ALPHA = (2 * DEPTH) ** 0.25
DN_BETA = (8 * DEPTH) ** -0.25

IN_SIZES = (GROUP_W, RW_IN, GROUP_W, GROUP_W, GROUP_W, GROUP_W, GROUP_W, GROUP_W, GROUP_W, ML_H, ML_H)
N_IN = sum(IN_SIZES)
RW_SIZES = (GROUP_W, GROUP_W, GROUP_W, RW_W_LORA, RW_A_LORA, RW_G_LORA)

kernel_name = 'hybrid_s5_rwkv7_stickbreak_mlstm_convffn'


def split_cols(z, sizes):
    cuts = [int(c) for c in np.cumsum(sizes)[:-1]]
    return jnp.split(z, cuts, axis=-1)


def layer_norm(x, g, b):
    xf = x.astype(jnp.float32)
    mu = jnp.mean(xf, -1, keepdims=True)
    var = jnp.mean(jnp.square(xf - mu), -1, keepdims=True)
    return ((xf - mu) * lax.rsqrt(var + LN_EPS) * g + b).astype(x.dtype)


def head_norm(x, g, b, eps):
    xf = x.astype(jnp.float32)
    mu = jnp.mean(xf, -1, keepdims=True)
    var = jnp.mean(jnp.square(xf - mu), -1, keepdims=True)
    y = (xf - mu) * lax.rsqrt(var + eps) * g.reshape(x.shape[-2:])
    if b is not None:
        y = y + b.reshape(x.shape[-2:])
    return y


def token_shift(z):
    return jnp.pad(z, ((0, 0), (1, 0), (0, 0)))[:, :-1]


def causal_dwconv(x, w, b):
    k_w, ch = w.shape
    xp = jnp.pad(x, ((0, 0), (k_w - 1, 0), (0, 0)))
    y = lax.conv_general_dilated(xp, w[:, None, :].astype(x.dtype), window_strides=(1,), padding='VALID',
                                 dimension_numbers=('NWC', 'WIO', 'NWC'), feature_group_count=ch)
    return y + b.astype(x.dtype)


def s5_mixer(u, lam_re, lam_im, log_dt, b_re, b_im, c_re, c_im, d, w_glu, b_glu):
    f32 = jnp.float32
    bsz, seq, _ = u.shape
    uf = u.astype(f32).reshape(bsz, seq, S5_G, S5_CH)
    lr, li = lam_re.astype(f32), lam_im.astype(f32)
    dt = jnp.exp(log_dt.astype(f32))[:, None]
    mag = jnp.exp(lr * dt)
    ab_re, ab_im = mag * jnp.cos(li * dt), mag * jnp.sin(li * dt)
    den = lr * lr + li * li
    z_re = ((ab_re - 1.0) * lr + ab_im * li) / den
    z_im = (ab_im * lr - (ab_re - 1.0) * li) / den
    br, bi = b_re.astype(f32), b_im.astype(f32)
    bb_re = z_re[..., None] * br - z_im[..., None] * bi
    bb_im = z_re[..., None] * bi + z_im[..., None] * br
    bu_re = jnp.einsum('bsgc,gpc->sbgp', uf, bb_re)
    bu_im = jnp.einsum('bsgc,gpc->sbgp', uf, bb_im)
    a_re = jnp.broadcast_to(ab_re, (seq, 1, S5_G, S5_P))
    a_im = jnp.broadcast_to(ab_im, (seq, 1, S5_G, S5_P))

    def combine(e1, e2):
        ar1, ai1, xr1, xi1 = e1
        ar2, ai2, xr2, xi2 = e2
        return (ar1 * ar2 - ai1 * ai2, ar1 * ai2 + ai1 * ar2,
                ar2 * xr1 - ai2 * xi1 + xr2, ar2 * xi1 + ai2 * xr1 + xi2)

    _, _, s_re, s_im = lax.associative_scan(combine, (a_re, a_im, bu_re, bu_im), axis=0)
    y = (jnp.einsum('sbgp,gcp->bsgc', s_re, c_re.astype(f32))
         - jnp.einsum('sbgp,gcp->bsgc', s_im, c_im.astype(f32)))
    y = (y + d.astype(f32).reshape(S5_G, S5_CH) * uf).reshape(bsz, seq, GROUP_W)
    y = jax.nn.gelu(y)
    return y * jax.nn.sigmoid(y @ w_glu.astype(f32) + b_glu.astype(f32))


def rwkv7_mixer(z, mu, w0, w2, a0, a2, g2, k_k, k_a, r_k, ln_g, ln_b):
    f32 = jnp.float32
    bsz, seq, _ = z.shape
    z = z.astype(f32)
    z = z + (token_shift(z) - z) * mu.astype(f32)
    r, k, v, xw, xa, xg = split_cols(z, RW_SIZES)
    w = -jax.nn.softplus(-(w0 + jnp.tanh(xw) @ w2)) - 0.5
    decay = jnp.exp(-jnp.exp(w))
    a = jax.nn.sigmoid(a0 + xa @ a2)
    g = jax.nn.sigmoid(xg) @ g2

    def heads(t):
        return t.reshape(bsz, seq, RW_H, RW_HD)

    r, k, v, decay, a = heads(r), heads(k), heads(v), heads(decay), heads(a)
    kk = k * k_k.reshape(RW_H, RW_HD)
    kk = kk / jnp.maximum(jnp.sqrt(jnp.sum(kk * kk, -1, keepdims=True)), 1e-12)
    k = k * (1.0 + (a - 1.0) * k_a.reshape(RW_H, RW_HD))

    def step(state, inp):
        r_t, w_t, k_t, v_t, kk_t, a_t = inp
        sa = jnp.einsum('bhvk,bhk->bhv', state, -kk_t)
        state = (state * w_t[:, :, None, :] + sa[..., None] * (kk_t * a_t)[:, :, None, :]
                 + v_t[..., None] * k_t[:, :, None, :])
        return state, jnp.einsum('bhvk,bhk->bhv', state, r_t)

    xs = tuple(jnp.moveaxis(t, 1, 0) for t in (r, decay, k, v, kk, a))
    state0 = jnp.zeros((bsz, RW_H, RW_HD, RW_HD), f32)
    _, y = lax.scan(step, state0, xs)
    y = jnp.moveaxis(y, 0, 1)
    y = head_norm(y, ln_g, ln_b, RW_GN_EPS)
    y = y + jnp.sum(r * k * r_k, -1, keepdims=True) * v
    return y.reshape(bsz, seq, GROUP_W) * g


def stick_breaking_attention(q, k, v):
    f32 = jnp.float32
    bsz, seq, _ = q.shape

    def heads(t):
        return t.astype(f32).reshape(bsz, seq, SB_H, SB_HD).transpose(0, 2, 1, 3)

    qh, kh, vh = heads(q) * (SB_HD ** -0.5), heads(k), heads(v)
    key_pos = jnp.arange(seq)

    def block(i):
        qb = lax.dynamic_slice_in_dim(qh, i * SB_QBLOCK, SB_QBLOCK, axis=2)
        logits = jnp.einsum('bhqd,bhkd->bhqk', qb, kh)
        q_pos = i * SB_QBLOCK + jnp.arange(SB_QBLOCK)
        mask = key_pos[None, :] < q_pos[:, None]
        log_stay = jnp.where(mask, jax.nn.log_sigmoid(-logits), 0.0)
        log_after = lax.cumsum(log_stay, axis=3, reverse=True) - log_stay
        weights = jnp.where(mask, jnp.exp(jax.nn.log_sigmoid(logits) + log_after), 0.0)
        return jnp.einsum('bhqk,bhkd->bhqd', weights, vh)

    out = lax.map(block, jnp.arange(seq // SB_QBLOCK))
    return out.transpose(1, 0, 3, 2, 4).reshape(bsz, seq, GROUP_W)


def mlstm_chunkwise(q, k, v, ig, fg):
    bsz, seq, nh, hd = q.shape
    nc = seq // CHUNK

    def chunks(t):
        t = t.reshape((bsz, nc, CHUNK, nh) + t.shape[3:])
        return jnp.moveaxis(t, 3, 1)

    q, k, v = chunks(q), chunks(k), chunks(v)
    ig = chunks(ig)
    b = jnp.cumsum(chunks(jax.nn.log_sigmoid(fg)), axis=-1)
    b_last = b[..., -1]
    g = b_last[..., None] - b + ig
    m_loc = jnp.max(g, -1)
    wk = jnp.exp(g - m_loc[..., None])
    c_loc = jnp.einsum('bhcl,bhclv,bhclk->bhcvk', wk, v, k)
    n_loc = jnp.einsum('bhcl,bhclk->bhck', wk, k)

    def step(carry, inp):
        c_st, n_st, m_st = carry
        cl, nl, ml, bl = inp
        m_new = jnp.maximum(bl + m_st, ml)
        s_old = jnp.exp(bl + m_st - m_new)
        s_new = jnp.exp(ml - m_new)
        c_new = s_old[..., None, None] * c_st + s_new[..., None, None] * cl
        n_new = s_old[..., None] * n_st + s_new[..., None] * nl
        return (c_new, n_new, m_new), (c_st, n_st, m_st)

    init = (jnp.zeros((bsz, nh, hd, hd), jnp.float32), jnp.zeros((bsz, nh, hd), jnp.float32),
            jnp.full((bsz, nh), -jnp.inf, jnp.float32))
    xs = tuple(jnp.moveaxis(t, 2, 0) for t in (c_loc, n_loc, m_loc, b_last))
    _, (c_prev, n_prev, m_prev) = lax.scan(step, init, xs)
    c_prev = jnp.moveaxis(c_prev, 0, 2)
    n_prev = jnp.moveaxis(n_prev, 0, 2)
    m_prev = jnp.moveaxis(m_prev, 0, 2)

    causal = jnp.tril(jnp.ones((CHUNK, CHUNK), bool))
    log_intra = jnp.where(causal, b[..., :, None] - b[..., None, :] + ig[..., None, :], -jnp.inf)
    log_inter = b + m_prev[..., None]
    m_t = jnp.maximum(log_inter, jnp.max(log_intra, -1))
    w_intra = jnp.exp(log_intra - m_t[..., None])
    w_inter = jnp.exp(log_inter - m_t)
    s = jnp.einsum('bhcld,bhcsd->bhcls', q, k) * w_intra
    num = (w_inter[..., None] * jnp.einsum('bhcvk,bhclk->bhclv', c_prev, q)
           + jnp.einsum('bhcls,bhcsv->bhclv', s, v))
    den = w_inter * jnp.einsum('bhck,bhclk->bhcl', n_prev, q) + jnp.sum(s, -1)
    h = num / jnp.maximum(jnp.abs(den), jnp.exp(-m_t))[..., None]
    return jnp.moveaxis(h, 1, 3).reshape(bsz, seq, nh, hd)


def mlstm_mixer(q, k, v, o, i_pre, f_pre, conv_w, conv_b, b_i, b_f, ln_g):
    f32 = jnp.float32
    bsz, seq, _ = q.shape
    qk = jax.nn.silu(causal_dwconv(jnp.concatenate([q, k], -1).astype(f32), conv_w, conv_b))
    q, k = qk[..., :GROUP_W], qk[..., GROUP_W:]

    def heads(t):
        return t.reshape(bsz, seq, ML_H, ML_HD)

    h = mlstm_chunkwise(heads(q), heads(k) * (ML_HD ** -0.5), heads(v.astype(f32)),
                        i_pre.astype(f32) + b_i.astype(f32), f_pre.astype(f32) + b_f.astype(f32))
    h = head_norm(h, ln_g, None, LN_EPS).reshape(bsz, seq, GROUP_W)
    return jax.nn.sigmoid(o.astype(f32)) * h


def conv_ffn(h, w_up, conv_w, conv_b, w_down):
    up = h @ w_up
    u, gate = jnp.split(up, 2, axis=-1)
    u = jax.nn.gelu(causal_dwconv(u, conv_w, conv_b))
    return (u * gate) @ w_down


def setup_inputs(seed: int = 0) -> dict:
    key = jax.random.key(seed)
    keys = iter(jax.random.split(key, 48))
    f32 = jnp.float32
    L = DEPTH

    def nrm(shape, scale):
        return scale * jax.random.normal(next(keys), shape, f32)

    n_idx = jnp.arange(S5_P, dtype=f32)
    ratio = jnp.arange(GROUP_W, dtype=f32) / (GROUP_W - 1)
    return {
        'x': nrm((BATCH, SEQ, D_MODEL), 1.0),
        'ln_in_g': 1.0 + nrm((D_MODEL,), 0.02),
        'ln_in_b': nrm((D_MODEL,), 0.02),
        'w_in': nrm((L, D_MODEL, N_IN), D_MODEL ** -0.5),
        's5_lambda_re': -0.5 + nrm((L, S5_G, S5_P), 0.01),
        's5_lambda_im': math.pi * n_idx + nrm((L, S5_G, S5_P), 0.01),
        's5_log_dt': jax.random.uniform(next(keys), (L, S5_G), f32, math.log(1e-3), math.log(1e-1)),
        's5_b_re': nrm((L, S5_G, S5_P, S5_CH), (2 * S5_CH) ** -0.5),
        's5_b_im': nrm((L, S5_G, S5_P, S5_CH), (2 * S5_CH) ** -0.5),
        's5_c_re': nrm((L, S5_G, S5_CH, S5_P), S5_P ** -0.5),
        's5_c_im': nrm((L, S5_G, S5_CH, S5_P), S5_P ** -0.5),
        's5_d': nrm((L, GROUP_W), 1.0),
        's5_w_glu': nrm((L, GROUP_W, GROUP_W), GROUP_W ** -0.5),
        's5_b_glu': nrm((L, GROUP_W), 0.02),
        'rw_mu': jax.random.uniform(next(keys), (L, RW_IN), f32),
        'rw_w0': (-6.5 + 5.0 * ratio ** 0.85) + nrm((L, GROUP_W), 0.1),
        'rw_w2': nrm((L, RW_W_LORA, GROUP_W), 0.5 * RW_W_LORA ** -0.5),
        'rw_a0': nrm((L, GROUP_W), 0.1),
        'rw_a2': nrm((L, RW_A_LORA, GROUP_W), RW_A_LORA ** -0.5),
        'rw_g2': nrm((L, RW_G_LORA, GROUP_W), RW_G_LORA ** -0.5),
        'rw_k_k': 0.85 + nrm((L, GROUP_W), 0.02),
        'rw_k_a': 1.0 + nrm((L, GROUP_W), 0.02),
        'rw_r_k': nrm((L, RW_H, RW_HD), 0.1),
        'rw_ln_g': 1.0 + nrm((L, GROUP_W), 0.02),
        'rw_ln_b': nrm((L, GROUP_W), 0.02),
        'ml_conv_w': nrm((L, ML_CONV, 2 * GROUP_W), ML_CONV ** -0.5),
        'ml_conv_b': nrm((L, 2 * GROUP_W), 0.02),
        'ml_b_i': nrm((L, ML_H), 0.1),
        'ml_b_f': jnp.linspace(3.0, 6.0, ML_H, dtype=f32) + nrm((L, ML_H), 0.1),
        'ml_ln_g': 1.0 + nrm((L, GROUP_W), 0.02),
        'w_out': nrm((L, D_MIX, D_MODEL), DN_BETA * D_MIX ** -0.5),
        'ln1_g': 1.0 + nrm((L, D_MODEL), 0.02),
        'ln1_b': nrm((L, D_MODEL), 0.02),
        'ffn_w_up': nrm((L, D_MODEL, 2 * D_FF), D_MODEL ** -0.5),
        'ffn_conv_w': nrm((L, FFN_CONV, D_FF), FFN_CONV ** -0.5),
        'ffn_conv_b': nrm((L, D_FF), 0.02),
        'ffn_w_down': nrm((L, D_FF, D_MODEL), DN_BETA * D_FF ** -0.5),
        'ln2_g': 1.0 + nrm((L, D_MODEL), 0.02),
        'ln2_b': nrm((L, D_MODEL), 0.02),
    }


def reference(x, ln_in_g, ln_in_b, w_in,
              s5_lambda_re, s5_lambda_im, s5_log_dt, s5_b_re, s5_b_im, s5_c_re, s5_c_im, s5_d,
              s5_w_glu, s5_b_glu,
              rw_mu, rw_w0, rw_w2, rw_a0, rw_a2, rw_g2, rw_k_k, rw_k_a, rw_r_k, rw_ln_g, rw_ln_b,
              ml_conv_w, ml_conv_b, ml_b_i, ml_b_f, ml_ln_g,
              w_out, ln1_g, ln1_b,
              ffn_w_up, ffn_conv_w, ffn_conv_b, ffn_w_down, ln2_g, ln2_b):
    h = layer_norm(x, ln_in_g, ln_in_b)
    for l in range(DEPTH):
        z = h @ w_in[l]
        (z_s5, z_rw, sb_q, sb_k, sb_v, ml_q, ml_k, ml_v, ml_o, ml_i, ml_f) = split_cols(z, IN_SIZES)
        y_s5 = s5_mixer(z_s5, s5_lambda_re[l], s5_lambda_im[l], s5_log_dt[l], s5_b_re[l], s5_b_im[l],
                        s5_c_re[l], s5_c_im[l], s5_d[l], s5_w_glu[l], s5_b_glu[l])
        y_rw = rwkv7_mixer(z_rw, rw_mu[l], rw_w0[l], rw_w2[l], rw_a0[l], rw_a2[l], rw_g2[l],
                           rw_k_k[l], rw_k_a[l], rw_r_k[l], rw_ln_g[l], rw_ln_b[l])
        y_sb = stick_breaking_attention(sb_q, sb_k, sb_v)
        y_ml = mlstm_mixer(ml_q, ml_k, ml_v, ml_o, ml_i, ml_f, ml_conv_w[l], ml_conv_b[l],
                           ml_b_i[l], ml_b_f[l], ml_ln_g[l])
        y = jnp.concatenate([y_s5, y_rw, y_sb, y_ml], axis=-1).astype(h.dtype)
        h = layer_norm(DN_ALPHA * h + y @ w_out[l], ln1_g[l], ln1_b[l])
        h = layer_norm(DN_ALPHA * h + conv_ffn(h, ffn_w_up[l], ffn_conv_w[l], ffn_conv_b[l], ffn_w_down[l]),
                       ln2_g[l], ln2_b[l])
    return h
```

```python
import math
from contextlib import ExitStack
import numpy as np
import concourse.bass as bass
import concourse.mybir as mybir
from concourse.bass_utils import run_bass_kernel_spmd

F32 = mybir.dt.float32
BF16 = mybir.dt.bfloat16
AF = mybir.ActivationFunctionType
ALU = mybir.AluOpType
AX = mybir.AxisListType

S = 4096
D = 1024
NIN = 3080
DFF = 2816
DEPTH = 2
LN_EPS = 1e-5
DN_ALPHA = (2 * DEPTH) ** 0.25


class R:
    def __init__(self, ap, tag):
        self.ap = ap
        self.tag = tag


def _u(x):
    if isinstance(x, R):
        return x.ap, "%s#%s" % (x.ap.name, x.tag)
    return x, x.name


class Prog:
    NQ = 8

    def __init__(self, nc):
        self.nc = nc
        self.es = ExitStack()
        self.ops = {k: [] for k in ("pe", "dve", "act", "pool", "sp")}
        self.sem = {k: self.es.enter_context(nc.semaphore("s_" + k)) for k in ("pe", "dve", "act", "pool")}
        self.cnt = {k: 0 for k in ("pe", "dve", "act", "pool")}
        self.qsem = [self.es.enter_context(nc.semaphore("q%d" % i)) for i in range(self.NQ)]
        self.ndma = 0
        self.waited = {k: {} for k in self.ops}
        self.lastw = {}
        self.readers = {}
        self.stack = [self.es]
        self.pend = {}
        self.nuniq = 0

    def push(self):
        self.stack.append(ExitStack())

    def pop(self):
        self.barrier()
        self.stack.pop().close()

    def barrier(self):
        toks = [(self.sem[k], self.cnt[k]) for k in self.cnt if self.cnt[k] > 0]
        for j in range(min(self.NQ, self.ndma)):
            n_on = (self.ndma - j + self.NQ - 1) // self.NQ
            toks.append((self.qsem[j], 16 * n_on))
        for e in self.ops:
            self.pend[e] = list(toks)

    def sb(self, name, shape, dtype):
        self.nuniq += 1
        return self.stack[-1].enter_context(self.nc.sbuf_tensor("%s_u%d" % (name, self.nuniq), list(shape), dtype))

    def ps(self, name, shape, dtype=F32):
        return self.es.enter_context(self.nc.psum_tensor(name, list(shape), dtype))

    def dram(self, name, shape, dtype, kind="Internal"):
        return self.nc.dram_tensor(name, list(shape), dtype, kind=kind)

    def _deps(self, eng, reads, writes):
        deps = {}

        def add(tok):
            key = id(tok[0])
            if key not in deps or deps[key][1] < tok[1]:
                deps[key] = tok
        for tok in self.pend.pop(eng, []):
            add(tok)
        for b in reads:
            for tok in self.lastw.get(b, {}).values():
                add(tok)
        for b in writes:
            for tok in self.lastw.get(b, {}).values():
                add(tok)
            for tok in self.readers.get(b, {}).values():
                add(tok)
        waits = []
        w = self.waited[eng]
        for key, tok in deps.items():
            if eng == "pe" and tok[0] is self.sem["pe"]:
                continue
            if w.get(key, 0) >= tok[1]:
                continue
            w[key] = tok[1]
            waits.append(tok)
        return waits

    def _commit(self, tok, reads, writes):
        key = id(tok[0])
        for b in writes:
            self.lastw[b] = {key: tok}
            self.readers[b] = {}
        for b in reads:
            self.readers.setdefault(b, {})[key] = tok

    def op(self, eng, fn, reads, writes):
        reads = [_u(x)[1] for x in reads]
        writes = [_u(x)[1] for x in writes]
        waits = self._deps(eng, reads, writes)
        self.cnt[eng] += 1
        tok = (self.sem[eng], self.cnt[eng])
        self.ops[eng].append((fn, waits, tok, 1))
        self._commit(tok, reads, writes)

    def dma(self, out, in_, **kw):
        oap, ob = _u(out)
        iap, ib = _u(in_)
        kw.setdefault("allow_slow_non_contiguous", True)
        waits = self._deps("sp", [ib], [ob])
        i = self.ndma
        self.ndma += 1
        sem = self.qsem[i % self.NQ]
        if i >= self.NQ:
            prev = (sem, 16 * (i // self.NQ))
            key = id(sem)
            if self.waited["sp"].get(key, 0) < prev[1]:
                self.waited["sp"][key] = prev[1]
                waits.append(prev)
        tok = (sem, 16 * (i // self.NQ + 1))
        self.ops["sp"].append((lambda e: e.dma_start(out=oap, in_=iap, **kw), waits, tok, 16))
        self._commit(tok, [ib], [ob])

    def mm(self, out, lhsT, rhs, start=True, stop=True):
        o, l, r = _u(out)[0], _u(lhsT)[0], _u(rhs)[0]
        self.op("pe", lambda e: e.matmul(o, l, r, start=start, stop=stop), [lhsT, rhs], [out])

    def tr(self, out, in_, ident):
        o, i, d = _u(out)[0], _u(in_)[0], _u(ident)[0]
        self.op("pe", lambda e: e.transpose(o, i, d), [in_, ident], [out])

    def act(self, out, in_, func, bias=None, scale=None, accum_out=None, eng="act"):
        o, i = _u(out)[0], _u(in_)[0]
        kw = {}
        rd = [in_]
        if bias is not None:
            kw["bias"] = bias if isinstance(bias, (int, float)) else _u(bias)[0]
            if not isinstance(bias, (int, float)):
                rd.append(bias)
        if scale is not None:
            kw["scale"] = scale if isinstance(scale, (int, float)) else _u(scale)[0]
            if not isinstance(scale, (int, float)):
                rd.append(scale)
        wr = [out]
        if accum_out is not None:
            kw["accum_out"] = _u(accum_out)[0]
            wr.append(accum_out)
        self.op("act", lambda e: e.activation(o, i, func, **kw), rd, wr)

    def tt(self, out, in0, in1, op, eng="dve"):
        o, a, b = _u(out)[0], _u(in0)[0], _u(in1)[0]
        self.op(eng, lambda e: e.tensor_tensor(o, a, b, op), [in0, in1], [out])

    def ts(self, out, in0, s1, s2, op0, op1=None, eng="dve", accum_out=None):
        o, a = _u(out)[0], _u(in0)[0]
        rd = [in0]
        v1 = s1
        if not isinstance(s1, (int, float)):
            v1 = _u(s1)[0]
            rd.append(s1)
        v2 = s2
        if s2 is not None and not isinstance(s2, (int, float)):
            v2 = _u(s2)[0]
            rd.append(s2)
        wr = [out]
        kw = {}
        if accum_out is not None:
            kw["accum_out"] = _u(accum_out)[0]
            wr.append(accum_out)
        if op1 is None:
            self.op(eng, lambda e: e.tensor_scalar(o, a, v1, None, op0, **kw), rd, wr)
        else:
            self.op(eng, lambda e: e.tensor_scalar(o, a, v1, v2, op0, op1, **kw), rd, wr)

    def stt(self, out, in0, scalar, in1, op0, op1, eng="dve"):
        o, a, b = _u(out)[0], _u(in0)[0], _u(in1)[0]
        rd = [in0, in1]
        sv = scalar
        if not isinstance(scalar, (int, float)):
            sv = _u(scalar)[0]
            rd.append(scalar)
        self.op("dve", lambda e: e.scalar_tensor_tensor(o, a, sv, b, op0, op1), rd, [out])

    def cp(self, out, in_, eng="dve"):
        o, i = _u(out)[0], _u(in_)[0]
        if eng == "act":
            self.op("act", lambda e: e.copy(o, i), [in_], [out])
        else:
            self.op(eng, lambda e: e.tensor_copy(o, i), [in_], [out])

    def memset(self, out, val, eng="pool"):
        o = _u(out)[0]
        self.op(eng, lambda e: e.memset(o, val), [], [out])

    def scan(self, out, d0, d1, init, op0, op1):
        o, a, b = _u(out)[0], _u(d0)[0], _u(d1)[0]
        rd = [d0, d1]
        iv = init
        if not isinstance(init, (int, float)):
            iv = _u(init)[0]
            rd.append(init)
        self.op("dve", lambda e: e.tensor_tensor_scan(o, a, b, iv, op0, op1), rd, [out])

    def recip(self, out, in_):
        o, i = _u(out)[0], _u(in_)[0]
        self.op("dve", lambda e: e.reciprocal(o, i), [in_], [out])

    def asel(self, out, in_, pattern, cmp, fill, base, cm):
        o, i = _u(out)[0], _u(in_)[0]
        self.op("pool", lambda e: e.affine_select(o, i, pattern, cmp, fill, base=base, channel_multiplier=cm),
                [in_], [out])

    def emit(self):
        nc = self.nc
        fin = []
        for j in range(min(self.NQ, self.ndma)):
            n_on = (self.ndma - j + self.NQ - 1) // self.NQ
            fin.append((self.qsem[j], 16 * n_on))
        ops = self.ops
        sems = self.sem

        def run(e, lst):
            for fn, waits, tok, inc in lst:
                for (s, v) in waits:
                    e.wait_ge(s, v)
                fn(e).then_inc(tok[0], inc)

        with nc.Block() as block:
            @block.sync
            def _(e):
                run(e, ops["sp"])
                for (s, v) in fin:
                    e.wait_ge(s, v)

            @block.tensor
            def _(e):
                run(e, ops["pe"])

            @block.vector
            def _(e):
                run(e, ops["dve"])

            @block.scalar
            def _(e):
                run(e, ops["act"])

            @block.gpsimd
            def _(e):
                run(e, ops["pool"])
        self.es.close()


PI = math.pi
PARAMS = [
    ("ln_in_g", [D]), ("ln_in_b", [D]), ("w_in", [DEPTH, D, NIN]),
    ("s5_lambda_re", [DEPTH, 16, 64]), ("s5_lambda_im", [DEPTH, 16, 64]), ("s5_log_dt", [DEPTH, 16]),
    ("s5_b_re", [DEPTH, 16, 64, 16]), ("s5_b_im", [DEPTH, 16, 64, 16]),
    ("s5_c_re", [DEPTH, 16, 16, 64]), ("s5_c_im", [DEPTH, 16, 16, 64]), ("s5_d", [DEPTH, 256]),
    ("s5_w_glu", [DEPTH, 256, 256]), ("s5_b_glu", [DEPTH, 256]),
    ("rw_mu", [DEPTH, 1024]), ("rw_w0", [DEPTH, 256]), ("rw_w2", [DEPTH, 64, 256]), ("rw_a0", [DEPTH, 256]),
    ("rw_a2", [DEPTH, 64, 256]), ("rw_g2", [DEPTH, 128, 256]), ("rw_k_k", [DEPTH, 256]), ("rw_k_a", [DEPTH, 256]),
    ("rw_r_k", [DEPTH, 4, 64]), ("rw_ln_g", [DEPTH, 256]), ("rw_ln_b", [DEPTH, 256]),
    ("ml_conv_w", [DEPTH, 4, 512]), ("ml_conv_b", [DEPTH, 512]), ("ml_b_i", [DEPTH, 4]), ("ml_b_f", [DEPTH, 4]),
    ("ml_ln_g", [DEPTH, 256]), ("w_out", [DEPTH, D, D]), ("ln1_g", [DEPTH, D]), ("ln1_b", [DEPTH, D]),
    ("ffn_w_up", [DEPTH, D, 2 * DFF]), ("ffn_conv_w", [DEPTH, 3, DFF]), ("ffn_conv_b", [DEPTH, DFF]),
    ("ffn_w_down", [DEPTH, DFF, D]), ("ln2_g", [DEPTH, D]), ("ln2_b", [DEPTH, D]),
]


nc_out_handle = [None]


def build_program(dbg=None, nlayers=DEPTH, mixers=("s5", "rw", "sb", "ml")):
    nc = bass.Bass("TRN2", target_bir_lowering=False)
    P = Prog(nc)
    x_in = nc.dram_tensor("x", [S, D], F32, kind="ExternalInput")
    prm = {n: nc.dram_tensor(n, shp, F32, kind="ExternalInput") for n, shp in PARAMS}
    out = nc.dram_tensor("out", [S, D], F32, kind="ExternalOutput")
    nc_out_handle[0] = out

    hT = P.dram("hT", [D, S], F32)
    h1T = P.dram("h1T", [D, S], F32)
    zT = P.dram("zT", [25 * 128, S], F32)
    yT = P.dram("yT", [D, S], BF16)
    vTM = P.dram("vTM", [S, 512], BF16)
    vrw = P.dram("vrw", [S, 256], F32)
    grw = P.dram("grw", [S, 256], F32)
    yrw = P.dram("yrw", [S, 256], F32)

    ident = P.sb("ident", [128, 128], F32)
    P.memset(ident[:], 1.0)
    P.asel(ident[:], ident[:], [[-1, 128]], ALU.is_equal, 0.0, 0, 1)
    identb = P.sb("identb", [128, 128], BF16)
    P.cp(identb[:], ident[:])
    onesm = P.sb("onesm", [128, 128], F32)
    P.memset(onesm[:], 1.0 / D)
    ones1 = P.sb("ones1", [128, 128], F32)
    P.memset(ones1[:], 1.0)
    psb = [P.ps("psb%d" % i, [128, 512], F32) for i in range(8)]
    rr = [0]

    def ev_eng():
        rr[0] += 1
        return "act" if rr[0] % 2 else "dve"

    def ln_block(L, src, dst_dram, tb, g, b):
        pm = psb[6]
        for k in range(8):
            P.mm(pm[:], onesm[:], src[:, k, :], start=(k == 0), stop=(k == 7))
        P.cp(L["mean"][:], pm[:], eng="act")
        for k in range(8):
            P.tt(src[:, k, :], src[:, k, :], L["mean"][:], ALU.subtract, eng=("dve" if k % 2 else "pool"))
        pv = psb[7]
        for k in range(8):
            sq = L["sq"][k % 2]
            P.act(sq[:], src[:, k, :], AF.Square)
            P.mm(pv[:], onesm[:], sq[:], start=(k == 0), stop=(k == 7))
        P.ts(L["rstd"][:], pv[:], LN_EPS, None, ALU.add)
        P.act(L["rstd"][:], L["rstd"][:], AF.Sqrt)
        P.recip(L["rstd"][:], L["rstd"][:])
        for k in range(8):
            P.tt(src[:, k, :], src[:, k, :], L["rstd"][:], ALU.mult, eng=("dve" if k % 2 else "pool"))
            P.ts(L["ho"][:, k, :], src[:, k, :], g[:, k:k + 1], b[:, k:k + 1], ALU.mult, ALU.add)
        P.dma(R(dst_dram[:, tb * 512:(tb + 1) * 512].rearrange("(k p) t -> p k t", p=128), tb), L["ho"][:])

    def ln_alloc():
        return {"mean": P.sb("ln_mean", [128, 512], F32), "rstd": P.sb("ln_rstd", [128, 512], F32),
                "sq": [P.sb("ln_sq%d" % i, [128, 512], F32) for i in range(2)],
                "ho": P.sb("ln_out", [128, 8, 512], F32)}

    def load_cols(dst, vec_ap, n):
        P.dma(dst, vec_ap.rearrange("(k p) -> p k", p=128), allow_slow_non_contiguous=True)

    P.push()
    xt = [P.sb("xt%d" % i, [128, D], F32) for i in range(2)]
    xT2 = [P.sb("xTblk%d" % i, [128, 8, 512], F32) for i in range(2)]
    gcol = P.sb("gcol", [128, 8], F32)
    bcol = P.sb("bcol", [128, 8], F32)
    load_cols(gcol[:], prm["ln_in_g"][:], 8)
    load_cols(bcol[:], prm["ln_in_b"][:], 8)
    L = ln_alloc()
    for tb in range(9):
        if tb < 8:
            xT = xT2[tb % 2]
            for j in range(4):
                tt_ = tb * 4 + j
                xb = xt[tt_ % 2]
                P.dma(xb[:], x_in[tt_ * 128:(tt_ + 1) * 128, :])
                for k in range(8):
                    pt = psb[k % 4]
                    P.tr(pt[:, 0:128], xb[:, k * 128:(k + 1) * 128], ident[:])
                    P.cp(xT[:, k, j * 128:(j + 1) * 128], pt[:, 0:128], eng=ev_eng())
        if tb >= 1:
            ln_block(L, xT2[(tb - 1) % 2], hT, tb - 1, gcol, bcol)
    P.pop()

    for l in range(nlayers):
        phase_A(P, nc, prm, l, hT, zT, vTM, vrw, psb, ev_eng)
        if "s5" in mixers:
            phase_s5(P, nc, prm, l, zT, yT, psb, ident, ev_eng)
        if "sb" in mixers:
            phase_sb(P, nc, prm, l, zT, yT, vTM, psb, ones1, ev_eng)
        if "ml" in mixers:
            phase_ml(P, nc, prm, l, zT, yT, vTM, psb, ident, ev_eng)
        if "rw" in mixers:
            phase_rw(P, nc, prm, l, zT, yT, vrw, grw, yrw, psb, ident, identb, ev_eng)
        if dbg == "mix":
            break
        phase_proj_ln(P, nc, l, prm["w_out"][l], 8, yT, True, hT, h1T, prm["ln1_g"][l], prm["ln1_b"][l],
                      psb, ln_alloc, ln_block, load_cols, ev_eng)
        phase_ffn(P, nc, prm, l, h1T, hT, psb, ln_alloc, ln_block, load_cols, ev_eng)

    P.push()
    import os
    if os.environ.get("DUMPT"):
        pass
    elif dbg == "mix" and os.environ.get("DUMP"):
        src = {"yrw": yrw, "vrw": vrw, "grw": grw}[os.environ["DUMP"]]
        stf = P.sb("stgf2", [128, 32, 256], F32)
        P.dma(stf[:], src[:, :].rearrange("(n p) c -> p n c", p=128))
        P.dma(out[:, 0:256].rearrange("(n p) c -> p n c", p=128), stf[:])
    elif dbg == "mix":
        stg = P.sb("stgb", [128, 8, 512], BF16)
        stf = P.sb("stgf", [128, 8, 512], F32)
        ov = out[:, :].rearrange("(a b) d -> a (b d)", a=D)
        for tb in range(8):
            P.dma(stg[:], yT[:, tb * 512:(tb + 1) * 512].rearrange("(k p) t -> p k t", p=128))
            P.cp(stf[:], stg[:])
            P.dma(R(ov[:, tb * 512:(tb + 1) * 512].rearrange("(k p) t -> p k t", p=128), tb), stf[:])
    else:
        hb = [P.sb("fin_h%d" % i, [128, 8, 512], F32) for i in range(2)]
        ob = [P.sb("fin_o%d" % i, [128, D], F32) for i in range(2)]
        n = 0
        for tb in range(8):
            hbb = hb[tb % 2]
            P.dma(hbb[:], hT[:, tb * 512:(tb + 1) * 512].rearrange("(k p) t -> p k t", p=128))
            for j in range(4):
                o = ob[n % 2]
                n += 1
                for k in range(8):
                    pt = psb[k % 4]
                    P.tr(pt[:, 0:128], hbb[:, k, j * 128:(j + 1) * 128], ident[:])
                    P.cp(o[:, k * 128:(k + 1) * 128], pt[:, 0:128], eng=ev_eng())
                tt_ = tb * 4 + j
                P.dma(R(out[tt_ * 128:(tt_ + 1) * 128, :], tt_), o[:])
    P.pop()
    P.emit()
    return nc


def load_cast(P, dst_bf, src_dram_ap, stage, eng):
    P.dma(stage, src_dram_ap)
    P.cp(dst_bf, stage, eng=eng)


def phase_A(P, nc, prm, l, hT, zT, vTM, vrw, psb, ev_eng):
    P.push()
    wA = P.sb("wA", [128, 8, NIN], BF16)
    hb = P.sb("hTb", [128, 8, S + 1], BF16)
    wst = [P.sb("wAst%d" % i, [128, NIN], F32) for i in range(2)]
    hst = [P.sb("hst%d" % i, [128, 2048], F32) for i in range(2)]
    zo = [P.sb("zo%d" % i, [128, 512], F32) for i in range(3)]
    vo = [P.sb("vo%d" % i, [128, 512], BF16) for i in range(2)]
    vo2 = [P.sb("vo2%d" % i, [128, 256], F32) for i in range(2)]
    wv1 = P.sb("wv1", [128, 8, 256], BF16)
    wv2 = P.sb("wv2", [128, 8, 256], BF16)
    mur = P.sb("mur", [128, 256], F32)
    tmpf = P.sb("wvtmp", [128, 256], F32)
    tmpg = P.sb("wvtmp2", [128, 256], F32)
    P.dma(mur[:], prm["rw_mu"][l, 512:768].partition_broadcast(128))
    P.memset(hb[:, :, 0:1], 0.0)
    n = 0
    for k in range(8):
        st = wst[k % 2]
        P.dma(st[:], prm["w_in"][l, k * 128:(k + 1) * 128, :])
        P.cp(wA[:, k, :], st[:], eng=("act" if k % 2 else "dve"))
        P.tt(tmpf[:], st[:, 768:1024], mur[:], ALU.mult)
        P.cp(wv2[:, k, :], tmpf[:], eng="pool")
        P.tt(tmpg[:], st[:, 768:1024], tmpf[:], ALU.subtract)
        P.cp(wv1[:, k, :], tmpg[:], eng="pool")
        for hh in range(2):
            s2 = hst[n % 2]
            n += 1
            P.dma(s2[:], hT[k * 128:(k + 1) * 128, hh * 2048:(hh + 1) * 2048])
            P.cp(hb[:, k, 1 + hh * 2048:1 + (hh + 1) * 2048], s2[:], eng=("act" if n % 2 else "dve"))
    n = 0
    for m in range(25):
        msz = min(128, NIN - m * 128)
        for tb in range(8):
            ps = psb[n % 4]
            for k in range(8):
                P.mm(ps[0:msz, :], wA[:, k, m * 128:m * 128 + msz], hb[:, k, 1 + tb * 512:1 + (tb + 1) * 512],
                     start=(k == 0), stop=(k == 7))
            o = zo[n % 3]
            P.cp(o[0:msz, :], ps[0:msz, :], eng=ev_eng())
            P.dma(R(zT[m * 128:m * 128 + msz, tb * 512:(tb + 1) * 512], "%d_%d" % (m, tb)), o[0:msz, :])
            n += 1
    for tt_ in range(32):
        ps = psb[4 + tt_ % 2]
        lo = 1 + tt_ * 128
        for (c0, o0) in ((1792, 0), (2560, 256)):
            for k in range(8):
                P.mm(ps[:, o0:o0 + 256], hb[:, k, lo:lo + 128], wA[:, k, c0:c0 + 256], start=(k == 0), stop=(k == 7))
        o = vo[tt_ % 2]
        P.cp(o[:], ps[:], eng=ev_eng())
        P.dma(R(vTM[tt_ * 128:(tt_ + 1) * 128, :], tt_), o[:])
        ps2 = psb[6 + tt_ % 2]
        for k in range(8):
            P.mm(ps2[:, 0:256], hb[:, k, lo:lo + 128], wv1[:, k, :], start=(k == 0), stop=False)
            P.mm(ps2[:, 0:256], hb[:, k, lo - 1:lo + 127], wv2[:, k, :], start=False, stop=(k == 7))
        o2 = vo2[tt_ % 2]
        P.cp(o2[:], ps2[:, 0:256], eng=ev_eng())
        P.dma(R(vrw[tt_ * 128:(tt_ + 1) * 128, :], tt_), o2[:])
    P.pop()


def phase_proj_ln(P, nc, l, w_dram, nk, src, src_is_bf, resid, dst, g_ap, b_ap, psb, ln_alloc, ln_block, load_cols, ev_eng):
    P.push()
    w = P.sb("pw", [128, nk, D], BF16)
    st = [P.sb("pwst%d" % i, [128, D], F32) for i in range(2)]
    for k in range(nk):
        P.dma(st[k % 2][:], w_dram[k * 128:(k + 1) * 128, :])
        P.cp(w[:, k, :], st[k % 2][:], eng=("act" if k % 2 else "dve"))
    gcol = P.sb("pg", [128, 8], F32)
    bcol = P.sb("pb", [128, 8], F32)
    load_cols(gcol[:], g_ap, 8)
    load_cols(bcol[:], b_ap, 8)
    sb_ = [P.sb("psrc%d" % i, [128, nk, 512], BF16) for i in range(2)]
    res2 = [P.sb("pres%d" % i, [128, 8, 512], F32) for i in range(3)]
    L = ln_alloc()

    def loads(tb):
        P.dma(sb_[tb % 2][:], src[:, tb * 512:(tb + 1) * 512].rearrange("(k p) t -> p k t", p=128))
        P.dma(res2[tb % 3][:], resid[:, tb * 512:(tb + 1) * 512].rearrange("(k p) t -> p k t", p=128))
    loads(0)
    for tb in range(9):
        if tb < 8:
            res = res2[tb % 3]
            s_ = sb_[tb % 2]
            for m in range(8):
                ps = psb[m % 4]
                for k in range(nk):
                    P.mm(ps[:], w[:, k, m * 128:(m + 1) * 128], s_[:, k, :], start=(k == 0), stop=(k == nk - 1))
                P.stt(res[:, m, :], res[:, m, :], DN_ALPHA, ps[:], ALU.mult, ALU.add)
                if m == 3 and tb + 1 < 8:
                    loads(tb + 1)
        if tb >= 1:
            ln_block(L, res2[(tb - 1) % 3], dst, tb - 1, gcol, bcol)
    P.pop()


def phase_ffn(P, nc, prm, l, h1T, hT, psb, ln_alloc, ln_block, load_cols, ev_eng):
    aT = P.dram("aT%d" % l, [DFF, S], BF16)
    NF = DFF // 128
    P.push()
    wup = P.sb("wup", [128, 8, 2 * DFF], BF16)
    st = [P.sb("wupst%d" % i, [128, 1408], F32) for i in range(2)]
    n = 0
    for k in range(8):
        for q in range(4):
            s_ = st[n % 2]
            P.dma(s_[:], prm["ffn_w_up"][l, k * 128:(k + 1) * 128, q * 1408:(q + 1) * 1408])
            P.cp(wup[:, k, q * 1408:(q + 1) * 1408], s_[:], eng=("act" if n % 2 else "dve"))
            n += 1
    cw = P.sb("fcw", [128, NF, 3], F32)
    cb = P.sb("fcb", [128, NF], F32)
    for i in range(3):
        P.dma(cw[:, :, i], prm["ffn_conv_w"][l, i].rearrange("(f p) -> p f", p=128))
    P.dma(cb[:], prm["ffn_conv_b"][l].rearrange("(f p) -> p f", p=128), allow_slow_non_contiguous=True)
    uprev = P.sb("uprev", [128, NF, 2], F32)
    P.memset(uprev[:], 0.0)
    hst = [P.sb("fhst%d" % i, [128, 8, 512], F32) for i in range(2)]
    hb = [P.sb("fhb%d" % i, [128, 8, 512], BF16) for i in range(2)]
    ub = [P.sb("fub%d" % i, [128, 514], F32) for i in range(4)]
    acc = [P.sb("facc%d" % i, [128, 512], F32) for i in range(4)]
    t1 = [P.sb("ft1%d" % i, [128, 512], F32) for i in range(4)]
    ao = [P.sb("fao%d" % i, [128, 512], BF16) for i in range(4)]
    n = 0
    def load_h(tb):
        P.dma(hst[tb % 2][:], h1T[:, tb * 512:(tb + 1) * 512].rearrange("(k p) t -> p k t", p=128))
        P.cp(hb[tb % 2][:], hst[tb % 2][:], eng="dve")
    load_h(0)
    for tb in range(8):
        hs, hbb = hst[tb % 2], hb[tb % 2]
        for f in range(NF):
            if f == 2 and tb + 1 < 8:
                load_h(tb + 1)
            pu, pg = psb[(2 * n) % 8], psb[(2 * n + 1) % 8]
            for k in range(8):
                P.mm(pu[:], wup[:, k, f * 128:(f + 1) * 128], hbb[:, k, :], start=(k == 0), stop=(k == 7))
            for k in range(8):
                P.mm(pg[:], wup[:, k, DFF + f * 128:DFF + (f + 1) * 128], hbb[:, k, :], start=(k == 0), stop=(k == 7))
            u, a, t = ub[n % 4], acc[n % 4], t1[n % 4]
            P.cp(u[:, 0:2], uprev[:, f, :], eng="pool")
            P.cp(u[:, 2:514], pu[:], eng="act")
            P.cp(uprev[:, f, :], u[:, 512:514], eng="pool")
            P.ts(a[:], u[:, 2:514], cw[:, f, 2:3], cb[:, f:f + 1], ALU.mult, ALU.add)
            P.stt(a[:], u[:, 1:513], cw[:, f, 1:2], a[:], ALU.mult, ALU.add)
            P.stt(a[:], u[:, 0:512], cw[:, f, 0:1], a[:], ALU.mult, ALU.add)
            gelu(P, t[:], a[:], u[:, 0:512])
            o = ao[n % 4]
            P.tt(o[:], t[:], pg[:], ALU.mult)
            P.dma(R(aT[f * 128:(f + 1) * 128, tb * 512:(tb + 1) * 512], "%d_%d" % (f, tb)), o[:])
            n += 1
    P.pop()
    P.push()
    wdn = P.sb("wdn", [128, NF, D], BF16)
    st = [P.sb("wdnst%d" % i, [128, D], F32) for i in range(2)]
    for k in range(NF):
        P.dma(st[k % 2][:], prm["ffn_w_down"][l, k * 128:(k + 1) * 128, :])
        P.cp(wdn[:, k, :], st[k % 2][:], eng=("act" if k % 2 else "dve"))
    gcol = P.sb("fg", [128, 8], F32)
    bcol = P.sb("fb", [128, 8], F32)
    load_cols(gcol[:], prm["ln2_g"][l], 8)
    load_cols(bcol[:], prm["ln2_b"][l], 8)
    ab = [P.sb("fab%d" % i, [128, NF, 512], BF16) for i in range(2)]
    res2 = [P.sb("fres%d" % i, [128, 8, 512], F32) for i in range(3)]
    L = ln_alloc()

    def loads(tb):
        P.dma(ab[tb % 2][:], aT[:, tb * 512:(tb + 1) * 512].rearrange("(k p) t -> p k t", p=128))
        P.dma(res2[tb % 3][:], h1T[:, tb * 512:(tb + 1) * 512].rearrange("(k p) t -> p k t", p=128))
    loads(0)
    for tb in range(9):
        if tb < 8:
            res = res2[tb % 3]
            a_ = ab[tb % 2]
            for m in range(8):
                ps = psb[m % 4]
                for k in range(NF):
                    P.mm(ps[:], wdn[:, k, m * 128:(m + 1) * 128], a_[:, k, :], start=(k == 0), stop=(k == NF - 1))
                P.stt(res[:, m, :], res[:, m, :], DN_ALPHA, ps[:], ALU.mult, ALU.add)
                if m == 3 and tb + 1 < 8:
                    loads(tb + 1)
        if tb >= 1:
            ln_block(L, res2[(tb - 1) % 3], hT, tb - 1, gcol, bcol)
    P.pop()


def gelu(P, out, x, tmp, eng="dve"):
    P.tt(tmp, x, x, ALU.mult, eng="pool")
    P.ts(tmp, tmp, 0.044715, 1.0, ALU.mult, ALU.add, eng="pool")
    P.tt(tmp, tmp, x, ALU.mult, eng="pool")
    P.act(tmp, tmp, AF.Sigmoid, scale=1.5957691216057308)
    P.tt(out, tmp, x, ALU.mult, eng=eng)


def sincos(P, pre, ang, shape):
    I32 = mybir.dt.int32
    outs = []
    npi = P.sb(pre + "_npi", [shape[0], 1], F32)
    P.memset(npi[:], -PI)
    for nm, off in (("c", PI / 2), ("s", 0.0)):
        t = P.sb(pre + "_t" + nm, shape, F32)
        tf = P.sb(pre + "_f" + nm, shape, F32)
        ti = P.sb(pre + "_i" + nm, shape, I32)
        o = P.sb(pre + "_o" + nm, shape, F32)
        P.ts(t[:], ang, 64 * PI + PI + off, None, ALU.add)
        P.ts(tf[:], t[:], 1.0 / (2 * PI), None, ALU.mult)
        P.cp(ti[:], tf[:])
        P.cp(tf[:], ti[:])
        P.stt(t[:], tf[:], -2 * PI, t[:], ALU.mult, ALU.add)
        P.ts(tf[:], t[:], 0.0, 2 * PI, ALU.is_lt, ALU.mult)
        P.tt(t[:], t[:], tf[:], ALU.add)
        P.ts(tf[:], t[:], 2 * PI, -2 * PI, ALU.is_ge, ALU.mult)
        P.tt(t[:], t[:], tf[:], ALU.add)
        P.act(o[:], t[:], AF.Sin, bias=npi[:])
        outs.append(o)
    return outs[0], outs[1]


def phase_s5(P, nc, prm, l, zT, yT, psb, ident, ev_eng):
    P.push()
    lr = P.sb("s5lr", [128, 8], F32)
    li = P.sb("s5li", [128, 8], F32)
    dt = P.sb("s5dt", [128, 8], F32)
    br = P.sb("s5br", [128, 8, 16], F32)
    bi = P.sb("s5bi", [128, 8, 16], F32)
    for gs in range(2):
        sl = slice(gs * 64, (gs + 1) * 64)
        P.dma(lr[sl, :], prm["s5_lambda_re"][l, gs::2, :].rearrange("j p -> p j"), allow_slow_non_contiguous=True)
        P.dma(li[sl, :], prm["s5_lambda_im"][l, gs::2, :].rearrange("j p -> p j"), allow_slow_non_contiguous=True)
        P.dma(dt[sl, :], prm["s5_log_dt"][l, gs::2].partition_broadcast(64))
        P.dma(br[sl, :, :], prm["s5_b_re"][l, gs::2].rearrange("j p c -> p j c"))
        P.dma(bi[sl, :, :], prm["s5_b_im"][l, gs::2].rearrange("j p c -> p j c"))
    P.act(dt[:], dt[:], AF.Exp)
    mag = P.sb("s5mag", [128, 8], F32)
    ang = P.sb("s5ang", [128, 8], F32)
    P.tt(mag[:], lr[:], dt[:], ALU.mult)
    P.act(mag[:], mag[:], AF.Exp)
    P.tt(ang[:], li[:], dt[:], ALU.mult)
    cs, sn = sincos(P, "s5sc", ang[:], [128, 8])
    abr = P.sb("s5abr", [128, 8], F32)
    abi = P.sb("s5abi", [128, 8], F32)
    P.tt(abr[:], mag[:], cs[:], ALU.mult)
    P.tt(abi[:], mag[:], sn[:], ALU.mult)
    den = P.sb("s5den", [128, 8], F32)
    t0 = P.sb("s5t0", [128, 8], F32)
    t1 = P.sb("s5t1", [128, 8], F32)
    zr = P.sb("s5zr", [128, 8], F32)
    zi = P.sb("s5zi", [128, 8], F32)
    nzi = P.sb("s5nzi", [128, 8], F32)
    P.tt(den[:], lr[:], lr[:], ALU.mult)
    P.tt(t0[:], li[:], li[:], ALU.mult)
    P.tt(den[:], den[:], t0[:], ALU.add)
    P.recip(den[:], den[:])
    P.ts(t0[:], abr[:], -1.0, None, ALU.add)
    P.tt(zr[:], t0[:], lr[:], ALU.mult)
    P.tt(t1[:], abi[:], li[:], ALU.mult)
    P.tt(zr[:], zr[:], t1[:], ALU.add)
    P.tt(zr[:], zr[:], den[:], ALU.mult)
    P.tt(zi[:], abi[:], lr[:], ALU.mult)
    P.tt(t1[:], t0[:], li[:], ALU.mult)
    P.tt(zi[:], zi[:], t1[:], ALU.subtract)
    P.tt(zi[:], zi[:], den[:], ALU.mult)
    P.ts(nzi[:], zi[:], -1.0, None, ALU.mult)
    bbr = P.sb("s5bbr", [128, 8, 16], F32)
    bbi = P.sb("s5bbi", [128, 8, 16], F32)
    tb_ = P.sb("s5tb", [128, 8, 16], F32)
    for j in range(8):
        P.ts(tb_[:, j, :], bi[:, j, :], nzi[:, j:j + 1], None, ALU.mult)
        P.stt(bbr[:, j, :], br[:, j, :], zr[:, j:j + 1], tb_[:, j, :], ALU.mult, ALU.add)
        P.ts(tb_[:, j, :], br[:, j, :], zi[:, j:j + 1], None, ALU.mult)
        P.stt(bbi[:, j, :], bi[:, j, :], zr[:, j:j + 1], tb_[:, j, :], ALU.mult, ALU.add)
    LB = P.sb("s5LB", [128, 2, 8, 128], BF16)
    zz = [P.sb("s5zz%d" % i, [128, 128], F32) for i in range(2)]
    n = 0
    for comp, bb in enumerate((bbr, bbi)):
        for j in range(8):
            z_ = zz[n % 2]
            P.memset(z_[:], 0.0)
            for gs in range(2):
                g = 2 * j + gs
                c0 = 16 * (g % 8)
                P.cp(z_[gs * 64:(gs + 1) * 64, c0:c0 + 16], bb[gs * 64:(gs + 1) * 64, j, :], eng="pool")
            pt = psb[n % 2]
            P.tr(pt[:, 0:128], z_[:], ident[:])
            P.cp(LB[:, comp, j, :], pt[:, 0:128], eng=ev_eng())
            n += 1
    LC = P.sb("s5LC", [128, 2, 8, 128], BF16)
    P.memset(LC[:], 0.0)
    cc = P.sb("s5cc", [128, 128], F32)
    tt_ = P.sb("s5ctt", [128, 128], F32)
    for comp, nm in enumerate(("s5_c_re", "s5_c_im")):
        for hh in range(2):
            src = prm[nm][l, hh * 8:(hh + 1) * 8].rearrange("g c p -> (g c) p")
            P.dma(cc[:, 0:64], src)
            P.dma(cc[:, 64:128], src)
            pt = psb[2 + (comp * 2 + hh) % 2]
            P.tr(pt[:, 0:128], cc[:], ident[:])
            P.ts(tt_[:], pt[:, 0:128], (1.0 if comp == 0 else -1.0), None, ALU.mult)
            for jj in range(4):
                j = hh * 4 + jj
                for gs in range(2):
                    g = 2 * j + gs
                    c0 = 16 * (g % 8)
                    P.cp(LC[gs * 64:(gs + 1) * 64, comp, j, c0:c0 + 16], tt_[gs * 64:(gs + 1) * 64, c0:c0 + 16], eng="pool")
    wc = P.sb("s5wc", [128, 12, 8], F32)
    ws = P.sb("s5ws", [128, 12, 8], F32)
    P.cp(wc[:, 0, :], cs[:])
    P.cp(ws[:, 0, :], sn[:])
    for k in range(1, 12):
        P.tt(t0[:], wc[:, k - 1, :], wc[:, k - 1, :], ALU.mult)
        P.tt(t1[:], ws[:, k - 1, :], ws[:, k - 1, :], ALU.mult)
        P.tt(wc[:, k, :], t0[:], t1[:], ALU.subtract)
        P.tt(t0[:], wc[:, k - 1, :], ws[:, k - 1, :], ALU.mult)
        P.ts(ws[:, k, :], t0[:], 2.0, None, ALU.mult)
    ub = P.sb("s5ub", [128, 2, S], BF16)
    ust = [P.sb("s5ust%d" % i, [128, 2048], F32) for i in range(2)]
    n = 0
    for hh in range(2):
        for q in range(2):
            P.dma(ust[n % 2][:], zT[hh * 128:(hh + 1) * 128, q * 2048:(q + 1) * 2048])
            P.cp(ub[:, hh, q * 2048:(q + 1) * 2048], ust[n % 2][:], eng=("act" if n % 2 else "pool"))
            n += 1
    TC = P.sb("s5TC", [128, S], F32)
    TS = P.sb("s5TS", [128, S], F32)
    RHO = P.sb("s5RHO", [128, S], F32)
    Xr = P.sb("s5Xr", [128, S], F32)
    Xi = P.sb("s5Xi", [128, S], F32)
    yacc = P.sb("s5yacc", [128, 2, S], F32)
    tmp = [P.sb("s5tmp%d" % i, [128, 2048], F32) for i in range(2)]
    sbr = [P.sb("s5sbr%d" % i, [128, 512], BF16) for i in range(2)]
    sbi = [P.sb("s5sbi%d" % i, [128, 512], BF16) for i in range(2)]
    for j in range(8):
        hh = j // 4
        P.memset(TC[:, 0:1], 1.0, eng="dve")
        P.memset(TS[:, 0:1], 0.0, eng="dve")
        for k in range(12):
            n_ = 1 << k
            c_, s_ = wc[:, k, j:j + 1], ws[:, k, j:j + 1]
            tA, tB = tmp[0][:, 0:n_], tmp[1][:, 0:n_]
            P.act(tA, TS[:, 0:n_], AF.Copy, scale=s_)
            P.act(tB, TC[:, 0:n_], AF.Copy, scale=s_)
            P.stt(TC[:, n_:2 * n_], TC[:, 0:n_], c_, tA, ALU.mult, ALU.subtract)
            P.stt(TS[:, n_:2 * n_], TS[:, 0:n_], c_, tB, ALU.mult, ALU.add)
        P.ts(RHO[:], TC[:], 0.0, mag[:, j:j + 1], ALU.mult, ALU.add, eng="pool")
        for tb in range(8):
            sl = slice(tb * 512, (tb + 1) * 512)
            pr, pi_ = psb[(2 * tb) % 4], psb[(2 * tb + 1) % 4]
            P.mm(pr[:], LB[:, 0, j, :], ub[:, hh, sl])
            P.mm(pi_[:], LB[:, 1, j, :], ub[:, hh, sl])
            a_, b_ = tmp[0][:, (tb % 2) * 512:(tb % 2) * 512 + 512], tmp[1][:, (tb % 2) * 512:(tb % 2) * 512 + 512]
            P.tt(a_, pi_[:], TS[:, sl], ALU.mult)
            P.tt(Xr[:, sl], pr[:], TC[:, sl], ALU.mult)
            P.tt(Xr[:, sl], Xr[:, sl], a_, ALU.add, eng="pool")
            P.tt(b_, pr[:], TS[:, sl], ALU.mult)
            P.tt(Xi[:, sl], pi_[:], TC[:, sl], ALU.mult)
            P.tt(Xi[:, sl], Xi[:, sl], b_, ALU.subtract, eng="pool")
        P.scan(Xr[:], RHO[:], Xr[:], 0.0, ALU.mult, ALU.add)
        P.scan(Xi[:], RHO[:], Xi[:], 0.0, ALU.mult, ALU.add)
        for tb in range(8):
            sl = slice(tb * 512, (tb + 1) * 512)
            a_, b_ = tmp[0][:, (tb % 2) * 512:(tb % 2) * 512 + 512], tmp[1][:, (tb % 2) * 512:(tb % 2) * 512 + 512]
            sr_, si_ = sbr[tb % 2], sbi[tb % 2]
            P.tt(a_, Xi[:, sl], TS[:, sl], ALU.mult, eng="pool")
            P.tt(b_, Xr[:, sl], TC[:, sl], ALU.mult)
            P.tt(sr_[:], b_, a_, ALU.subtract)
            P.tt(a_, Xr[:, sl], TS[:, sl], ALU.mult, eng="pool")
            P.tt(b_, Xi[:, sl], TC[:, sl], ALU.mult)
            P.tt(si_[:], b_, a_, ALU.add)
            py = psb[4 + tb % 2]
            P.mm(py[:], LC[:, 0, j, :], sr_[:], start=True, stop=False)
            P.mm(py[:], LC[:, 1, j, :], si_[:], start=False, stop=True)
            if j % 4 == 0:
                P.cp(yacc[:, hh, sl], py[:], eng="act")
            else:
                P.tt(yacc[:, hh, sl], yacc[:, hh, sl], py[:], ALU.add)
    dcol = P.sb("s5d", [128, 2], F32)
    bg = P.sb("s5bg", [128, 2], F32)
    P.dma(dcol[:], prm["s5_d"][l].rearrange("(k p) -> p k", p=128), allow_slow_non_contiguous=True)
    P.dma(bg[:], prm["s5_b_glu"][l].rearrange("(k p) -> p k", p=128), allow_slow_non_contiguous=True)
    wg = P.sb("s5wg", [128, 2, 256], BF16)
    for k in range(2):
        P.dma(tmp[k][:, 0:256], prm["s5_w_glu"][l, k * 128:(k + 1) * 128, :])
        P.cp(wg[:, k, :], tmp[k][:, 0:256])
    yg = P.sb("s5yg", [128, 2, 512], F32)
    ygb = P.sb("s5ygb", [128, 2, 512], BF16)
    gt = P.sb("s5gt", [128, 512], F32)
    sg = P.sb("s5sg", [128, 512], F32)
    yo = [P.sb("s5yo%d" % i, [128, 512], BF16) for i in range(2)]
    u32 = [P.sb("s5u32%d" % i, [128, 512], F32) for i in range(2)]
    for tb in range(8):
        sl = slice(tb * 512, (tb + 1) * 512)
        for hh in range(2):
            P.dma(u32[hh][:], zT[hh * 128:(hh + 1) * 128, sl])
            P.stt(yacc[:, hh, sl], u32[hh][:], dcol[:, hh:hh + 1], yacc[:, hh, sl], ALU.mult, ALU.add)
            gelu(P, yg[:, hh, :], yacc[:, hh, sl], gt[:])
            P.cp(ygb[:, hh, :], yg[:, hh, :], eng="act")
        for h2 in range(2):
            ps = psb[6 + h2]
            for k in range(2):
                P.mm(ps[:], wg[:, k, h2 * 128:(h2 + 1) * 128], ygb[:, k, :], start=(k == 0), stop=(k == 1))
            P.act(sg[:], ps[:], AF.Sigmoid, bias=bg[:, h2:h2 + 1])
            o = yo[h2]
            P.tt(o[:], yg[:, h2, :], sg[:], ALU.mult)
            P.dma(R(yT[h2 * 128:(h2 + 1) * 128, sl], "s5_%d_%d" % (h2, tb)), o[:])
    P.pop()


def make_masks(P, pre, strict):
    ms = []
    for o in range(4):
        m = P.sb("%s_m%d" % (pre, o), [128, 512], F32)
        P.memset(m[:], 1.0)
        P.asel(m[:], m[:], [[1, 512]], (ALU.is_gt if strict else ALU.is_ge), 0.0, -128 * o, -1)
        ms.append(m)
    return ms


def load_heads_bf(P, pre, dst, zT, row0, scale, stages):
    n = 0
    for t in range(2):
        for q in range(2):
            st = stages[n % 2]
            P.dma(st[:], zT[row0 + t * 128:row0 + (t + 1) * 128, q * 2048:(q + 1) * 2048])
            if scale == 1.0:
                P.cp(dst[:, t, q * 2048:(q + 1) * 2048], st[:], eng=("act" if n % 2 else "pool"))
            else:
                P.act(dst[:, t, q * 2048:(q + 1) * 2048], st[:], AF.Copy, scale=scale)
            n += 1


def load_heads_zpad(P, dst, zT, row0, scale, stages):
    P.memset(dst[:], 0.0)
    n = 0
    for t in range(2):
        for q in range(2):
            st = stages[n % 2]
            n += 1
            P.dma(st[:], zT[row0 + t * 128:row0 + (t + 1) * 128, q * 2048:(q + 1) * 2048])
            for gs in range(2):
                pb = gs * 64
                P.act(dst[pb:pb + 64, 2 * t + gs, q * 2048:(q + 1) * 2048], st[pb:pb + 64, :], AF.Copy, scale=scale)


def phase_sb(P, nc, prm, l, zT, yT, vTM, psb, ones1, ev_eng):
    P.push()
    qb = P.sb("sbq", [128, 4, S], BF16)
    kb_ = P.sb("sbk", [128, 2, S], BF16)
    stg = [P.sb("sbst%d" % i, [128, 2048], F32) for i in range(2)]
    load_heads_zpad(P, qb, zT, 1280, 0.125, stg)
    load_heads_bf(P, "sbk", kb_, zT, 1536, 1.0, stg)
    vb = P.sb("sbv", [128, 32, 256], BF16)
    P.dma(vb[:], vTM[:, 0:256].rearrange("(n p) c -> p n c", p=128))
    masks = make_masks(P, "sbm", True)
    tri = P.sb("sbtri", [128, 128], F32)
    P.memset(tri[:], 1.0)
    P.asel(tri[:], tri[:], [[-1, 128]], ALU.is_ge, 0.0, 0, 1)
    trib = P.sb("sbtrib", [128, 128], BF16)
    P.cp(trib[:], tri[:])
    onesb = P.sb("sbonesb", [128, 128], BF16)
    P.memset(onesb[:], 1.0)
    hi_ = [P.sb("sbhi%d" % i, [128, 512], BF16) for i in range(2)]
    lo_ = [P.sb("sblo%d" % i, [128, 512], BF16) for i in range(2)]
    carry = P.sb("sbcarry", [128, 512], F32)
    e_ = [P.sb("sbe%d" % i, [128, 512], F32) for i in range(2)]
    sp_ = [P.sb("sbsp%d" % i, [128, 512], F32) for i in range(2)]
    tm_ = [P.sb("sbtm%d" % i, [128, 512], F32) for i in range(2)]
    w_ = [P.sb("sbw%d" % i, [128, 512], BF16) for i in range(2)]
    yo = [P.sb("sbyo%d" % i, [128, 512], BF16) for i in range(2)]
    steps = []
    n = 0
    for h in range(4):
        for sb in range(8):
            kbs = list(range(4 * sb + 3, -1, -1))
            for idx, kb in enumerate(kbs):
                steps.append(dict(h=h, sb=sb, kb=kb, idx=idx, nk=len(kbs), n=n))
                n += 1

    def banks(n):
        return psb[n % 3], psb[3 + n % 2], psb[5 + n % 2]

    def cols(st):
        o = st["kb"] - 4 * st["sb"]
        q0 = 128 * o if o > 0 else 0
        return slice(q0, 512), slice(st["sb"] * 512 + q0, (st["sb"] + 1) * 512)

    def stage_a(st):
        h, sb, kb, n = st["h"], st["sb"], st["kb"], st["n"]
        t = h // 2
        cq, qs = cols(st)
        diag = kb >= 4 * sb
        pl, pc, pt = banks(n)
        e, sp, hi, lo = e_[n % 2], sp_[n % 2], hi_[n % 2], lo_[n % 2]
        P.mm(pl[:, cq], kb_[:, t, kb * 128:(kb + 1) * 128], qb[:, h, qs])
        P.act(e[:, cq], pl[:, cq], AF.Exp)
        P.act(sp[:, cq], e[:, cq], AF.Ln, bias=1.0)
        if diag:
            P.tt(sp[:, cq], sp[:, cq], masks[kb - 4 * sb][:, cq], ALU.mult, eng="pool")
        P.cp(hi[:, cq], sp[:, cq], eng="act")
        P.tt(lo[:, cq], sp[:, cq], hi[:, cq], ALU.subtract, eng="pool")

    def stage_a2(st):
        n = st["n"]
        cq, qs = cols(st)
        pl, pc, pt = banks(n)
        hi, lo = hi_[n % 2], lo_[n % 2]
        P.mm(pc[:, cq], trib[:], hi[:, cq], start=True, stop=False)
        P.mm(pc[:, cq], trib[:], lo[:, cq], start=False, stop=True)
        if st["idx"] < st["nk"] - 1:
            P.mm(pt[:, cq], onesb[:], hi[:, cq], start=True, stop=False)
            P.mm(pt[:, cq], onesb[:], lo[:, cq], start=False, stop=True)

    def stage_b1(st):
        h, sb, kb, n, idx, nk = st["h"], st["sb"], st["kb"], st["n"], st["idx"], st["nk"]
        diag = kb >= 4 * sb
        cq, qs = cols(st)
        pl, pc, pt = banks(n)
        tm, w = tm_[n % 2], w_[n % 2]
        if idx == 0:
            P.memset(carry[:], 0.0)
            P.cp(tm[:, cq], pc[:, cq])
        else:
            P.tt(tm[:, cq], pc[:, cq], carry[:, cq], ALU.add)
        P.tt(tm[:, cq], pl[:, cq], tm[:, cq], ALU.subtract)
        if diag:
            P.act(tm[:, cq], tm[:, cq], AF.Exp)
            P.tt(w[:, cq], tm[:, cq], masks[kb - 4 * sb][:, cq], ALU.mult, eng="pool")
        else:
            P.act(w[:, cq], tm[:, cq], AF.Exp)

    def stage_b2(st):
        h, sb, kb, n, idx, nk = st["h"], st["sb"], st["kb"], st["n"], st["idx"], st["nk"]
        t, pb = h // 2, (h % 2) * 64
        cq, _ = cols(st)
        qs = slice(sb * 512, (sb + 1) * 512)
        pl, pc, pt = banks(n)
        w = w_[n % 2]
        po = psb[7]
        P.mm(po[:, cq], vb[:, kb, t * 128:(t + 1) * 128], w[:, cq], start=(idx == 0), stop=(idx == nk - 1))
        if idx < nk - 1:
            P.tt(carry[:, cq], carry[:, cq], pt[:, cq], ALU.add)
        else:
            o = yo[(h * 8 + sb) % 2]
            P.cp(o[pb:pb + 64, :], po[pb:pb + 64, :], eng="act")
            P.dma(R(yT[512 + h * 64:512 + (h + 1) * 64, qs], "sb_%d_%d" % (h, sb)), o[pb:pb + 64, :])

    ns = len(steps)
    for i in range(ns + 2):
        if i >= 2:
            stage_b1(steps[i - 2])
        if i < ns:
            stage_a(steps[i])
        if 1 <= i <= ns:
            stage_a2(steps[i - 1])
        if i >= 2:
            stage_b2(steps[i - 2])
    P.pop()


def phase_ml(P, nc, prm, l, zT, yT, vTM, psb, ident, ev_eng):
    P.push()
    cw = P.sb("mlcw", [128, 4, 4], F32)
    cb = P.sb("mlcb", [128, 4], F32)
    for i in range(4):
        P.dma(cw[:, :, i], prm["ml_conv_w"][l, i].rearrange("(t p) -> p t", p=128))
    P.dma(cb[:], prm["ml_conv_b"][l].rearrange("(t p) -> p t", p=128), allow_slow_non_contiguous=True)
    qz = P.sb("mlqz", [128, 4, S], BF16)
    kk_ = P.sb("mlkk", [128, 2, S], BF16)
    P.memset(qz[:], 0.0)
    va = P.sb("mlva", [128, 32, 4, 128], BF16)
    P.memset(va[:], 0.0)
    P.memset(va[:, :, :, 64:65], 1.0)
    P.push()
    xin2 = [P.sb("mlxin%d" % i, [128, S + 3], F32) for i in range(2)]
    acc = P.sb("mlacc", [128, S], F32)
    vst = P.sb("mlvst", [128, 32, 256], BF16)
    P.dma(vst[:], vTM[:, 256:512].rearrange("(n p) c -> p n c", p=128))
    for h in range(4):
        P.cp(va[:, :, h, 0:64], vst[:, :, h * 64:(h + 1) * 64], eng=("pool" if h % 2 else "dve"))
    for i in range(2):
        P.memset(xin2[i][:, 0:3], 0.0)
    def ld_x(t):
        row0 = 2048 + t * 128
        for q in range(2):
            P.dma(xin2[t % 2][:, 3 + q * 2048:3 + (q + 1) * 2048], zT[row0:row0 + 128, q * 2048:(q + 1) * 2048])
    ld_x(0)
    ld_x(1)
    for t in range(4):
        xin = xin2[t % 2]
        P.ts(acc[:], xin[:, 3:S + 3], cw[:, t, 3:4], cb[:, t:t + 1], ALU.mult, ALU.add)
        for i in range(3):
            P.stt(acc[:], xin[:, i:S + i], cw[:, t, i:i + 1], acc[:], ALU.mult, ALU.add)
        P.act(acc[:], acc[:], AF.Silu)
        if t < 2:
            for gs in range(2):
                pb = gs * 64
                P.cp(qz[pb:pb + 64, 2 * t + gs, :], acc[pb:pb + 64, :], eng=("act" if gs else "pool"))
        else:
            P.act(kk_[:, t - 2, :], acc[:], AF.Copy, scale=0.125)
        if t + 2 < 4:
            ld_x(t + 2)
    P.pop()
    gi = P.sb("mlgi", [4, S], F32)
    gf = P.sb("mlgf", [4, S], F32)
    bi_ = P.sb("mlbi", [4, 1], F32)
    bf_ = P.sb("mlbf", [4, 1], F32)
    P.dma(gi[:], zT[3072:3076, :])
    P.dma(gf[:], zT[3076:3080, :])
    P.dma(bi_[:], prm["ml_b_i"][l].rearrange("(p o) -> p o", o=1))
    P.dma(bf_[:], prm["ml_b_f"][l].rearrange("(p o) -> p o", o=1))
    P.ts(bf_[:], bf_[:], -1.0, None, ALU.mult)
    P.act(gf[:], gf[:], AF.Exp, bias=bf_[:], scale=-1.0)
    P.act(gf[:], gf[:], AF.Ln, bias=1.0)
    P.ts(gf[:], gf[:], -1.0, None, ALU.mult)
    Fc = gf
    zer = P.sb("mlzer", [4, 512], F32)
    P.memset(zer[:], 0.0)
    for q in range(8):
        sl = slice(q * 512, (q + 1) * 512)
        if q == 0:
            P.scan(Fc[:, sl], gf[:, sl], zer[:], 0.0, ALU.add, ALU.add)
        else:
            P.scan(Fc[:, sl], gf[:, sl], zer[:], Fc[:, q * 512 - 1:q * 512], ALU.add, ALU.add)
    P.ts(gi[:], gi[:], bi_[:], None, ALU.add)
    P.tt(gi[:], gi[:], Fc[:], ALU.subtract)
    colb = P.sb("mlcolb", [128, 32, 4], F32)
    for kb in range(32):
        pt = psb[kb % 2]
        P.tr(pt[:, 0:4], gi[0:4, kb * 128:(kb + 1) * 128], ident[0:4, 0:4])
        P.cp(colb[:, kb, :], pt[:, 0:4], eng=ev_eng())
    sel = P.sb("mlsel", [4, 4, 128], F32)
    P.memset(sel[:], 1.0)
    P.asel(sel[:], sel[:], [[-1, 4], [0, 128]], ALU.is_equal, 0.0, 0, 1)
    masks = make_masks(P, "mlm", False)
    selden = P.sb("mlselden", [128, 64], F32)
    P.memset(selden[:], 0.0)
    P.memset(selden[64:65, :], 1.0)
    j64 = P.sb("mlj64", [64, 64], F32)
    P.memset(j64[:], 1.0 / 64)
    lng = P.sb("mllng", [64, 4], F32)
    P.dma(lng[:], prm["ml_ln_g"][l].rearrange("(h p) -> p h", p=64), allow_slow_non_contiguous=True)
    frow = [P.sb("mlfrow%d" % i, [128, 512], F32) for i in range(2)]
    da = [P.sb("mlda%d" % i, [128, 512], F32) for i in range(4)]
    pp = [P.sb("mlpp%d" % i, [128, 512], BF16) for i in range(4)]
    o65 = P.sb("mlo65", [128, 512], F32)
    hh_ = P.sb("mlhh", [64, 512], F32)
    t1 = P.sb("mlt1", [64, 512], F32)
    t2 = P.sb("mlt2", [64, 512], F32)
    og = P.sb("mlog", [64, 512], F32)
    yo = [P.sb("mlyo%d" % i, [64, 512], BF16) for i in range(2)]
    steps = []
    n = 0
    for h in range(4):
        for sb in range(8):
            nkb = 4 * sb + 4
            for kb in range(nkb):
                steps.append(dict(h=h, sb=sb, kb=kb, nkb=nkb, n=n))
                n += 1

    def stage_a(st):
        h, sb, kb, n = st["h"], st["sb"], st["kb"], st["n"]
        t = h // 2
        qs = slice(sb * 512, (sb + 1) * 512)
        diag = kb >= 4 * sb
        if kb == 0:
            pf = psb[4]
            P.mm(pf[:], sel[:, h, :], Fc[:, qs])
            P.cp(frow[(h * 8 + sb) % 2][:], pf[:], eng="act")
        fr = frow[(h * 8 + sb) % 2]
        pl = psb[n % 4]
        d_ = da[n % 4]
        o_ = kb - 4 * sb
        q0 = 128 * o_ if o_ > 0 else 0
        cq = slice(q0, 512)
        P.mm(pl[:, cq], kk_[:, t, kb * 128:(kb + 1) * 128], qz[:, h, sb * 512 + q0:(sb + 1) * 512])
        if diag:
            P.ts(d_[:, cq], fr[:, cq], colb[:, kb, h:h + 1], 30.0, ALU.add, ALU.min)
            P.act(d_[:, cq], d_[:, cq], AF.Exp)
            P.tt(d_[:, cq], d_[:, cq], masks[kb - 4 * sb][:, cq], ALU.mult, eng="pool")
        else:
            P.act(d_[:], fr[:], AF.Exp, bias=colb[:, kb, h:h + 1])

    def stage_b(st):
        h, sb, kb, n, nkb = st["h"], st["sb"], st["kb"], st["n"], st["nkb"]
        qs = slice(sb * 512, (sb + 1) * 512)
        pl = psb[n % 4]
        d_, p_ = da[n % 4], pp[n % 4]
        po = psb[5]
        o_ = kb - 4 * sb
        q0 = 128 * o_ if o_ > 0 else 0
        cq = slice(q0, 512)
        P.tt(p_[:, cq], pl[:, cq], d_[:, cq], ALU.mult)
        P.mm(po[:, cq], va[:, kb, h, :], p_[:, cq], start=(kb == 0), stop=(kb == nkb - 1))
        if kb == nkb - 1:
            P.cp(o65[:], po[:], eng="act")
            pending.extend(epilogue(h, sb))
        for _ in range(2):
            if pending:
                pending.pop(0)()

    def epilogue(h, sb):
        qs = slice(sb * 512, (sb + 1) * 512)
        pd, pm = psb[6], psb[7]
        o = yo[(h * 8 + sb) % 2]
        return [
            lambda: P.dma(og[:], zT[2816 + h * 64:2816 + (h + 1) * 64, qs]),
            lambda: P.mm(pd[0:64, :], selden[:], o65[:]),
            lambda: P.act(t1[:], pd[0:64, :], AF.Abs),
            lambda: P.ts(t1[:], t1[:], 1.0, None, ALU.max),
            lambda: P.recip(t1[:], t1[:]),
            lambda: P.tt(hh_[:], o65[0:64, :], t1[:], ALU.mult),
            lambda: P.mm(pm[0:64, :], j64[:], hh_[:]),
            lambda: P.tt(hh_[:], hh_[:], pm[0:64, :], ALU.subtract),
            lambda: P.tt(t1[:], hh_[:], hh_[:], ALU.mult, eng="pool"),
            lambda: P.mm(pd[0:64, :], j64[:], t1[:]),
            lambda: P.ts(t2[:], pd[0:64, :], LN_EPS, None, ALU.add),
            lambda: P.act(t2[:], t2[:], AF.Sqrt),
            lambda: P.recip(t2[:], t2[:]),
            lambda: P.tt(hh_[:], hh_[:], t2[:], ALU.mult),
            lambda: P.act(og[:], og[:], AF.Sigmoid),
            lambda: P.stt(o[:], hh_[:], lng[:, h:h + 1], og[:], ALU.mult, ALU.mult),
            lambda: P.dma(R(yT[768 + h * 64:768 + (h + 1) * 64, qs], "ml_%d_%d" % (h, sb)), o[:]),
        ]

    pending = []
    for i, st in enumerate(steps):
        stage_a(st)
        if i >= 2:
            stage_b(steps[i - 2])
    stage_b(steps[-2])
    stage_b(steps[-1])
    while pending:
        pending.pop(0)()
    P.pop()


def phase_rw(P, nc, prm, l, zT, yT, vrw, grw, yrw, psb, ident, identb, ev_eng):
    RW0 = 256
    H = 2048
    P.push()
    def col2(name, src):
        t = P.sb(name, [128, 2], F32)
        P.dma(t[:], src.rearrange("(k p) -> p k", p=128), allow_slow_non_contiguous=True)
        return t
    mu = P.sb("rwmu", [128, 8], F32)
    P.dma(mu[:], prm["rw_mu"][l].rearrange("(k p) -> p k", p=128), allow_slow_non_contiguous=True)
    w0 = col2("rww0", prm["rw_w0"][l])
    a0 = col2("rwa0", prm["rw_a0"][l])
    kkc = col2("rwkk", prm["rw_k_k"][l])
    kac = col2("rwka", prm["rw_k_a"][l])
    rkc = col2("rwrk", prm["rw_r_k"][l].rearrange("h d -> (h d)"))
    omka = P.sb("rwomka", [128, 2], F32)
    P.ts(omka[:], kac[:], -1.0, 1.0, ALU.mult, ALU.add)
    stw = P.sb("rwstw", [128, 256], F32)
    w2b = P.sb("rww2b", [128, 256], BF16)
    P.dma(stw[0:64, :], prm["rw_w2"][l])
    P.dma(stw[64:128, :], prm["rw_a2"][l])
    P.cp(w2b[:], stw[:])
    stg2 = P.sb("rwstg2", [128, 256], F32)
    g2b = P.sb("rwg2b", [128, 256], BF16)
    P.dma(stg2[:], prm["rw_g2"][l])
    P.cp(g2b[:], stg2[:])
    lngr = P.sb("rwlng", [128, 256], F32)
    lnbr = P.sb("rwlnb", [128, 256], F32)
    P.dma(lngr[:], prm["rw_ln_g"][l].partition_broadcast(128))
    P.dma(lnbr[:], prm["rw_ln_b"][l].partition_broadcast(128))
    bo = P.sb("rwbo", [128, 128], F32)
    P.memset(bo[:], 0.0)
    P.memset(bo[0:64, 0:64], 1.0)
    P.memset(bo[64:128, 64:128], 1.0)
    hsel = P.sb("rwhsel", [128, 2], F32)
    P.memset(hsel[:], 0.0)
    P.memset(hsel[0:64, 0:1], 1.0)
    P.memset(hsel[64:128, 1:2], 1.0)
    cmask = P.sb("rwcmask", [128, H], F32)
    P.memset(cmask[:], 1.0)
    P.memset(cmask[:].rearrange("p (c t) -> p c t", t=64)[:, :, 0:1], 0.0)
    rt = P.sb("rwrt", [128, 2, S], BF16)
    kt = P.sb("rwkt", [128, 2, S], BF16)
    bt = P.sb("rwbt", [128, 2, S], BF16)
    at = P.sb("rwat", [128, 2, S], BF16)
    gend = P.sb("rwgend", [128, 2, 64], F32)
    rk = P.sb("rwrk_tm", [128, 32, 4], F32)
    lora = P.sb("rwlora", [128, S], BF16)
    sgb = P.sb("rwsgb", [128, S], BF16)
    P.push()
    buf = P.sb("rwbuf", [128, H + 1], F32)
    T = [P.sb("rwT%d" % i, [128, H], F32) for i in range(8)]

    def load_shift(dst, tile_idx, hf):
        r0 = RW0 + tile_idx * 128
        if hf == 0:
            P.memset(buf[:, 0:1], 0.0)
            P.dma(buf[:, 1:H + 1], zT[r0:r0 + 128, 0:H])
        else:
            P.dma(buf[:, 0:H + 1], zT[r0:r0 + 128, H - 1:2 * H])
        P.tt(dst, buf[:, 0:H], buf[:, 1:H + 1], ALU.subtract, eng="pool")
        P.stt(dst, dst, mu[:, tile_idx:tile_idx + 1], buf[:, 1:H + 1], ALU.mult, ALU.add)

    for hf in range(2):
        hs = slice(hf * H, (hf + 1) * H)
        load_shift(T[0][:], 6, hf)
        P.act(lora[0:64, hs], T[0][0:64, :], AF.Tanh)
        P.cp(lora[64:128, hs], T[0][64:128, :])
        load_shift(T[0][:], 7, hf)
        P.act(sgb[:, hs], T[0][:], AF.Sigmoid)
    n = 0
    import os
    RWSUB = int(os.environ.get("RWSUB", "9"))
    for hp in range(2 if RWSUB > 0 else 0):
        for hf in range(2):
            Tr, Tk, Ta, Tlw, Tkk, TG, Tt, Te = [t[:] for t in T]
            hs = slice(hf * H, (hf + 1) * H)
            load_shift(Tr, hp, hf)
            load_shift(Tk, 2 + hp, hf)
            for tb in range(4):
                bs = slice(tb * 512, (tb + 1) * 512)
                gs_ = slice(hf * H + tb * 512, hf * H + (tb + 1) * 512)
                pw, pa = psb[(2 * n) % 4], psb[(2 * n + 1) % 4]
                n += 1
                P.mm(pw[:], w2b[0:64, hp * 128:(hp + 1) * 128], lora[0:64, gs_])
                P.mm(pa[:], w2b[64:128, hp * 128:(hp + 1) * 128], lora[64:128, gs_])
                P.act(Tlw[:, bs], pw[:], AF.Sigmoid, bias=w0[:, hp:hp + 1])
                P.act(Ta[:, bs], pa[:], AF.Sigmoid, bias=a0[:, hp:hp + 1])
            if RWSUB <= 1:
                continue
            P.ts(Tlw, Tlw, -0.6065306597126334, None, ALU.mult)
            P.act(Tkk, Tk, AF.Copy, scale=kkc[:, hp:hp + 1])
            P.tt(Tt, Tkk, Tkk, ALU.mult, eng="pool")
            for tb in range(4):
                bs = slice(tb * 512, (tb + 1) * 512)
                ps = psb[4 + tb % 2]
                P.mm(ps[:], bo[:], Tt[:, bs])
                P.act(Te[:, bs], ps[:], AF.Sqrt)
            P.ts(Te, Te, 1e-12, None, ALU.max)
            P.recip(Te, Te)
            P.tt(Tkk, Tkk, Te, ALU.mult)
            if RWSUB <= 2:
                continue
            P.ts(Tt, Ta, kac[:, hp:hp + 1], omka[:, hp:hp + 1], ALU.mult, ALU.add)
            P.tt(Tk, Tk, Tt, ALU.mult)
            P.stt(Tt, Tr, rkc[:, hp:hp + 1], Tk, ALU.mult, ALU.mult)
            for t16 in range(16):
                tt_ = hf * 16 + t16
                ps = psb[6 + t16 % 2]
                P.mm(ps[:, 0:2], Tt[:, t16 * 128:(t16 + 1) * 128], hsel[:])
                P.cp(rk[:, tt_, 2 * hp:2 * hp + 2], ps[:, 0:2], eng=ev_eng())
            if RWSUB <= 3:
                continue
            P.scan(TG, cmask[:], Tlw, 0.0, ALU.mult, ALU.add)
            if RWSUB <= 4:
                continue
            P.act(Te, TG, AF.Exp)
            P.tt(rt[:, hp, hs], Tr, Te, ALU.mult)
            if RWSUB <= 5:
                continue
            P.cp(gend[:, hp, hf * 32:(hf + 1) * 32], Te.rearrange("p (c t) -> p c t", t=64)[:, :, 63], eng="pool")
            if RWSUB <= 6:
                continue
            P.tt(Tt, TG, Tlw, ALU.subtract, eng="pool")
            P.act(Tt, Tt, AF.Exp)
            P.stt(at[:, hp, hs], Tkk, -1.0, Tt, ALU.mult, ALU.mult)
            P.act(Te, TG, AF.Exp, scale=-1.0)
            P.tt(kt[:, hp, hs], Tk, Te, ALU.mult)
            P.tt(Tt, Tkk, Ta, ALU.mult, eng="pool")
            P.tt(bt[:, hp, hs], Tt, Te, ALU.mult)
    P.pop()
    import os
    if os.environ.get("DUMPT"):
        outd = nc_out_handle[0][:, :].rearrange("(p a) d -> p (a d)", p=128)
        dst_ = [P.sb("dst%d" % i, [128, 2048], F32) for i in range(2)]
        n_ = 0
        for ai, arr in enumerate((rt, kt, bt, at)):
            for hp in range(2):
                for q in range(2):
                    d_ = dst_[n_ % 2]
                    P.cp(d_[:], arr[:, hp, q * 2048:(q + 1) * 2048])
                    off = ai * 8192 + hp * 4096 + q * 2048
                    P.dma(R(outd[:, off:off + 2048], n_), d_[:])
                    n_ += 1
        P.pop()
        return
    RWSTOP = int(os.environ.get("RWSTOP", "9"))
    if RWSTOP <= 1:
        P.pop()
        return
    gst = [P.sb("rwgst%d" % i, [128, 256], F32) for i in range(2)]
    for tt_ in range(32):
        ps = psb[tt_ % 2]
        P.mm(ps[:, 0:256], sgb[:, tt_ * 128:(tt_ + 1) * 128], g2b[:])
        P.cp(gst[tt_ % 2][:], ps[:, 0:256], eng=ev_eng())
        P.dma(R(grw[tt_ * 128:(tt_ + 1) * 128, :], tt_), gst[tt_ % 2][:])
    odd = {}
    for nm, arr in (("r", rt), ("k", kt), ("b", bt), ("a", at)):
        o_ = P.sb("rwodd_" + nm, [64, 2, S], BF16)
        P.dma(o_[:], arr[64:128, :, :])
        odd[nm] = o_
    gall = P.sb("rwgall", [64, 4, 64], F32)
    for hp in range(2):
        P.cp(gall[:, 2 * hp, :], gend[0:64, hp, :], eng="pool")
        P.dma(gall[:, 2 * hp + 1, :], gend[64:128, hp, :])

    def fm(nm, arr, h, cs):
        return arr[0:64, h // 2, cs] if h % 2 == 0 else odd[nm][:, h // 2, cs]

    def mask_n(name, specs):
        n_ = len(specs)
        m = P.sb(name, [64, n_ * 4, 64], F32)
        P.memset(m[:], 1.0)
        for i, sp_ in enumerate(specs):
            if sp_ is None:
                continue
            cm, step, cmp = sp_
            P.asel(m[:, 4 * i:4 * i + 4, :], m[:, 4 * i:4 * i + 4, :], [[0, 4], [step, 64]], cmp, 0.0, 0, cm)
        return m
    S_MU = (-1, 1, ALU.is_gt)
    S_MUI = (-1, 1, ALU.is_ge)
    S_ML = (1, -1, ALU.is_gt)
    M0 = mask_n("rwM0", [S_MU, S_ML])
    M1 = mask_n("rwM1", [S_MU, S_MUI])
    M2 = mask_n("rwM2", [S_MUI, None])
    I4 = mask_n("rwI4", [(1, -1, ALU.is_equal)])
    M32 = P.sb("rwM32", [64, 4, 64], F32)
    Mb = P.sb("rwMb", [64, 4, 64], BF16)
    P.memset(M32[:], 0.0)
    P.memset(Mb[:], 0.0)
    NN = [P.sb("rwNN%d" % i, [64, 8, 64], BF16) for i in range(2)]
    AR = [P.sb("rwAR%d" % i, [64, 8, 64], BF16) for i in range(2)]
    RB = [P.sb("rwRB%d" % i, [64, 8, 64], BF16) for i in range(2)]
    KT = [P.sb("rwKT%d" % i, [64, 4, 64], BF16) for i in range(2)]
    Nk = [P.sb("rwNk%d" % i, [64, 4, 64], BF16) for i in range(2)]
    NkT = [P.sb("rwNkT%d" % i, [64, 4, 64], BF16) for i in range(2)]
    P32 = P.sb("rwP32", [64, 4, 64], F32)
    Pb = [P.sb("rwPb%d" % i, [64, 4, 64], BF16) for i in range(2)]
    V32 = [P.sb("rwV32%d" % i, [64, 4, 64], F32) for i in range(2)]
    Vb = [P.sb("rwVb%d" % i, [64, 4, 64], BF16) for i in range(2)]
    Xb = P.sb("rwXb", [64, 4, 64], BF16)
    Ub = P.sb("rwUb", [64, 4, 64], BF16)
    Yc = [P.sb("rwYc%d" % i, [64, 4, 64], F32) for i in range(2)]
    B0, B1, B2, B3, B4, B5, B6 = psb[0], psb[1], psb[2], psb[3], psb[4], psb[5], psb[6]

    def pvn(bank, lo, n_):
        return bank[0:64, lo:lo + 64 * n_].rearrange("p (a t) -> p a t", a=n_)

    def hc(h, lo=0):
        return slice(lo + h * 64, lo + (h + 1) * 64)

    def rw_pre(c):
        cs = slice(c * 64, (c + 1) * 64)
        nn, ar, rb, kt_ = NN[c % 2], AR[c % 2], RB[c % 2], KT[c % 2]
        pb_ = Pb[c % 2]

        def init():
            for h in range(4):
                r_, k_, b_, a_ = fm("r", rt, h, cs), fm("k", kt, h, cs), fm("b", bt, h, cs), fm("a", at, h, cs)
                idn = identb[0:64, 0:64]
                P.mm(B0[0:64, hc(h)], b_, a_)
                P.mm(B0[0:64, hc(h, 256)], a_, b_)
                P.mm(B1[0:64, hc(h)], k_, a_)
                P.mm(B1[0:64, hc(h, 256)], b_, r_)
                P.mm(B2[0:64, hc(h)], k_, r_)
                P.mm(B2[0:64, hc(h, 256)], b_, idn)
                P.mm(B3[0:64, hc(h)], k_, idn)
            P.tt(nn[:], pvn(B0, 0, 8), M0[:], ALU.mult)
            P.tt(P32[:], nn[:, 0:4, :], I4[:], ALU.add)
            P.cp(pb_[:], P32[:], eng="dve")
            P.tt(ar[:], pvn(B1, 0, 8), M1[:], ALU.mult)
            P.tt(rb[:], pvn(B2, 0, 8), M2[:], ALU.mult)
            P.cp(kt_[:], pvn(B3, 0, 4), eng="dve")

        def stage(i):
            def f():
                last = (i == 5)
                nxt = i % 2
                curN, curT = (nn[:, 0:4, :], nn[:, 4:8, :]) if i == 1 else (Nk[1 - nxt][:], NkT[1 - nxt][:])
                for h in range(4):
                    if not last:
                        P.mm(B4[0:64, hc(h)], curT[:, h, :], curN[:, h, :])
                    P.mm(B4[0:64, hc(h, 256)], curN[:, h, :], curT[:, h, :])
                P.cp(NkT[nxt][:], pvn(B4, 256, 4), eng="dve")
                if not last:
                    P.cp(Nk[nxt][:], pvn(B4, 0, 4), eng="dve")
                for h in range(4):
                    P.mm(B5[0:64, hc(h)], NkT[nxt][:, h, :], pb_[:, h, :])
                P.tt(P32[:], P32[:], pvn(B5, 0, 4), ALU.add)
                P.cp(pb_[:], P32[:], eng="dve")
            return f
        return [init] + [stage(i) for i in range(1, 6)]

    def rw_post(c):
        cs = slice(c * 64, (c + 1) * 64)
        ar, rb, kt_, pb_ = AR[c % 2], RB[c % 2], KT[c % 2], Pb[c % 2]
        v32, vb_ = V32[c % 2], Vb[c % 2]
        yc = Yc[c % 2]

        def fx():
            P.dma(v32[:], vrw[c * 64:(c + 1) * 64, :].rearrange("t (h v) -> t h v", h=4))
            P.cp(vb_[:], v32[:], eng="pool")
            for h in range(4):
                P.mm(B6[0:64, hc(h)], fm("a", at, h, cs), Mb[:, h, :], start=True, stop=False)
                P.mm(B6[0:64, hc(h)], ar[:, h, :], vb_[:, h, :], start=False, stop=True)
            P.cp(Xb[:], pvn(B6, 0, 4), eng="dve")

        def fu():
            for h in range(4):
                P.mm(B7[0:64, hc(h)], pb_[:, h, :], Xb[:, h, :])
            P.cp(Ub[:], pvn(B7, 0, 4), eng="dve")

        def fm_():
            for h in range(4):
                P.mm(B3[0:64, hc(h, 256)], rb[:, 4 + h, :], Ub[:, h, :], start=True, stop=False)
                P.mm(B3[0:64, hc(h, 256)], kt_[:, h, :], vb_[:, h, :], start=False, stop=True)
            P.tt(M32[:], M32[:], pvn(B3, 256, 4), ALU.add)
            for h in range(4):
                P.ts(M32[:, h, :], M32[:, h, :], gall[:, h, c:c + 1], None, ALU.mult)
            P.cp(Mb[:], M32[:], eng="dve")

        def fy():
            for h in range(4):
                P.mm(B6[0:64, hc(h, 256)], fm("r", rt, h, cs), Mbp[:, h, :], start=True, stop=False)
                P.mm(B6[0:64, hc(h, 256)], ar[:, 4 + h, :], Ub[:, h, :], start=False, stop=False)
                P.mm(B6[0:64, hc(h, 256)], rb[:, h, :], vb_[:, h, :], start=False, stop=True)
            P.cp(yc[:], pvn(B6, 256, 4), eng="pool" if False else "dve")
            P.dma(R(yrw[c * 64:(c + 1) * 64, :].rearrange("t (h v) -> t h v", h=4), c), yc[:])
        return [fx, fu, fy, fm_]

    B7 = psb[7]
    Mbp = Mb
    NCH = 64
    for f in rw_pre(0):
        f()
    for c in range(NCH):
        pre = rw_pre(c + 1) if c + 1 < NCH else []
        post = rw_post(c)
        order = []
        for i in range(max(len(pre), len(post))):
            if i < len(post):
                order.append(post[i])
            if i < len(pre):
                order.append(pre[i])
        for f in order:
            f()
    P.barrier()
    if RWSTOP <= 2:
        P.pop()
        return
    yin = [P.sb("rwyin%d" % i, [128, 4, 64], F32) for i in range(2)]
    vin = [P.sb("rwvin%d" % i, [128, 4, 64], F32) for i in range(2)]
    gin = [P.sb("rwgin%d" % i, [128, 256], F32) for i in range(2)]
    sq = P.sb("rwsq", [128, 4, 64], F32)
    st4 = P.sb("rwst4", [128, 4], F32)
    st5 = P.sb("rwst5", [128, 4], F32)
    yo = [P.sb("rwyo%d" % i, [128, 2, 512], BF16) for i in range(2)]
    for tt_ in range(32):
        y_, v_, g_ = yin[tt_ % 2], vin[tt_ % 2], gin[tt_ % 2]
        ts_ = slice(tt_ * 128, (tt_ + 1) * 128)
        P.dma(y_[:], yrw[ts_, :].rearrange("t (h d) -> t h d", h=4))
        P.dma(v_[:], vrw[ts_, :].rearrange("t (h d) -> t h d", h=4))
        P.dma(g_[:], grw[ts_, :])
        P.op("dve", lambda e, o=st4[:], i=y_[:]: e.reduce_sum(o, i, AX.X), [y_[:]], [st4[:]])
        P.ts(st4[:], st4[:], -1.0 / 64, None, ALU.mult)
        for h in range(4):
            P.ts(y_[:, h, :], y_[:, h, :], st4[:, h:h + 1], None, ALU.add, eng=("pool" if h % 2 else "dve"))
        P.tt(sq[:], y_[:], y_[:], ALU.mult, eng="pool")
        P.op("dve", lambda e, o=st5[:], i=sq[:]: e.reduce_sum(o, i, AX.X), [sq[:]], [st5[:]])
        P.ts(st5[:], st5[:], 1.0 / 64, 64e-5, ALU.mult, ALU.add)
        P.act(st5[:], st5[:], AF.Sqrt)
        P.recip(st5[:], st5[:])
        for h in range(4):
            P.ts(y_[:, h, :], y_[:, h, :], st5[:, h:h + 1], None, ALU.mult, eng=("pool" if h % 2 else "dve"))
        yf = y_[:].rearrange("p h d -> p (h d)")
        P.tt(yf, yf, lngr[:], ALU.mult)
        P.tt(yf, yf, lnbr[:], ALU.add, eng="pool")
        for h in range(4):
            P.stt(y_[:, h, :], v_[:, h, :], rk[:, tt_, h:h + 1], y_[:, h, :], ALU.mult, ALU.add)
        P.tt(yf, yf, g_[:], ALU.mult)
        o = yo[(tt_ // 4) % 2]
        for t in range(2):
            pt = psb[(2 * tt_ + t) % 4]
            P.tr(pt[:, 0:128], yf[:, t * 128:(t + 1) * 128], ident[:])
            P.cp(o[:, t, (tt_ % 4) * 128:(tt_ % 4 + 1) * 128], pt[:, 0:128], eng=ev_eng())
        if tt_ % 4 == 3:
            tb = tt_ // 4
            P.dma(R(yT[256:512, tb * 512:(tb + 1) * 512].rearrange("(k p) t -> p k t", p=128), "rw_%d" % tb), o[:])
    P.pop()


def kernel(**inputs):
    dbg = inputs.pop("_dbg", None)
    ncores = inputs.pop("_ncores", 8)
    nlayers = inputs.pop("_nlayers", DEPTH)
    mixers = inputs.pop("_mixers", ("s5", "rw", "sb", "ml"))
    nc = build_program(dbg, nlayers, mixers)
    x = np.ascontiguousarray(inputs["x"], dtype=np.float32)
    shared = {n: np.ascontiguousarray(inputs[n], dtype=np.float32) for n, _ in PARAMS}
    in_maps = []
    for c in range(ncores):
        m = {"x": x[c]}
        m.update(shared)
        in_maps.append(m)
    res = run_bass_kernel_spmd(nc, in_maps, core_ids=list(range(ncores)))
    return np.stack([r["out"] for r in res.results], axis=0)
```

```python
import math
from contextlib import ExitStack
import numpy as np
import concourse.bass as bass
import concourse.mybir as mybir
from concourse.bass_utils import run_bass_kernel_spmd

F32 = mybir.dt.float32
BF16 = mybir.dt.bfloat16
AF = mybir.ActivationFunctionType
ALU = mybir.AluOpType
AX = mybir.AxisListType

S = 4096
D = 1024
NIN = 3080
DFF = 2816
DEPTH = 2
LN_EPS = 1e-5
DN_ALPHA = (2 * DEPTH) ** 0.25


class R:
    def __init__(self, ap, tag):
        self.ap = ap
        self.tag = tag


def _u(x):
    if isinstance(x, R):
        return x.ap, "%s#%s" % (x.ap.name, x.tag)
    return x, x.name


class Prog:
    NQ = 8

    def __init__(self, nc):
        self.nc = nc
        self.es = ExitStack()
        self.ops = {k: [] for k in ("pe", "dve", "act", "pool", "sp")}
        self.sem = {k: self.es.enter_context(nc.semaphore("s_" + k)) for k in ("pe", "dve", "act", "pool")}
        self.cnt = {k: 0 for k in ("pe", "dve", "act", "pool")}
        self.qsem = [self.es.enter_context(nc.semaphore("q%d" % i)) for i in range(self.NQ)]
        self.ndma = 0
        self.waited = {k: {} for k in self.ops}
        self.lastw = {}
        self.readers = {}
        self.stack = [self.es]
        self.pend = {}
        self.nuniq = 0

    def push(self):
        self.stack.append(ExitStack())

    def pop(self):
        self.barrier()
        self.stack.pop().close()

    def barrier(self):
        toks = [(self.sem[k], self.cnt[k]) for k in self.cnt if self.cnt[k] > 0]
        for j in range(min(self.NQ, self.ndma)):
            n_on = (self.ndma - j + self.NQ - 1) // self.NQ
            toks.append((self.qsem[j], 16 * n_on))
        for e in self.ops:
            self.pend[e] = list(toks)

    def sb(self, name, shape, dtype):
        self.nuniq += 1
        return self.stack[-1].enter_context(self.nc.sbuf_tensor("%s_u%d" % (name, self.nuniq), list(shape), dtype))

    def ps(self, name, shape, dtype=F32):
        return self.es.enter_context(self.nc.psum_tensor(name, list(shape), dtype))

    def dram(self, name, shape, dtype, kind="Internal"):
        return self.nc.dram_tensor(name, list(shape), dtype, kind=kind)

    def _deps(self, eng, reads, writes):
        deps = {}

        def add(tok):
            key = id(tok[0])
            if key not in deps or deps[key][1] < tok[1]:
                deps[key] = tok
        for tok in self.pend.pop(eng, []):
            add(tok)
        for b in reads:
            for tok in self.lastw.get(b, {}).values():
                add(tok)
        for b in writes:
            for tok in self.lastw.get(b, {}).values():
                add(tok)
            for tok in self.readers.get(b, {}).values():
                add(tok)
        waits = []
        w = self.waited[eng]
        for key, tok in deps.items():
            if eng == "pe" and tok[0] is self.sem["pe"]:
                continue
            if w.get(key, 0) >= tok[1]:
                continue
            w[key] = tok[1]
            waits.append(tok)
        return waits

    def _commit(self, tok, reads, writes):
        key = id(tok[0])
        for b in writes:
            self.lastw[b] = {key: tok}
            self.readers[b] = {}
        for b in reads:
            self.readers.setdefault(b, {})[key] = tok

    def op(self, eng, fn, reads, writes):
        reads = [_u(x)[1] for x in reads]
        writes = [_u(x)[1] for x in writes]
        waits = self._deps(eng, reads, writes)
        self.cnt[eng] += 1
        tok = (self.sem[eng], self.cnt[eng])
        self.ops[eng].append((fn, waits, tok, 1))
        self._commit(tok, reads, writes)

    def dma(self, out, in_, **kw):
        oap, ob = _u(out)
        iap, ib = _u(in_)
        kw.setdefault("allow_slow_non_contiguous", True)
        waits = self._deps("sp", [ib], [ob])
        i = self.ndma
        self.ndma += 1
        sem = self.qsem[i % self.NQ]
        if i >= self.NQ:
            prev = (sem, 16 * (i // self.NQ))
            key = id(sem)
            if self.waited["sp"].get(key, 0) < prev[1]:
                self.waited["sp"][key] = prev[1]
                waits.append(prev)
        tok = (sem, 16 * (i // self.NQ + 1))
        self.ops["sp"].append((lambda e: e.dma_start(out=oap, in_=iap, **kw), waits, tok, 16))
        self._commit(tok, [ib], [ob])

    def mm(self, out, lhsT, rhs, start=True, stop=True):
        o, l, r = _u(out)[0], _u(lhsT)[0], _u(rhs)[0]
        self.op("pe", lambda e: e.matmul(o, l, r, start=start, stop=stop), [lhsT, rhs], [out])

    def tr(self, out, in_, ident):
        o, i, d = _u(out)[0], _u(in_)[0], _u(ident)[0]
        self.op("pe", lambda e: e.transpose(o, i, d), [in_, ident], [out])

    def act(self, out, in_, func, bias=None, scale=None, accum_out=None, eng="act"):
        o, i = _u(out)[0], _u(in_)[0]
        kw = {}
        rd = [in_]
        if bias is not None:
            kw["bias"] = bias if isinstance(bias, (int, float)) else _u(bias)[0]
            if not isinstance(bias, (int, float)):
                rd.append(bias)
        if scale is not None:
            kw["scale"] = scale if isinstance(scale, (int, float)) else _u(scale)[0]
            if not isinstance(scale, (int, float)):
                rd.append(scale)
        wr = [out]
        if accum_out is not None:
            kw["accum_out"] = _u(accum_out)[0]
            wr.append(accum_out)
        self.op("act", lambda e: e.activation(o, i, func, **kw), rd, wr)

    def tt(self, out, in0, in1, op, eng="dve"):
        o, a, b = _u(out)[0], _u(in0)[0], _u(in1)[0]
        self.op(eng, lambda e: e.tensor_tensor(o, a, b, op), [in0, in1], [out])

    def ts(self, out, in0, s1, s2, op0, op1=None, eng="dve", accum_out=None):
        o, a = _u(out)[0], _u(in0)[0]
        rd = [in0]
        v1 = s1
        if not isinstance(s1, (int, float)):
            v1 = _u(s1)[0]
            rd.append(s1)
        v2 = s2
        if s2 is not None and not isinstance(s2, (int, float)):
            v2 = _u(s2)[0]
            rd.append(s2)
        wr = [out]
        kw = {}
        if accum_out is not None:
            kw["accum_out"] = _u(accum_out)[0]
            wr.append(accum_out)
        if op1 is None:
            self.op(eng, lambda e: e.tensor_scalar(o, a, v1, None, op0, **kw), rd, wr)
        else:
            self.op(eng, lambda e: e.tensor_scalar(o, a, v1, v2, op0, op1, **kw), rd, wr)

    def stt(self, out, in0, scalar, in1, op0, op1, eng="dve"):
        o, a, b = _u(out)[0], _u(in0)[0], _u(in1)[0]
        rd = [in0, in1]
        sv = scalar
        if not isinstance(scalar, (int, float)):
            sv = _u(scalar)[0]
            rd.append(scalar)
        self.op("dve", lambda e: e.scalar_tensor_tensor(o, a, sv, b, op0, op1), rd, [out])

    def cp(self, out, in_, eng="dve"):
        o, i = _u(out)[0], _u(in_)[0]
        if eng == "act":
            self.op("act", lambda e: e.copy(o, i), [in_], [out])
        else:
            self.op(eng, lambda e: e.tensor_copy(o, i), [in_], [out])

    def memset(self, out, val, eng="pool"):
        o = _u(out)[0]
        self.op(eng, lambda e: e.memset(o, val), [], [out])

    def scan(self, out, d0, d1, init, op0, op1):
        o, a, b = _u(out)[0], _u(d0)[0], _u(d1)[0]
        rd = [d0, d1]
        iv = init
        if not isinstance(init, (int, float)):
            iv = _u(init)[0]
            rd.append(init)
        self.op("dve", lambda e: e.tensor_tensor_scan(o, a, b, iv, op0, op1), rd, [out])

    def recip(self, out, in_):
        o, i = _u(out)[0], _u(in_)[0]
        self.op("dve", lambda e: e.reciprocal(o, i), [in_], [out])

    def asel(self, out, in_, pattern, cmp, fill, base, cm):
        o, i = _u(out)[0], _u(in_)[0]
        self.op("pool", lambda e: e.affine_select(o, i, pattern, cmp, fill, base=base, channel_multiplier=cm),
                [in_], [out])

    def emit(self):
        nc = self.nc
        fin = []
        for j in range(min(self.NQ, self.ndma)):
            n_on = (self.ndma - j + self.NQ - 1) // self.NQ
            fin.append((self.qsem[j], 16 * n_on))
        ops = self.ops
        sems = self.sem

        def run(e, lst):
            for fn, waits, tok, inc in lst:
                for (s, v) in waits:
                    e.wait_ge(s, v)
                fn(e).then_inc(tok[0], inc)

        with nc.Block() as block:
            @block.sync
            def _(e):
                run(e, ops["sp"])
                for (s, v) in fin:
                    e.wait_ge(s, v)

            @block.tensor
            def _(e):
                run(e, ops["pe"])

            @block.vector
            def _(e):
                run(e, ops["dve"])

            @block.scalar
            def _(e):
                run(e, ops["act"])

            @block.gpsimd
            def _(e):
                run(e, ops["pool"])
        self.es.close()


PI = math.pi
PARAMS = [
    ("ln_in_g", [D]), ("ln_in_b", [D]), ("w_in", [DEPTH, D, NIN]),
    ("s5_lambda_re", [DEPTH, 16, 64]), ("s5_lambda_im", [DEPTH, 16, 64]), ("s5_log_dt", [DEPTH, 16]),
    ("s5_b_re", [DEPTH, 16, 64, 16]), ("s5_b_im", [DEPTH, 16, 64, 16]),
    ("s5_c_re", [DEPTH, 16, 16, 64]), ("s5_c_im", [DEPTH, 16, 16, 64]), ("s5_d", [DEPTH, 256]),
    ("s5_w_glu", [DEPTH, 256, 256]), ("s5_b_glu", [DEPTH, 256]),
    ("rw_mu", [DEPTH, 1024]), ("rw_w0", [DEPTH, 256]), ("rw_w2", [DEPTH, 64, 256]), ("rw_a0", [DEPTH, 256]),
    ("rw_a2", [DEPTH, 64, 256]), ("rw_g2", [DEPTH, 128, 256]), ("rw_k_k", [DEPTH, 256]), ("rw_k_a", [DEPTH, 256]),
    ("rw_r_k", [DEPTH, 4, 64]), ("rw_ln_g", [DEPTH, 256]), ("rw_ln_b", [DEPTH, 256]),
    ("ml_conv_w", [DEPTH, 4, 512]), ("ml_conv_b", [DEPTH, 512]), ("ml_b_i", [DEPTH, 4]), ("ml_b_f", [DEPTH, 4]),
    ("ml_ln_g", [DEPTH, 256]), ("w_out", [DEPTH, D, D]), ("ln1_g", [DEPTH, D]), ("ln1_b", [DEPTH, D]),
    ("ffn_w_up", [DEPTH, D, 2 * DFF]), ("ffn_conv_w", [DEPTH, 3, DFF]), ("ffn_conv_b", [DEPTH, DFF]),
    ("ffn_w_down", [DEPTH, DFF, D]), ("ln2_g", [DEPTH, D]), ("ln2_b", [DEPTH, D]),
]


nc_out_handle = [None]


def build_program(dbg=None, nlayers=DEPTH, mixers=("s5", "rw", "sb", "ml")):
    nc = bass.Bass("TRN2", target_bir_lowering=False)
    P = Prog(nc)
    x_in = nc.dram_tensor("x", [S, D], F32, kind="ExternalInput")
    prm = {n: nc.dram_tensor(n, shp, F32, kind="ExternalInput") for n, shp in PARAMS}
    out = nc.dram_tensor("out", [S, D], F32, kind="ExternalOutput")
    nc_out_handle[0] = out

    hT = P.dram("hT", [D, S], F32)
    h1T = P.dram("h1T", [D, S], F32)
    zT = P.dram("zT", [25 * 128, S], F32)
    yT = P.dram("yT", [D, S], BF16)
    vTM = P.dram("vTM", [S, 512], BF16)
    vrw = P.dram("vrw", [S, 256], F32)
    grw = P.dram("grw", [S, 256], F32)
    yrw = P.dram("yrw", [S, 256], F32)

    ident = P.sb("ident", [128, 128], F32)
    P.memset(ident[:], 1.0)
    P.asel(ident[:], ident[:], [[-1, 128]], ALU.is_equal, 0.0, 0, 1)
    identb = P.sb("identb", [128, 128], BF16)
    P.cp(identb[:], ident[:])
    onesm = P.sb("onesm", [128, 128], F32)
    P.memset(onesm[:], 1.0 / D)
    ones1 = P.sb("ones1", [128, 128], F32)
    P.memset(ones1[:], 1.0)
    psb = [P.ps("psb%d" % i, [128, 512], F32) for i in range(8)]
    rr = [0]

    def ev_eng():
        rr[0] += 1
        return "act" if rr[0] % 2 else "dve"

    def ln_block(L, src, dst_dram, tb, g, b):
        pm = psb[6]
        for k in range(8):
            P.mm(pm[:], onesm[:], src[:, k, :], start=(k == 0), stop=(k == 7))
        P.cp(L["mean"][:], pm[:], eng="act")
        for k in range(8):
            P.tt(src[:, k, :], src[:, k, :], L["mean"][:], ALU.subtract, eng=("dve" if k % 2 else "pool"))
        pv = psb[7]
        for k in range(8):
            sq = L["sq"][k % 2]
            P.act(sq[:], src[:, k, :], AF.Square)
            P.mm(pv[:], onesm[:], sq[:], start=(k == 0), stop=(k == 7))
        P.ts(L["rstd"][:], pv[:], LN_EPS, None, ALU.add)
        P.act(L["rstd"][:], L["rstd"][:], AF.Sqrt)
        P.recip(L["rstd"][:], L["rstd"][:])
        for k in range(8):
            P.tt(src[:, k, :], src[:, k, :], L["rstd"][:], ALU.mult, eng=("dve" if k % 2 else "pool"))
            P.ts(L["ho"][:, k, :], src[:, k, :], g[:, k:k + 1], b[:, k:k + 1], ALU.mult, ALU.add)
        P.dma(R(dst_dram[:, tb * 512:(tb + 1) * 512].rearrange("(k p) t -> p k t", p=128), tb), L["ho"][:])

    def ln_alloc():
        return {"mean": P.sb("ln_mean", [128, 512], F32), "rstd": P.sb("ln_rstd", [128, 512], F32),
                "sq": [P.sb("ln_sq%d" % i, [128, 512], F32) for i in range(2)],
                "ho": P.sb("ln_out", [128, 8, 512], F32)}

    def load_cols(dst, vec_ap, n):
        P.dma(dst, vec_ap.rearrange("(k p) -> p k", p=128), allow_slow_non_contiguous=True)

    P.push()
    xt = [P.sb("xt%d" % i, [128, D], F32) for i in range(2)]
    xT2 = [P.sb("xTblk%d" % i, [128, 8, 512], F32) for i in range(2)]
    gcol = P.sb("gcol", [128, 8], F32)
    bcol = P.sb("bcol", [128, 8], F32)
    load_cols(gcol[:], prm["ln_in_g"][:], 8)
    load_cols(bcol[:], prm["ln_in_b"][:], 8)
    L = ln_alloc()
    for tb in range(9):
        if tb < 8:
            xT = xT2[tb % 2]
            for j in range(4):
                tt_ = tb * 4 + j
                xb = xt[tt_ % 2]
                P.dma(xb[:], x_in[tt_ * 128:(tt_ + 1) * 128, :])
                for k in range(8):
                    pt = psb[k % 4]
                    P.tr(pt[:, 0:128], xb[:, k * 128:(k + 1) * 128], ident[:])
                    P.cp(xT[:, k, j * 128:(j + 1) * 128], pt[:, 0:128], eng=ev_eng())
        if tb >= 1:
            ln_block(L, xT2[(tb - 1) % 2], hT, tb - 1, gcol, bcol)
    P.pop()

    for l in range(nlayers):
        phase_A(P, nc, prm, l, hT, zT, vTM, vrw, psb, ev_eng)
        if "s5" in mixers:
            phase_s5(P, nc, prm, l, zT, yT, psb, ident, ev_eng)
        if "sb" in mixers:
            phase_sb(P, nc, prm, l, zT, yT, vTM, psb, ones1, ev_eng)
        if "ml" in mixers:
            phase_ml(P, nc, prm, l, zT, yT, vTM, psb, ident, ev_eng)
        if "rw" in mixers:
            phase_rw(P, nc, prm, l, zT, yT, vrw, grw, yrw, psb, ident, identb, ev_eng)
        if dbg == "mix":
            break
        phase_proj_ln(P, nc, l, prm["w_out"][l], 8, yT, True, hT, h1T, prm["ln1_g"][l], prm["ln1_b"][l],
                      psb, ln_alloc, ln_block, load_cols, ev_eng)
        phase_ffn(P, nc, prm, l, h1T, hT, psb, ln_alloc, ln_block, load_cols, ev_eng)

    P.push()
    import os
    if os.environ.get("DUMPT"):
        pass
    elif dbg == "mix" and os.environ.get("DUMP"):
        src = {"yrw": yrw, "vrw": vrw, "grw": grw}[os.environ["DUMP"]]
        stf = P.sb("stgf2", [128, 32, 256], F32)
        P.dma(stf[:], src[:, :].rearrange("(n p) c -> p n c", p=128))
        P.dma(out[:, 0:256].rearrange("(n p) c -> p n c", p=128), stf[:])
    elif dbg == "mix":
        stg = P.sb("stgb", [128, 8, 512], BF16)
        stf = P.sb("stgf", [128, 8, 512], F32)
        ov = out[:, :].rearrange("(a b) d -> a (b d)", a=D)
        for tb in range(8):
            P.dma(stg[:], yT[:, tb * 512:(tb + 1) * 512].rearrange("(k p) t -> p k t", p=128))
            P.cp(stf[:], stg[:])
            P.dma(R(ov[:, tb * 512:(tb + 1) * 512].rearrange("(k p) t -> p k t", p=128), tb), stf[:])
    else:
        hb = [P.sb("fin_h%d" % i, [128, 8, 512], F32) for i in range(2)]
        ob = [P.sb("fin_o%d" % i, [128, D], F32) for i in range(2)]
        n = 0
        for tb in range(8):
            hbb = hb[tb % 2]
            P.dma(hbb[:], hT[:, tb * 512:(tb + 1) * 512].rearrange("(k p) t -> p k t", p=128))
            for j in range(4):
                o = ob[n % 2]
                n += 1
                for k in range(8):
                    pt = psb[k % 4]
                    P.tr(pt[:, 0:128], hbb[:, k, j * 128:(j + 1) * 128], ident[:])
                    P.cp(o[:, k * 128:(k + 1) * 128], pt[:, 0:128], eng=ev_eng())
                tt_ = tb * 4 + j
                P.dma(R(out[tt_ * 128:(tt_ + 1) * 128, :], tt_), o[:])
    P.pop()
    P.emit()
    return nc


def load_cast(P, dst_bf, src_dram_ap, stage, eng):
    P.dma(stage, src_dram_ap)
    P.cp(dst_bf, stage, eng=eng)


def phase_A(P, nc, prm, l, hT, zT, vTM, vrw, psb, ev_eng):
    P.push()
    wA = P.sb("wA", [128, 8, NIN], BF16)
    hb = P.sb("hTb", [128, 8, S + 1], BF16)
    wst = [P.sb("wAst%d" % i, [128, NIN], F32) for i in range(2)]
    hst = [P.sb("hst%d" % i, [128, 2048], F32) for i in range(2)]
    zo = [P.sb("zo%d" % i, [128, 512], F32) for i in range(3)]
    vo = [P.sb("vo%d" % i, [128, 512], BF16) for i in range(2)]
    vo2 = [P.sb("vo2%d" % i, [128, 256], F32) for i in range(2)]
    wv1 = P.sb("wv1", [128, 8, 256], BF16)
    wv2 = P.sb("wv2", [128, 8, 256], BF16)
    mur = P.sb("mur", [128, 256], F32)
    tmpf = P.sb("wvtmp", [128, 256], F32)
    tmpg = P.sb("wvtmp2", [128, 256], F32)
    P.dma(mur[:], prm["rw_mu"][l, 512:768].partition_broadcast(128))
    P.memset(hb[:, :, 0:1], 0.0)
    n = 0
    for k in range(8):
        st = wst[k % 2]
        P.dma(st[:], prm["w_in"][l, k * 128:(k + 1) * 128, :])
        P.cp(wA[:, k, :], st[:], eng=("act" if k % 2 else "dve"))
        P.tt(tmpf[:], st[:, 768:1024], mur[:], ALU.mult)
        P.cp(wv2[:, k, :], tmpf[:], eng="pool")
        P.tt(tmpg[:], st[:, 768:1024], tmpf[:], ALU.subtract)
        P.cp(wv1[:, k, :], tmpg[:], eng="pool")
        for hh in range(2):
            s2 = hst[n % 2]
            n += 1
            P.dma(s2[:], hT[k * 128:(k + 1) * 128, hh * 2048:(hh + 1) * 2048])
            P.cp(hb[:, k, 1 + hh * 2048:1 + (hh + 1) * 2048], s2[:], eng=("act" if n % 2 else "dve"))
    n = 0
    for m in range(25):
        msz = min(128, NIN - m * 128)
        for tb in range(8):
            ps = psb[n % 4]
            for k in range(8):
                P.mm(ps[0:msz, :], wA[:, k, m * 128:m * 128 + msz], hb[:, k, 1 + tb * 512:1 + (tb + 1) * 512],
                     start=(k == 0), stop=(k == 7))
            o = zo[n % 3]
            P.cp(o[0:msz, :], ps[0:msz, :], eng=ev_eng())
            P.dma(R(zT[m * 128:m * 128 + msz, tb * 512:(tb + 1) * 512], "%d_%d" % (m, tb)), o[0:msz, :])
            n += 1
    for tt_ in range(32):
        ps = psb[4 + tt_ % 2]
        lo = 1 + tt_ * 128
        for (c0, o0) in ((1792, 0), (2560, 256)):
            for k in range(8):
                P.mm(ps[:, o0:o0 + 256], hb[:, k, lo:lo + 128], wA[:, k, c0:c0 + 256], start=(k == 0), stop=(k == 7))
        o = vo[tt_ % 2]
        P.cp(o[:], ps[:], eng=ev_eng())
        P.dma(R(vTM[tt_ * 128:(tt_ + 1) * 128, :], tt_), o[:])
        ps2 = psb[6 + tt_ % 2]
        for k in range(8):
            P.mm(ps2[:, 0:256], hb[:, k, lo:lo + 128], wv1[:, k, :], start=(k == 0), stop=False)
            P.mm(ps2[:, 0:256], hb[:, k, lo - 1:lo + 127], wv2[:, k, :], start=False, stop=(k == 7))
        o2 = vo2[tt_ % 2]
        P.cp(o2[:], ps2[:, 0:256], eng=ev_eng())
        P.dma(R(vrw[tt_ * 128:(tt_ + 1) * 128, :], tt_), o2[:])
    P.pop()


def phase_proj_ln(P, nc, l, w_dram, nk, src, src_is_bf, resid, dst, g_ap, b_ap, psb, ln_alloc, ln_block, load_cols, ev_eng):
    P.push()
    w = P.sb("pw", [128, nk, D], BF16)
    st = [P.sb("pwst%d" % i, [128, D], F32) for i in range(4)]
    for k in range(nk):
        P.dma(st[k % 4][:], w_dram[k * 128:(k + 1) * 128, :])
        P.cp(w[:, k, :], st[k % 4][:], eng=("act" if k % 2 else "dve"))
    gcol = P.sb("pg", [128, 8], F32)
    bcol = P.sb("pb", [128, 8], F32)
    load_cols(gcol[:], g_ap, 8)
    load_cols(bcol[:], b_ap, 8)
    sb_ = [P.sb("psrc%d" % i, [128, nk, 512], BF16) for i in range(2)]
    res2 = [P.sb("pres%d" % i, [128, 8, 512], F32) for i in range(3)]
    L = ln_alloc()

    def loads(tb):
        P.dma(sb_[tb % 2][:], src[:, tb * 512:(tb + 1) * 512].rearrange("(k p) t -> p k t", p=128))
        P.dma(res2[tb % 3][:], resid[:, tb * 512:(tb + 1) * 512].rearrange("(k p) t -> p k t", p=128))
    loads(0)
    for tb in range(9):
        if tb < 8:
            res = res2[tb % 3]
            s_ = sb_[tb % 2]
            for m in range(8):
                ps = psb[m % 4]
                for k in range(nk):
                    P.mm(ps[:], w[:, k, m * 128:(m + 1) * 128], s_[:, k, :], start=(k == 0), stop=(k == nk - 1))
                P.stt(res[:, m, :], res[:, m, :], DN_ALPHA, ps[:], ALU.mult, ALU.add)
                if m == 3 and tb + 1 < 8:
                    loads(tb + 1)
        if tb >= 1:
            ln_block(L, res2[(tb - 1) % 3], dst, tb - 1, gcol, bcol)
    P.pop()


def phase_ffn(P, nc, prm, l, h1T, hT, psb, ln_alloc, ln_block, load_cols, ev_eng):
    aT = P.dram("aT%d" % l, [DFF, S], BF16)
    NF = DFF // 128
    P.push()
    wup = P.sb("wup", [128, 8, 2 * DFF], BF16)
    st = [P.sb("wupst%d" % i, [128, 1408], F32) for i in range(4)]
    n = 0
    for k in range(8):
        for q in range(4):
            s_ = st[n % 4]
            P.dma(s_[:], prm["ffn_w_up"][l, k * 128:(k + 1) * 128, q * 1408:(q + 1) * 1408])
            P.cp(wup[:, k, q * 1408:(q + 1) * 1408], s_[:], eng=("act" if n % 2 else "dve"))
            n += 1
    cw = P.sb("fcw", [128, NF, 3], F32)
    cb = P.sb("fcb", [128, NF], F32)
    for i in range(3):
        P.dma(cw[:, :, i], prm["ffn_conv_w"][l, i].rearrange("(f p) -> p f", p=128))
    P.dma(cb[:], prm["ffn_conv_b"][l].rearrange("(f p) -> p f", p=128), allow_slow_non_contiguous=True)
    uprev = P.sb("uprev", [128, NF, 2], F32)
    P.memset(uprev[:], 0.0)
    hst = [P.sb("fhst%d" % i, [128, 8, 512], F32) for i in range(2)]
    hb = [P.sb("fhb%d" % i, [128, 8, 512], BF16) for i in range(2)]
    ub = [P.sb("fub%d" % i, [128, 514], F32) for i in range(4)]
    acc = [P.sb("facc%d" % i, [128, 512], F32) for i in range(4)]
    t1 = [P.sb("ft1%d" % i, [128, 512], F32) for i in range(4)]
    ao = [P.sb("fao%d" % i, [128, 512], BF16) for i in range(4)]
    n = 0
    def load_h(tb):
        P.dma(hst[tb % 2][:], h1T[:, tb * 512:(tb + 1) * 512].rearrange("(k p) t -> p k t", p=128))
        P.cp(hb[tb % 2][:], hst[tb % 2][:], eng="dve")
    load_h(0)
    for tb in range(8):
        hs, hbb = hst[tb % 2], hb[tb % 2]
        for f in range(NF):
            if f == 2 and tb + 1 < 8:
                load_h(tb + 1)
            pu, pg = psb[(2 * n) % 8], psb[(2 * n + 1) % 8]
            for k in range(8):
                P.mm(pu[:], wup[:, k, f * 128:(f + 1) * 128], hbb[:, k, :], start=(k == 0), stop=(k == 7))
            for k in range(8):
                P.mm(pg[:], wup[:, k, DFF + f * 128:DFF + (f + 1) * 128], hbb[:, k, :], start=(k == 0), stop=(k == 7))
            u, a, t = ub[n % 4], acc[n % 4], t1[n % 4]
            P.cp(u[:, 0:2], uprev[:, f, :], eng="pool")
            P.cp(u[:, 2:514], pu[:], eng="act")
            P.cp(uprev[:, f, :], u[:, 512:514], eng="pool")
            P.ts(a[:], u[:, 2:514], cw[:, f, 2:3], cb[:, f:f + 1], ALU.mult, ALU.add)
            P.stt(a[:], u[:, 1:513], cw[:, f, 1:2], a[:], ALU.mult, ALU.add)
            P.stt(a[:], u[:, 0:512], cw[:, f, 0:1], a[:], ALU.mult, ALU.add)
            gelu(P, t[:], a[:], u[:, 0:512])
            o = ao[n % 4]
            P.tt(o[:], t[:], pg[:], ALU.mult)
            P.dma(R(aT[f * 128:(f + 1) * 128, tb * 512:(tb + 1) * 512], "%d_%d" % (f, tb)), o[:])
            n += 1
    P.pop()
    P.push()
    wdn = P.sb("wdn", [128, NF, D], BF16)
    st = [P.sb("wdnst%d" % i, [128, D], F32) for i in range(4)]
    for k in range(NF):
        P.dma(st[k % 4][:], prm["ffn_w_down"][l, k * 128:(k + 1) * 128, :])
        P.cp(wdn[:, k, :], st[k % 4][:], eng=("act" if k % 2 else "dve"))
    gcol = P.sb("fg", [128, 8], F32)
    bcol = P.sb("fb", [128, 8], F32)
    load_cols(gcol[:], prm["ln2_g"][l], 8)
    load_cols(bcol[:], prm["ln2_b"][l], 8)
    ab = [P.sb("fab%d" % i, [128, NF, 512], BF16) for i in range(2)]
    res2 = [P.sb("fres%d" % i, [128, 8, 512], F32) for i in range(3)]
    L = ln_alloc()

    def loads(tb):
        P.dma(ab[tb % 2][:], aT[:, tb * 512:(tb + 1) * 512].rearrange("(k p) t -> p k t", p=128))
        P.dma(res2[tb % 3][:], h1T[:, tb * 512:(tb + 1) * 512].rearrange("(k p) t -> p k t", p=128))
    loads(0)
    for tb in range(9):
        if tb < 8:
            res = res2[tb % 3]
            a_ = ab[tb % 2]
            for m in range(8):
                ps = psb[m % 4]
                for k in range(NF):
                    P.mm(ps[:], wdn[:, k, m * 128:(m + 1) * 128], a_[:, k, :], start=(k == 0), stop=(k == NF - 1))
                P.stt(res[:, m, :], res[:, m, :], DN_ALPHA, ps[:], ALU.mult, ALU.add)
                if m == 3 and tb + 1 < 8:
                    loads(tb + 1)
        if tb >= 1:
            ln_block(L, res2[(tb - 1) % 3], hT, tb - 1, gcol, bcol)
    P.pop()


def gelu(P, out, x, tmp, eng="dve"):
    P.tt(tmp, x, x, ALU.mult, eng="pool")
    P.ts(tmp, tmp, 0.044715, 1.0, ALU.mult, ALU.add, eng="pool")
    P.tt(tmp, tmp, x, ALU.mult, eng="pool")
    P.act(tmp, tmp, AF.Sigmoid, scale=1.5957691216057308)
    P.tt(out, tmp, x, ALU.mult, eng=eng)


def sincos(P, pre, ang, shape):
    I32 = mybir.dt.int32
    outs = []
    npi = P.sb(pre + "_npi", [shape[0], 1], F32)
    P.memset(npi[:], -PI)
    for nm, off in (("c", PI / 2), ("s", 0.0)):
        t = P.sb(pre + "_t" + nm, shape, F32)
        tf = P.sb(pre + "_f" + nm, shape, F32)
        ti = P.sb(pre + "_i" + nm, shape, I32)
        o = P.sb(pre + "_o" + nm, shape, F32)
        P.ts(t[:], ang, 64 * PI + PI + off, None, ALU.add)
        P.ts(tf[:], t[:], 1.0 / (2 * PI), None, ALU.mult)
        P.cp(ti[:], tf[:])
        P.cp(tf[:], ti[:])
        P.stt(t[:], tf[:], -2 * PI, t[:], ALU.mult, ALU.add)
        P.ts(tf[:], t[:], 0.0, 2 * PI, ALU.is_lt, ALU.mult)
        P.tt(t[:], t[:], tf[:], ALU.add)
        P.ts(tf[:], t[:], 2 * PI, -2 * PI, ALU.is_ge, ALU.mult)
        P.tt(t[:], t[:], tf[:], ALU.add)
        P.act(o[:], t[:], AF.Sin, bias=npi[:])
        outs.append(o)
    return outs[0], outs[1]


def phase_s5(P, nc, prm, l, zT, yT, psb, ident, ev_eng):
    P.push()
    lr = P.sb("s5lr", [128, 8], F32)
    li = P.sb("s5li", [128, 8], F32)
    dt = P.sb("s5dt", [128, 8], F32)
    br = P.sb("s5br", [128, 8, 16], F32)
    bi = P.sb("s5bi", [128, 8, 16], F32)
    for gs in range(2):
        sl = slice(gs * 64, (gs + 1) * 64)
        P.dma(lr[sl, :], prm["s5_lambda_re"][l, gs::2, :].rearrange("j p -> p j"), allow_slow_non_contiguous=True)
        P.dma(li[sl, :], prm["s5_lambda_im"][l, gs::2, :].rearrange("j p -> p j"), allow_slow_non_contiguous=True)
        P.dma(dt[sl, :], prm["s5_log_dt"][l, gs::2].partition_broadcast(64))
        P.dma(br[sl, :, :], prm["s5_b_re"][l, gs::2].rearrange("j p c -> p j c"))
        P.dma(bi[sl, :, :], prm["s5_b_im"][l, gs::2].rearrange("j p c -> p j c"))
    P.act(dt[:], dt[:], AF.Exp)
    mag = P.sb("s5mag", [128, 8], F32)
    ang = P.sb("s5ang", [128, 8], F32)
    P.tt(mag[:], lr[:], dt[:], ALU.mult)
    P.act(mag[:], mag[:], AF.Exp)
    P.tt(ang[:], li[:], dt[:], ALU.mult)
    cs, sn = sincos(P, "s5sc", ang[:], [128, 8])
    abr = P.sb("s5abr", [128, 8], F32)
    abi = P.sb("s5abi", [128, 8], F32)
    P.tt(abr[:], mag[:], cs[:], ALU.mult)
    P.tt(abi[:], mag[:], sn[:], ALU.mult)
    den = P.sb("s5den", [128, 8], F32)
    t0 = P.sb("s5t0", [128, 8], F32)
    t1 = P.sb("s5t1", [128, 8], F32)
    zr = P.sb("s5zr", [128, 8], F32)
    zi = P.sb("s5zi", [128, 8], F32)
    nzi = P.sb("s5nzi", [128, 8], F32)
    P.tt(den[:], lr[:], lr[:], ALU.mult)
    P.tt(t0[:], li[:], li[:], ALU.mult)
    P.tt(den[:], den[:], t0[:], ALU.add)
    P.recip(den[:], den[:])
    P.ts(t0[:], abr[:], -1.0, None, ALU.add)
    P.tt(zr[:], t0[:], lr[:], ALU.mult)
    P.tt(t1[:], abi[:], li[:], ALU.mult)
    P.tt(zr[:], zr[:], t1[:], ALU.add)
    P.tt(zr[:], zr[:], den[:], ALU.mult)
    P.tt(zi[:], abi[:], lr[:], ALU.mult)
    P.tt(t1[:], t0[:], li[:], ALU.mult)
    P.tt(zi[:], zi[:], t1[:], ALU.subtract)
    P.tt(zi[:], zi[:], den[:], ALU.mult)
    P.ts(nzi[:], zi[:], -1.0, None, ALU.mult)
    bbr = P.sb("s5bbr", [128, 8, 16], F32)
    bbi = P.sb("s5bbi", [128, 8, 16], F32)
    tb_ = P.sb("s5tb", [128, 8, 16], F32)
    for j in range(8):
        P.ts(tb_[:, j, :], bi[:, j, :], nzi[:, j:j + 1], None, ALU.mult)
        P.stt(bbr[:, j, :], br[:, j, :], zr[:, j:j + 1], tb_[:, j, :], ALU.mult, ALU.add)
        P.ts(tb_[:, j, :], br[:, j, :], zi[:, j:j + 1], None, ALU.mult)
        P.stt(bbi[:, j, :], bi[:, j, :], zr[:, j:j + 1], tb_[:, j, :], ALU.mult, ALU.add)
    LB = P.sb("s5LB", [128, 2, 8, 128], BF16)
    zz = [P.sb("s5zz%d" % i, [128, 128], F32) for i in range(2)]
    n = 0
    for comp, bb in enumerate((bbr, bbi)):
        for j in range(8):
            z_ = zz[n % 2]
            P.memset(z_[:], 0.0)
            for gs in range(2):
                g = 2 * j + gs
                c0 = 16 * (g % 8)
                P.cp(z_[gs * 64:(gs + 1) * 64, c0:c0 + 16], bb[gs * 64:(gs + 1) * 64, j, :], eng="pool")
            pt = psb[n % 2]
            P.tr(pt[:, 0:128], z_[:], ident[:])
            P.cp(LB[:, comp, j, :], pt[:, 0:128], eng=ev_eng())
            n += 1
    LC = P.sb("s5LC", [128, 2, 8, 128], BF16)
    P.memset(LC[:], 0.0)
    cc = P.sb("s5cc", [128, 128], F32)
    tt_ = P.sb("s5ctt", [128, 128], F32)
    for comp, nm in enumerate(("s5_c_re", "s5_c_im")):
        for hh in range(2):
            src = prm[nm][l, hh * 8:(hh + 1) * 8].rearrange("g c p -> (g c) p")
            P.dma(cc[:, 0:64], src)
            P.dma(cc[:, 64:128], src)
            pt = psb[2 + (comp * 2 + hh) % 2]
            P.tr(pt[:, 0:128], cc[:], ident[:])
            P.ts(tt_[:], pt[:, 0:128], (1.0 if comp == 0 else -1.0), None, ALU.mult)
            for jj in range(4):
                j = hh * 4 + jj
                for gs in range(2):
                    g = 2 * j + gs
                    c0 = 16 * (g % 8)
                    P.cp(LC[gs * 64:(gs + 1) * 64, comp, j, c0:c0 + 16], tt_[gs * 64:(gs + 1) * 64, c0:c0 + 16], eng="pool")
    wc = P.sb("s5wc", [128, 12, 8], F32)
    ws = P.sb("s5ws", [128, 12, 8], F32)
    P.cp(wc[:, 0, :], cs[:])
    P.cp(ws[:, 0, :], sn[:])
    for k in range(1, 12):
        P.tt(t0[:], wc[:, k - 1, :], wc[:, k - 1, :], ALU.mult)
        P.tt(t1[:], ws[:, k - 1, :], ws[:, k - 1, :], ALU.mult)
        P.tt(wc[:, k, :], t0[:], t1[:], ALU.subtract)
        P.tt(t0[:], wc[:, k - 1, :], ws[:, k - 1, :], ALU.mult)
        P.ts(ws[:, k, :], t0[:], 2.0, None, ALU.mult)
    ub = P.sb("s5ub", [128, 2, S], BF16)
    ust = [P.sb("s5ust%d" % i, [128, 2048], F32) for i in range(2)]
    n = 0
    for hh in range(2):
        for q in range(2):
            P.dma(ust[n % 2][:], zT[hh * 128:(hh + 1) * 128, q * 2048:(q + 1) * 2048])
            P.cp(ub[:, hh, q * 2048:(q + 1) * 2048], ust[n % 2][:], eng=("act" if n % 2 else "pool"))
            n += 1
    TC = P.sb("s5TC", [128, S], F32)
    TS = P.sb("s5TS", [128, S], F32)
    RHO = P.sb("s5RHO", [128, S], F32)
    Xr = P.sb("s5Xr", [128, S], F32)
    Xi = P.sb("s5Xi", [128, S], F32)
    yacc = P.sb("s5yacc", [128, 2, S], F32)
    tmp = [P.sb("s5tmp%d" % i, [128, 2048], F32) for i in range(2)]
    sbr = [P.sb("s5sbr%d" % i, [128, 512], BF16) for i in range(2)]
    sbi = [P.sb("s5sbi%d" % i, [128, 512], BF16) for i in range(2)]
    for j in range(8):
        hh = j // 4
        P.memset(TC[:, 0:1], 1.0, eng="dve")
        P.memset(TS[:, 0:1], 0.0, eng="dve")
        for k in range(12):
            n_ = 1 << k
            c_, s_ = wc[:, k, j:j + 1], ws[:, k, j:j + 1]
            tA, tB = tmp[0][:, 0:n_], tmp[1][:, 0:n_]
            P.act(tA, TS[:, 0:n_], AF.Copy, scale=s_)
            P.act(tB, TC[:, 0:n_], AF.Copy, scale=s_)
            P.stt(TC[:, n_:2 * n_], TC[:, 0:n_], c_, tA, ALU.mult, ALU.subtract)
            P.stt(TS[:, n_:2 * n_], TS[:, 0:n_], c_, tB, ALU.mult, ALU.add)
        P.ts(RHO[:], TC[:], 0.0, mag[:, j:j + 1], ALU.mult, ALU.add, eng="pool")
        for tb in range(8):
            sl = slice(tb * 512, (tb + 1) * 512)
            pr, pi_ = psb[(2 * tb) % 4], psb[(2 * tb + 1) % 4]
            P.mm(pr[:], LB[:, 0, j, :], ub[:, hh, sl])
            P.mm(pi_[:], LB[:, 1, j, :], ub[:, hh, sl])
            a_, b_ = tmp[0][:, (tb % 2) * 512:(tb % 2) * 512 + 512], tmp[1][:, (tb % 2) * 512:(tb % 2) * 512 + 512]
            P.tt(a_, pi_[:], TS[:, sl], ALU.mult)
            P.tt(Xr[:, sl], pr[:], TC[:, sl], ALU.mult)
            P.tt(Xr[:, sl], Xr[:, sl], a_, ALU.add, eng="pool")
            P.tt(b_, pr[:], TS[:, sl], ALU.mult)
            P.tt(Xi[:, sl], pi_[:], TC[:, sl], ALU.mult)
            P.tt(Xi[:, sl], Xi[:, sl], b_, ALU.subtract, eng="pool")
        P.scan(Xr[:], RHO[:], Xr[:], 0.0, ALU.mult, ALU.add)
        P.scan(Xi[:], RHO[:], Xi[:], 0.0, ALU.mult, ALU.add)
        for tb in range(8):
            sl = slice(tb * 512, (tb + 1) * 512)
            a_, b_ = tmp[0][:, (tb % 2) * 512:(tb % 2) * 512 + 512], tmp[1][:, (tb % 2) * 512:(tb % 2) * 512 + 512]
            sr_, si_ = sbr[tb % 2], sbi[tb % 2]
            P.tt(a_, Xi[:, sl], TS[:, sl], ALU.mult, eng="pool")
            P.tt(b_, Xr[:, sl], TC[:, sl], ALU.mult)
            P.tt(sr_[:], b_, a_, ALU.subtract)
            P.tt(a_, Xr[:, sl], TS[:, sl], ALU.mult, eng="pool")
            P.tt(b_, Xi[:, sl], TC[:, sl], ALU.mult)
            P.tt(si_[:], b_, a_, ALU.add)
            py = psb[4 + tb % 2]
            P.mm(py[:], LC[:, 0, j, :], sr_[:], start=True, stop=False)
            P.mm(py[:], LC[:, 1, j, :], si_[:], start=False, stop=True)
            if j % 4 == 0:
                P.cp(yacc[:, hh, sl], py[:], eng="act")
            else:
                P.tt(yacc[:, hh, sl], yacc[:, hh, sl], py[:], ALU.add)
    dcol = P.sb("s5d", [128, 2], F32)
    bg = P.sb("s5bg", [128, 2], F32)
    P.dma(dcol[:], prm["s5_d"][l].rearrange("(k p) -> p k", p=128), allow_slow_non_contiguous=True)
    P.dma(bg[:], prm["s5_b_glu"][l].rearrange("(k p) -> p k", p=128), allow_slow_non_contiguous=True)
    wg = P.sb("s5wg", [128, 2, 256], BF16)
    for k in range(2):
        P.dma(tmp[k][:, 0:256], prm["s5_w_glu"][l, k * 128:(k + 1) * 128, :])
        P.cp(wg[:, k, :], tmp[k][:, 0:256])
    yg = P.sb("s5yg", [128, 2, 512], F32)
    ygb = P.sb("s5ygb", [128, 2, 512], BF16)
    gt = P.sb("s5gt", [128, 512], F32)
    sg = P.sb("s5sg", [128, 512], F32)
    yo = [P.sb("s5yo%d" % i, [128, 512], BF16) for i in range(2)]
    u32 = [P.sb("s5u32%d" % i, [128, 512], F32) for i in range(2)]
    for tb in range(8):
        sl = slice(tb * 512, (tb + 1) * 512)
        for hh in range(2):
            P.dma(u32[hh][:], zT[hh * 128:(hh + 1) * 128, sl])
            P.stt(yacc[:, hh, sl], u32[hh][:], dcol[:, hh:hh + 1], yacc[:, hh, sl], ALU.mult, ALU.add)
            gelu(P, yg[:, hh, :], yacc[:, hh, sl], gt[:])
            P.cp(ygb[:, hh, :], yg[:, hh, :], eng="act")
        for h2 in range(2):
            ps = psb[6 + h2]
            for k in range(2):
                P.mm(ps[:], wg[:, k, h2 * 128:(h2 + 1) * 128], ygb[:, k, :], start=(k == 0), stop=(k == 1))
            P.act(sg[:], ps[:], AF.Sigmoid, bias=bg[:, h2:h2 + 1])
            o = yo[h2]
            P.tt(o[:], yg[:, h2, :], sg[:], ALU.mult)
            P.dma(R(yT[h2 * 128:(h2 + 1) * 128, sl], "s5_%d_%d" % (h2, tb)), o[:])
    P.pop()


def make_masks(P, pre, strict):
    ms = []
    for o in range(4):
        m = P.sb("%s_m%d" % (pre, o), [128, 512], F32)
        P.memset(m[:], 1.0)
        P.asel(m[:], m[:], [[1, 512]], (ALU.is_gt if strict else ALU.is_ge), 0.0, -128 * o, -1)
        ms.append(m)
    return ms


def load_heads_bf(P, pre, dst, zT, row0, scale, stages):
    n = 0
    for t in range(2):
        for q in range(2):
            st = stages[n % 2]
            P.dma(st[:], zT[row0 + t * 128:row0 + (t + 1) * 128, q * 2048:(q + 1) * 2048])
            if scale == 1.0:
                P.cp(dst[:, t, q * 2048:(q + 1) * 2048], st[:], eng=("act" if n % 2 else "pool"))
            else:
                P.act(dst[:, t, q * 2048:(q + 1) * 2048], st[:], AF.Copy, scale=scale)
            n += 1


def load_heads_zpad(P, dst, zT, row0, scale, stages):
    P.memset(dst[:], 0.0)
    n = 0
    for t in range(2):
        for q in range(2):
            st = stages[n % 2]
            n += 1
            P.dma(st[:], zT[row0 + t * 128:row0 + (t + 1) * 128, q * 2048:(q + 1) * 2048])
            for gs in range(2):
                pb = gs * 64
                P.act(dst[pb:pb + 64, 2 * t + gs, q * 2048:(q + 1) * 2048], st[pb:pb + 64, :], AF.Copy, scale=scale)


def phase_sb(P, nc, prm, l, zT, yT, vTM, psb, ones1, ev_eng):
    P.push()
    qb = P.sb("sbq", [128, 4, S], BF16)
    kb_ = P.sb("sbk", [128, 2, S], BF16)
    stg = [P.sb("sbst%d" % i, [128, 2048], F32) for i in range(2)]
    load_heads_zpad(P, qb, zT, 1280, 0.125, stg)
    load_heads_bf(P, "sbk", kb_, zT, 1536, 1.0, stg)
    vb = P.sb("sbv", [128, 32, 256], BF16)
    P.dma(vb[:], vTM[:, 0:256].rearrange("(n p) c -> p n c", p=128))
    masks = make_masks(P, "sbm", True)
    tri = P.sb("sbtri", [128, 128], F32)
    P.memset(tri[:], 1.0)
    P.asel(tri[:], tri[:], [[-1, 128]], ALU.is_ge, 0.0, 0, 1)
    trib = P.sb("sbtrib", [128, 128], BF16)
    P.cp(trib[:], tri[:])
    onesb = P.sb("sbonesb", [128, 128], BF16)
    P.memset(onesb[:], 1.0)
    hi_ = [P.sb("sbhi%d" % i, [128, 512], BF16) for i in range(2)]
    lo_ = [P.sb("sblo%d" % i, [128, 512], BF16) for i in range(2)]
    carry = P.sb("sbcarry", [128, 512], F32)
    e_ = [P.sb("sbe%d" % i, [128, 512], F32) for i in range(2)]
    sp_ = [P.sb("sbsp%d" % i, [128, 512], F32) for i in range(2)]
    tm_ = [P.sb("sbtm%d" % i, [128, 512], F32) for i in range(2)]
    w_ = [P.sb("sbw%d" % i, [128, 512], BF16) for i in range(2)]
    yo = [P.sb("sbyo%d" % i, [128, 512], BF16) for i in range(2)]
    steps = []
    n = 0
    for h in range(4):
        for sb in range(8):
            kbs = list(range(4 * sb + 3, -1, -1))
            for idx, kb in enumerate(kbs):
                steps.append(dict(h=h, sb=sb, kb=kb, idx=idx, nk=len(kbs), n=n))
                n += 1

    def banks(n):
        return psb[n % 3], psb[3 + n % 2], psb[5 + n % 2]

    def cols(st):
        o = st["kb"] - 4 * st["sb"]
        q0 = 128 * o if o > 0 else 0
        return slice(q0, 512), slice(st["sb"] * 512 + q0, (st["sb"] + 1) * 512)

    def stage_a(st):
        h, sb, kb, n = st["h"], st["sb"], st["kb"], st["n"]
        t = h // 2
        cq, qs = cols(st)
        diag = kb >= 4 * sb
        pl, pc, pt = banks(n)
        e, sp, hi, lo = e_[n % 2], sp_[n % 2], hi_[n % 2], lo_[n % 2]
        P.mm(pl[:, cq], kb_[:, t, kb * 128:(kb + 1) * 128], qb[:, h, qs])
        P.act(e[:, cq], pl[:, cq], AF.Exp)
        P.act(sp[:, cq], e[:, cq], AF.Ln, bias=1.0)
        if diag:
            P.tt(sp[:, cq], sp[:, cq], masks[kb - 4 * sb][:, cq], ALU.mult, eng="pool")
        P.cp(hi[:, cq], sp[:, cq], eng="act")
        P.tt(lo[:, cq], sp[:, cq], hi[:, cq], ALU.subtract, eng="pool")

    def stage_a2(st):
        n = st["n"]
        cq, qs = cols(st)
        pl, pc, pt = banks(n)
        hi, lo = hi_[n % 2], lo_[n % 2]
        P.mm(pc[:, cq], trib[:], hi[:, cq], start=True, stop=False)
        P.mm(pc[:, cq], trib[:], lo[:, cq], start=False, stop=True)
        if st["idx"] < st["nk"] - 1:
            P.mm(pt[:, cq], onesb[:], hi[:, cq], start=True, stop=False)
            P.mm(pt[:, cq], onesb[:], lo[:, cq], start=False, stop=True)

    def stage_b1(st):
        h, sb, kb, n, idx, nk = st["h"], st["sb"], st["kb"], st["n"], st["idx"], st["nk"]
        diag = kb >= 4 * sb
        cq, qs = cols(st)
        pl, pc, pt = banks(n)
        tm, w = tm_[n % 2], w_[n % 2]
        if idx == 0:
            P.memset(carry[:], 0.0)
            P.cp(tm[:, cq], pc[:, cq])
        else:
            P.tt(tm[:, cq], pc[:, cq], carry[:, cq], ALU.add)
        P.tt(tm[:, cq], pl[:, cq], tm[:, cq], ALU.subtract)
        if diag:
            P.act(tm[:, cq], tm[:, cq], AF.Exp)
            P.tt(w[:, cq], tm[:, cq], masks[kb - 4 * sb][:, cq], ALU.mult, eng="pool")
        else:
            P.act(w[:, cq], tm[:, cq], AF.Exp)

    def stage_b2(st):
        h, sb, kb, n, idx, nk = st["h"], st["sb"], st["kb"], st["n"], st["idx"], st["nk"]
        t, pb = h // 2, (h % 2) * 64
        cq, _ = cols(st)
        qs = slice(sb * 512, (sb + 1) * 512)
        pl, pc, pt = banks(n)
        w = w_[n % 2]
        po = psb[7]
        P.mm(po[:, cq], vb[:, kb, t * 128:(t + 1) * 128], w[:, cq], start=(idx == 0), stop=(idx == nk - 1))
        if idx < nk - 1:
            P.tt(carry[:, cq], carry[:, cq], pt[:, cq], ALU.add)
        else:
            o = yo[(h * 8 + sb) % 2]
            P.cp(o[pb:pb + 64, :], po[pb:pb + 64, :], eng="act")
            P.dma(R(yT[512 + h * 64:512 + (h + 1) * 64, qs], "sb_%d_%d" % (h, sb)), o[pb:pb + 64, :])

    ns = len(steps)
    for i in range(ns + 2):
        if i >= 2:
            stage_b1(steps[i - 2])
        if i < ns:
            stage_a(steps[i])
        if 1 <= i <= ns:
            stage_a2(steps[i - 1])
        if i >= 2:
            stage_b2(steps[i - 2])
    P.pop()


def phase_ml(P, nc, prm, l, zT, yT, vTM, psb, ident, ev_eng):
    P.push()
    cw = P.sb("mlcw", [128, 4, 4], F32)
    cb = P.sb("mlcb", [128, 4], F32)
    for i in range(4):
        P.dma(cw[:, :, i], prm["ml_conv_w"][l, i].rearrange("(t p) -> p t", p=128))
    P.dma(cb[:], prm["ml_conv_b"][l].rearrange("(t p) -> p t", p=128), allow_slow_non_contiguous=True)
    qz = P.sb("mlqz", [128, 4, S], BF16)
    kk_ = P.sb("mlkk", [128, 2, S], BF16)
    P.memset(qz[:], 0.0)
    va = P.sb("mlva", [128, 32, 4, 128], BF16)
    P.memset(va[:], 0.0)
    P.memset(va[:, :, :, 64:65], 1.0)
    P.push()
    xin2 = [P.sb("mlxin%d" % i, [128, S + 3], F32) for i in range(2)]
    acc = P.sb("mlacc", [128, S], F32)
    vst = P.sb("mlvst", [128, 32, 256], BF16)
    P.dma(vst[:], vTM[:, 256:512].rearrange("(n p) c -> p n c", p=128))
    for h in range(4):
        P.cp(va[:, :, h, 0:64], vst[:, :, h * 64:(h + 1) * 64], eng=("pool" if h % 2 else "dve"))
    for i in range(2):
        P.memset(xin2[i][:, 0:3], 0.0)
    def ld_x(t):
        row0 = 2048 + t * 128
        for q in range(2):
            P.dma(xin2[t % 2][:, 3 + q * 2048:3 + (q + 1) * 2048], zT[row0:row0 + 128, q * 2048:(q + 1) * 2048])
    ld_x(0)
    ld_x(1)
    for t in range(4):
        xin = xin2[t % 2]
        P.ts(acc[:], xin[:, 3:S + 3], cw[:, t, 3:4], cb[:, t:t + 1], ALU.mult, ALU.add)
        for i in range(3):
            P.stt(acc[:], xin[:, i:S + i], cw[:, t, i:i + 1], acc[:], ALU.mult, ALU.add)
        P.act(acc[:], acc[:], AF.Silu)
        if t < 2:
            for gs in range(2):
                pb = gs * 64
                P.cp(qz[pb:pb + 64, 2 * t + gs, :], acc[pb:pb + 64, :], eng=("act" if gs else "pool"))
        else:
            P.act(kk_[:, t - 2, :], acc[:], AF.Copy, scale=0.125)
        if t + 2 < 4:
            ld_x(t + 2)
    P.pop()
    gi = P.sb("mlgi", [4, S], F32)
    gf = P.sb("mlgf", [4, S], F32)
    bi_ = P.sb("mlbi", [4, 1], F32)
    bf_ = P.sb("mlbf", [4, 1], F32)
    P.dma(gi[:], zT[3072:3076, :])
    P.dma(gf[:], zT[3076:3080, :])
    P.dma(bi_[:], prm["ml_b_i"][l].rearrange("(p o) -> p o", o=1))
    P.dma(bf_[:], prm["ml_b_f"][l].rearrange("(p o) -> p o", o=1))
    P.ts(bf_[:], bf_[:], -1.0, None, ALU.mult)
    P.act(gf[:], gf[:], AF.Exp, bias=bf_[:], scale=-1.0)
    P.act(gf[:], gf[:], AF.Ln, bias=1.0)
    P.ts(gf[:], gf[:], -1.0, None, ALU.mult)
    Fc = gf
    zer = P.sb("mlzer", [4, 512], F32)
    P.memset(zer[:], 0.0)
    for q in range(8):
        sl = slice(q * 512, (q + 1) * 512)
        if q == 0:
            P.scan(Fc[:, sl], gf[:, sl], zer[:], 0.0, ALU.add, ALU.add)
        else:
            P.scan(Fc[:, sl], gf[:, sl], zer[:], Fc[:, q * 512 - 1:q * 512], ALU.add, ALU.add)
    P.ts(gi[:], gi[:], bi_[:], None, ALU.add)
    P.tt(gi[:], gi[:], Fc[:], ALU.subtract)
    colb = P.sb("mlcolb", [128, 32, 4], F32)
    for kb in range(32):
        pt = psb[kb % 2]
        P.tr(pt[:, 0:4], gi[0:4, kb * 128:(kb + 1) * 128], ident[0:4, 0:4])
        P.cp(colb[:, kb, :], pt[:, 0:4], eng=ev_eng())
    sel = P.sb("mlsel", [4, 4, 128], F32)
    P.memset(sel[:], 1.0)
    P.asel(sel[:], sel[:], [[-1, 4], [0, 128]], ALU.is_equal, 0.0, 0, 1)
    masks = make_masks(P, "mlm", False)
    selden = P.sb("mlselden", [128, 64], F32)
    P.memset(selden[:], 0.0)
    P.memset(selden[64:65, :], 1.0)
    j64 = P.sb("mlj64", [64, 64], F32)
    P.memset(j64[:], 1.0 / 64)
    lng = P.sb("mllng", [64, 4], F32)
    P.dma(lng[:], prm["ml_ln_g"][l].rearrange("(h p) -> p h", p=64), allow_slow_non_contiguous=True)
    frow = [P.sb("mlfrow%d" % i, [128, 512], F32) for i in range(2)]
    da = [P.sb("mlda%d" % i, [128, 512], F32) for i in range(4)]
    pp = [P.sb("mlpp%d" % i, [128, 512], BF16) for i in range(4)]
    o65 = P.sb("mlo65", [128, 512], F32)
    hh_ = P.sb("mlhh", [64, 512], F32)
    t1 = P.sb("mlt1", [64, 512], F32)
    t2 = P.sb("mlt2", [64, 512], F32)
    og = P.sb("mlog", [64, 512], F32)
    yo = [P.sb("mlyo%d" % i, [64, 512], BF16) for i in range(2)]
    steps = []
    n = 0
    for h in range(4):
        for sb in range(8):
            nkb = 4 * sb + 4
            for kb in range(nkb):
                steps.append(dict(h=h, sb=sb, kb=kb, nkb=nkb, n=n))
                n += 1

    def stage_a(st):
        h, sb, kb, n = st["h"], st["sb"], st["kb"], st["n"]
        t = h // 2
        qs = slice(sb * 512, (sb + 1) * 512)
        diag = kb >= 4 * sb
        if kb == 0:
            pf = psb[4]
            P.mm(pf[:], sel[:, h, :], Fc[:, qs])
            P.cp(frow[(h * 8 + sb) % 2][:], pf[:], eng="act")
        fr = frow[(h * 8 + sb) % 2]
        pl = psb[n % 4]
        d_ = da[n % 4]
        o_ = kb - 4 * sb
        q0 = 128 * o_ if o_ > 0 else 0
        cq = slice(q0, 512)
        P.mm(pl[:, cq], kk_[:, t, kb * 128:(kb + 1) * 128], qz[:, h, sb * 512 + q0:(sb + 1) * 512])
        if diag:
            P.ts(d_[:, cq], fr[:, cq], colb[:, kb, h:h + 1], 30.0, ALU.add, ALU.min)
            P.act(d_[:, cq], d_[:, cq], AF.Exp)
            P.tt(d_[:, cq], d_[:, cq], masks[kb - 4 * sb][:, cq], ALU.mult, eng="pool")
        else:
            P.act(d_[:], fr[:], AF.Exp, bias=colb[:, kb, h:h + 1])

    def stage_b(st):
        h, sb, kb, n, nkb = st["h"], st["sb"], st["kb"], st["n"], st["nkb"]
        qs = slice(sb * 512, (sb + 1) * 512)
        pl = psb[n % 4]
        d_, p_ = da[n % 4], pp[n % 4]
        po = psb[5]
        o_ = kb - 4 * sb
        q0 = 128 * o_ if o_ > 0 else 0
        cq = slice(q0, 512)
        P.tt(p_[:, cq], pl[:, cq], d_[:, cq], ALU.mult)
        P.mm(po[:, cq], va[:, kb, h, :], p_[:, cq], start=(kb == 0), stop=(kb == nkb - 1))
        if kb == nkb - 1:
            P.cp(o65[:], po[:], eng="act")
            pending.extend(epilogue(h, sb))
        for _ in range(2):
            if pending:
                pending.pop(0)()

    def epilogue(h, sb):
        qs = slice(sb * 512, (sb + 1) * 512)
        pd, pm = psb[6], psb[7]
        o = yo[(h * 8 + sb) % 2]
        return [
            lambda: P.dma(og[:], zT[2816 + h * 64:2816 + (h + 1) * 64, qs]),
            lambda: P.mm(pd[0:64, :], selden[:], o65[:]),
            lambda: P.act(t1[:], pd[0:64, :], AF.Abs),
            lambda: P.ts(t1[:], t1[:], 1.0, None, ALU.max),
            lambda: P.recip(t1[:], t1[:]),
            lambda: P.tt(hh_[:], o65[0:64, :], t1[:], ALU.mult),
            lambda: P.mm(pm[0:64, :], j64[:], hh_[:]),
            lambda: P.tt(hh_[:], hh_[:], pm[0:64, :], ALU.subtract),
            lambda: P.tt(t1[:], hh_[:], hh_[:], ALU.mult, eng="pool"),
            lambda: P.mm(pd[0:64, :], j64[:], t1[:]),
            lambda: P.ts(t2[:], pd[0:64, :], LN_EPS, None, ALU.add),
            lambda: P.act(t2[:], t2[:], AF.Sqrt),
            lambda: P.recip(t2[:], t2[:]),
            lambda: P.tt(hh_[:], hh_[:], t2[:], ALU.mult),
            lambda: P.act(og[:], og[:], AF.Sigmoid),
            lambda: P.stt(o[:], hh_[:], lng[:, h:h + 1], og[:], ALU.mult, ALU.mult),
            lambda: P.dma(R(yT[768 + h * 64:768 + (h + 1) * 64, qs], "ml_%d_%d" % (h, sb)), o[:]),
        ]

    pending = []
    for i, st in enumerate(steps):
        stage_a(st)
        if i >= 2:
            stage_b(steps[i - 2])
    stage_b(steps[-2])
    stage_b(steps[-1])
    while pending:
        pending.pop(0)()
    P.pop()


def phase_rw(P, nc, prm, l, zT, yT, vrw, grw, yrw, psb, ident, identb, ev_eng):
    RW0 = 256
    H = 2048
    P.push()
    def col2(name, src):
        t = P.sb(name, [128, 2], F32)
        P.dma(t[:], src.rearrange("(k p) -> p k", p=128), allow_slow_non_contiguous=True)
        return t
    mu = P.sb("rwmu", [128, 8], F32)
    P.dma(mu[:], prm["rw_mu"][l].rearrange("(k p) -> p k", p=128), allow_slow_non_contiguous=True)
    w0 = col2("rww0", prm["rw_w0"][l])
    a0 = col2("rwa0", prm["rw_a0"][l])
    kkc = col2("rwkk", prm["rw_k_k"][l])
    kac = col2("rwka", prm["rw_k_a"][l])
    rkc = col2("rwrk", prm["rw_r_k"][l].rearrange("h d -> (h d)"))
    omka = P.sb("rwomka", [128, 2], F32)
    P.ts(omka[:], kac[:], -1.0, 1.0, ALU.mult, ALU.add)
    stw = P.sb("rwstw", [128, 256], F32)
    w2b = P.sb("rww2b", [128, 256], BF16)
    P.dma(stw[0:64, :], prm["rw_w2"][l])
    P.dma(stw[64:128, :], prm["rw_a2"][l])
    P.cp(w2b[:], stw[:])
    stg2 = P.sb("rwstg2", [128, 256], F32)
    g2b = P.sb("rwg2b", [128, 256], BF16)
    P.dma(stg2[:], prm["rw_g2"][l])
    P.cp(g2b[:], stg2[:])
    lngr = P.sb("rwlng", [128, 256], F32)
    lnbr = P.sb("rwlnb", [128, 256], F32)
    P.dma(lngr[:], prm["rw_ln_g"][l].partition_broadcast(128))
    P.dma(lnbr[:], prm["rw_ln_b"][l].partition_broadcast(128))
    bo = P.sb("rwbo", [128, 128], F32)
    P.memset(bo[:], 0.0)
    P.memset(bo[0:64, 0:64], 1.0)
    P.memset(bo[64:128, 64:128], 1.0)
    hsel = P.sb("rwhsel", [128, 2], F32)
    P.memset(hsel[:], 0.0)
    P.memset(hsel[0:64, 0:1], 1.0)
    P.memset(hsel[64:128, 1:2], 1.0)
    cmask = P.sb("rwcmask", [128, H], F32)
    P.memset(cmask[:], 1.0)
    P.memset(cmask[:].rearrange("p (c t) -> p c t", t=64)[:, :, 0:1], 0.0)
    rt = P.sb("rwrt", [128, 2, S], BF16)
    kt = P.sb("rwkt", [128, 2, S], BF16)
    bt = P.sb("rwbt", [128, 2, S], BF16)
    at = P.sb("rwat", [128, 2, S], BF16)
    gend = P.sb("rwgend", [128, 2, 64], F32)
    rk = P.sb("rwrk_tm", [128, 32, 4], F32)
    lora = P.sb("rwlora", [128, S], BF16)
    sgb = P.sb("rwsgb", [128, S], BF16)
    P.push()
    buf = P.sb("rwbuf", [128, H + 1], F32)
    T = [P.sb("rwT%d" % i, [128, H], F32) for i in range(8)]

    def load_shift(dst, tile_idx, hf):
        r0 = RW0 + tile_idx * 128
        if hf == 0:
            P.memset(buf[:, 0:1], 0.0)
            P.dma(buf[:, 1:H + 1], zT[r0:r0 + 128, 0:H])
        else:
            P.dma(buf[:, 0:H + 1], zT[r0:r0 + 128, H - 1:2 * H])
        P.tt(dst, buf[:, 0:H], buf[:, 1:H + 1], ALU.subtract, eng="pool")
        P.stt(dst, dst, mu[:, tile_idx:tile_idx + 1], buf[:, 1:H + 1], ALU.mult, ALU.add)

    for hf in range(2):
        hs = slice(hf * H, (hf + 1) * H)
        load_shift(T[0][:], 6, hf)
        P.act(lora[0:64, hs], T[0][0:64, :], AF.Tanh)
        P.cp(lora[64:128, hs], T[0][64:128, :])
        load_shift(T[0][:], 7, hf)
        P.act(sgb[:, hs], T[0][:], AF.Sigmoid)
    n = 0
    import os
    RWSUB = int(os.environ.get("RWSUB", "9"))
    for hp in range(2 if RWSUB > 0 else 0):
        for hf in range(2):
            Tr, Tk, Ta, Tlw, Tkk, TG, Tt, Te = [t[:] for t in T]
            hs = slice(hf * H, (hf + 1) * H)
            load_shift(Tr, hp, hf)
            load_shift(Tk, 2 + hp, hf)
            for tb in range(4):
                bs = slice(tb * 512, (tb + 1) * 512)
                gs_ = slice(hf * H + tb * 512, hf * H + (tb + 1) * 512)
                pw, pa = psb[(2 * n) % 4], psb[(2 * n + 1) % 4]
                n += 1
                P.mm(pw[:], w2b[0:64, hp * 128:(hp + 1) * 128], lora[0:64, gs_])
                P.mm(pa[:], w2b[64:128, hp * 128:(hp + 1) * 128], lora[64:128, gs_])
                P.act(Tlw[:, bs], pw[:], AF.Sigmoid, bias=w0[:, hp:hp + 1])
                P.act(Ta[:, bs], pa[:], AF.Sigmoid, bias=a0[:, hp:hp + 1])
            if RWSUB <= 1:
                continue
            P.ts(Tlw, Tlw, -0.6065306597126334, None, ALU.mult)
            P.act(Tkk, Tk, AF.Copy, scale=kkc[:, hp:hp + 1])
            P.tt(Tt, Tkk, Tkk, ALU.mult, eng="pool")
            for tb in range(4):
                bs = slice(tb * 512, (tb + 1) * 512)
                ps = psb[4 + tb % 2]
                P.mm(ps[:], bo[:], Tt[:, bs])
                P.act(Te[:, bs], ps[:], AF.Sqrt)
            P.ts(Te, Te, 1e-12, None, ALU.max)
            P.recip(Te, Te)
            P.tt(Tkk, Tkk, Te, ALU.mult)
            if RWSUB <= 2:
                continue
            P.ts(Tt, Ta, kac[:, hp:hp + 1], omka[:, hp:hp + 1], ALU.mult, ALU.add)
            P.tt(Tk, Tk, Tt, ALU.mult)
            P.stt(Tt, Tr, rkc[:, hp:hp + 1], Tk, ALU.mult, ALU.mult)
            for t16 in range(16):
                tt_ = hf * 16 + t16
                ps = psb[6 + t16 % 2]
                P.mm(ps[:, 0:2], Tt[:, t16 * 128:(t16 + 1) * 128], hsel[:])
                P.cp(rk[:, tt_, 2 * hp:2 * hp + 2], ps[:, 0:2], eng=ev_eng())
            if RWSUB <= 3:
                continue
            P.scan(TG, cmask[:], Tlw, 0.0, ALU.mult, ALU.add)
            if RWSUB <= 4:
                continue
            P.act(Te, TG, AF.Exp)
            P.tt(rt[:, hp, hs], Tr, Te, ALU.mult)
            if RWSUB <= 5:
                continue
            P.cp(gend[:, hp, hf * 32:(hf + 1) * 32], Te.rearrange("p (c t) -> p c t", t=64)[:, :, 63], eng="pool")
            if RWSUB <= 6:
                continue
            P.tt(Tt, TG, Tlw, ALU.subtract, eng="pool")
            P.act(Tt, Tt, AF.Exp)
            P.stt(at[:, hp, hs], Tkk, -1.0, Tt, ALU.mult, ALU.mult)
            P.act(Te, TG, AF.Exp, scale=-1.0)
            P.tt(kt[:, hp, hs], Tk, Te, ALU.mult)
            P.tt(Tt, Tkk, Ta, ALU.mult, eng="pool")
            P.tt(bt[:, hp, hs], Tt, Te, ALU.mult)
    P.pop()
    import os
    if os.environ.get("DUMPT"):
        outd = nc_out_handle[0][:, :].rearrange("(p a) d -> p (a d)", p=128)
        dst_ = [P.sb("dst%d" % i, [128, 2048], F32) for i in range(2)]
        n_ = 0
        for ai, arr in enumerate((rt, kt, bt, at)):
            for hp in range(2):
                for q in range(2):
                    d_ = dst_[n_ % 2]
                    P.cp(d_[:], arr[:, hp, q * 2048:(q + 1) * 2048])
                    off = ai * 8192 + hp * 4096 + q * 2048
                    P.dma(R(outd[:, off:off + 2048], n_), d_[:])
                    n_ += 1
        P.pop()
        return
    RWSTOP = int(os.environ.get("RWSTOP", "9"))
    if RWSTOP <= 1:
        P.pop()
        return
    gst = [P.sb("rwgst%d" % i, [128, 256], F32) for i in range(2)]
    for tt_ in range(32):
        ps = psb[tt_ % 2]
        P.mm(ps[:, 0:256], sgb[:, tt_ * 128:(tt_ + 1) * 128], g2b[:])
        P.cp(gst[tt_ % 2][:], ps[:, 0:256], eng=ev_eng())
        P.dma(R(grw[tt_ * 128:(tt_ + 1) * 128, :], tt_), gst[tt_ % 2][:])
    odd = {}
    for nm, arr in (("r", rt), ("k", kt), ("b", bt), ("a", at)):
        o_ = P.sb("rwodd_" + nm, [64, 2, S], BF16)
        P.dma(o_[:], arr[64:128, :, :])
        odd[nm] = o_
    gall = P.sb("rwgall", [64, 4, 64], F32)
    for hp in range(2):
        P.cp(gall[:, 2 * hp, :], gend[0:64, hp, :], eng="pool")
        P.dma(gall[:, 2 * hp + 1, :], gend[64:128, hp, :])

    def fm(nm, arr, h, cs):
        return arr[0:64, h // 2, cs] if h % 2 == 0 else odd[nm][:, h // 2, cs]

    def mask_n(name, specs):
        n_ = len(specs)
        m = P.sb(name, [64, n_ * 4, 64], F32)
        P.memset(m[:], 1.0)
        for i, sp_ in enumerate(specs):
            if sp_ is None:
                continue
            cm, step, cmp = sp_
            P.asel(m[:, 4 * i:4 * i + 4, :], m[:, 4 * i:4 * i + 4, :], [[0, 4], [step, 64]], cmp, 0.0, 0, cm)
        return m
    S_MU = (-1, 1, ALU.is_gt)
    S_MUI = (-1, 1, ALU.is_ge)
    S_ML = (1, -1, ALU.is_gt)
    M0 = mask_n("rwM0", [S_MU, S_ML])
    M1 = mask_n("rwM1", [S_MU, S_MUI])
    M2 = mask_n("rwM2", [S_MUI, None])
    I4 = mask_n("rwI4", [(1, -1, ALU.is_equal)])
    M32 = P.sb("rwM32", [64, 4, 64], F32)
    Mb = P.sb("rwMb", [64, 4, 64], BF16)
    P.memset(M32[:], 0.0)
    P.memset(Mb[:], 0.0)
    NN = [P.sb("rwNN%d" % i, [64, 8, 64], BF16) for i in range(2)]
    AR = [P.sb("rwAR%d" % i, [64, 8, 64], BF16) for i in range(2)]
    RB = [P.sb("rwRB%d" % i, [64, 8, 64], BF16) for i in range(2)]
    KT = [P.sb("rwKT%d" % i, [64, 4, 64], BF16) for i in range(2)]
    Nk = [P.sb("rwNk%d" % i, [64, 4, 64], BF16) for i in range(2)]
    NkT = [P.sb("rwNkT%d" % i, [64, 4, 64], BF16) for i in range(2)]
    P32 = P.sb("rwP32", [64, 4, 64], F32)
    Pb = [P.sb("rwPb%d" % i, [64, 4, 64], BF16) for i in range(2)]
    V32 = [P.sb("rwV32%d" % i, [64, 4, 64], F32) for i in range(2)]
    Vb = [P.sb("rwVb%d" % i, [64, 4, 64], BF16) for i in range(2)]
    Xb = P.sb("rwXb", [64, 4, 64], BF16)
    Ub = P.sb("rwUb", [64, 4, 64], BF16)
    Yc = [P.sb("rwYc%d" % i, [64, 4, 64], F32) for i in range(2)]
    B0, B1, B2, B3, B4, B5, B6 = psb[0], psb[1], psb[2], psb[3], psb[4], psb[5], psb[6]

    def pvn(bank, lo, n_):
        return bank[0:64, lo:lo + 64 * n_].rearrange("p (a t) -> p a t", a=n_)

    def hc(h, lo=0):
        return slice(lo + h * 64, lo + (h + 1) * 64)

    def rw_pre(c):
        cs = slice(c * 64, (c + 1) * 64)
        nn, ar, rb, kt_ = NN[c % 2], AR[c % 2], RB[c % 2], KT[c % 2]
        pb_ = Pb[c % 2]

        def init():
            for h in range(4):
                r_, k_, b_, a_ = fm("r", rt, h, cs), fm("k", kt, h, cs), fm("b", bt, h, cs), fm("a", at, h, cs)
                idn = identb[0:64, 0:64]
                P.mm(B0[0:64, hc(h)], b_, a_)
                P.mm(B0[0:64, hc(h, 256)], a_, b_)
                P.mm(B1[0:64, hc(h)], k_, a_)
                P.mm(B1[0:64, hc(h, 256)], b_, r_)
                P.mm(B2[0:64, hc(h)], k_, r_)
                P.mm(B2[0:64, hc(h, 256)], b_, idn)
                P.mm(B3[0:64, hc(h)], k_, idn)
            P.tt(nn[:], pvn(B0, 0, 8), M0[:], ALU.mult)
            P.tt(P32[:], nn[:, 0:4, :], I4[:], ALU.add)
            P.cp(pb_[:], P32[:], eng="dve")
            P.tt(ar[:], pvn(B1, 0, 8), M1[:], ALU.mult)
            P.tt(rb[:], pvn(B2, 0, 8), M2[:], ALU.mult)
            P.cp(kt_[:], pvn(B3, 0, 4), eng="dve")

        def stage(i):
            def f():
                last = (i == 5)
                nxt = i % 2
                curN, curT = (nn[:, 0:4, :], nn[:, 4:8, :]) if i == 1 else (Nk[1 - nxt][:], NkT[1 - nxt][:])
                for h in range(4):
                    if not last:
                        P.mm(B4[0:64, hc(h)], curT[:, h, :], curN[:, h, :])
                    P.mm(B4[0:64, hc(h, 256)], curN[:, h, :], curT[:, h, :])
                P.cp(NkT[nxt][:], pvn(B4, 256, 4), eng="dve")
                if not last:
                    P.cp(Nk[nxt][:], pvn(B4, 0, 4), eng="dve")
                for h in range(4):
                    P.mm(B5[0:64, hc(h)], NkT[nxt][:, h, :], pb_[:, h, :])
                P.tt(P32[:], P32[:], pvn(B5, 0, 4), ALU.add)
                P.cp(pb_[:], P32[:], eng="dve")
            return f
        return [init] + [stage(i) for i in range(1, 6)]

    def rw_post(c):
        cs = slice(c * 64, (c + 1) * 64)
        ar, rb, kt_, pb_ = AR[c % 2], RB[c % 2], KT[c % 2], Pb[c % 2]
        v32, vb_ = V32[c % 2], Vb[c % 2]
        yc = Yc[c % 2]

        def fx():
            P.dma(v32[:], vrw[c * 64:(c + 1) * 64, :].rearrange("t (h v) -> t h v", h=4))
            P.cp(vb_[:], v32[:], eng="pool")
            for h in range(4):
                P.mm(B6[0:64, hc(h)], fm("a", at, h, cs), Mb[:, h, :], start=True, stop=False)
                P.mm(B6[0:64, hc(h)], ar[:, h, :], vb_[:, h, :], start=False, stop=True)
            P.cp(Xb[:], pvn(B6, 0, 4), eng="dve")

        def fu():
            for h in range(4):
                P.mm(B7[0:64, hc(h)], pb_[:, h, :], Xb[:, h, :])
            P.cp(Ub[:], pvn(B7, 0, 4), eng="dve")

        def fm_():
            for h in range(4):
                P.mm(B3[0:64, hc(h, 256)], rb[:, 4 + h, :], Ub[:, h, :], start=True, stop=False)
                P.mm(B3[0:64, hc(h, 256)], kt_[:, h, :], vb_[:, h, :], start=False, stop=True)
            P.tt(M32[:], M32[:], pvn(B3, 256, 4), ALU.add)
            for h in range(4):
                P.ts(M32[:, h, :], M32[:, h, :], gall[:, h, c:c + 1], None, ALU.mult)
            P.cp(Mb[:], M32[:], eng="dve")

        def fy():
            for h in range(4):
                P.mm(B6[0:64, hc(h, 256)], fm("r", rt, h, cs), Mbp[:, h, :], start=True, stop=False)
                P.mm(B6[0:64, hc(h, 256)], ar[:, 4 + h, :], Ub[:, h, :], start=False, stop=False)
                P.mm(B6[0:64, hc(h, 256)], rb[:, h, :], vb_[:, h, :], start=False, stop=True)
            P.cp(yc[:], pvn(B6, 256, 4), eng="pool" if False else "dve")
            P.dma(R(yrw[c * 64:(c + 1) * 64, :].rearrange("t (h v) -> t h v", h=4), c), yc[:])
        return [fx, fu, fy, fm_]

    B7 = psb[7]
    Mbp = Mb
    NCH = 64
    for f in rw_pre(0):
        f()
    for c in range(NCH):
        pre = rw_pre(c + 1) if c + 1 < NCH else []
        post = rw_post(c)
        order = []
        for i in range(max(len(pre), len(post))):
            if i < len(post):
                order.append(post[i])
            if i < len(pre):
                order.append(pre[i])
        for f in order:
            f()
    P.barrier()
    if RWSTOP <= 2:
        P.pop()
        return
    yin = [P.sb("rwyin%d" % i, [128, 4, 64], F32) for i in range(2)]
    vin = [P.sb("rwvin%d" % i, [128, 4, 64], F32) for i in range(2)]
    gin = [P.sb("rwgin%d" % i, [128, 256], F32) for i in range(2)]
    sq = P.sb("rwsq", [128, 4, 64], F32)
    st4 = P.sb("rwst4", [128, 4], F32)
    st5 = P.sb("rwst5", [128, 4], F32)
    yo = [P.sb("rwyo%d" % i, [128, 2, 512], BF16) for i in range(2)]
    for tt_ in range(32):
        y_, v_, g_ = yin[tt_ % 2], vin[tt_ % 2], gin[tt_ % 2]
        ts_ = slice(tt_ * 128, (tt_ + 1) * 128)
        P.dma(y_[:], yrw[ts_, :].rearrange("t (h d) -> t h d", h=4))
        P.dma(v_[:], vrw[ts_, :].rearrange("t (h d) -> t h d", h=4))
        P.dma(g_[:], grw[ts_, :])
        P.op("dve", lambda e, o=st4[:], i=y_[:]: e.reduce_sum(o, i, AX.X), [y_[:]], [st4[:]])
        P.ts(st4[:], st4[:], -1.0 / 64, None, ALU.mult)
        for h in range(4):
            P.ts(y_[:, h, :], y_[:, h, :], st4[:, h:h + 1], None, ALU.add, eng=("pool" if h % 2 else "dve"))
        P.tt(sq[:], y_[:], y_[:], ALU.mult, eng="pool")
        P.op("dve", lambda e, o=st5[:], i=sq[:]: e.reduce_sum(o, i, AX.X), [sq[:]], [st5[:]])
        P.ts(st5[:], st5[:], 1.0 / 64, 64e-5, ALU.mult, ALU.add)
        P.act(st5[:], st5[:], AF.Sqrt)
        P.recip(st5[:], st5[:])
        for h in range(4):
            P.ts(y_[:, h, :], y_[:, h, :], st5[:, h:h + 1], None, ALU.mult, eng=("pool" if h % 2 else "dve"))
        yf = y_[:].rearrange("p h d -> p (h d)")
        P.tt(yf, yf, lngr[:], ALU.mult)
        P.tt(yf, yf, lnbr[:], ALU.add, eng="pool")
        for h in range(4):
            P.stt(y_[:, h, :], v_[:, h, :], rk[:, tt_, h:h + 1], y_[:, h, :], ALU.mult, ALU.add)
        P.tt(yf, yf, g_[:], ALU.mult)
        o = yo[(tt_ // 4) % 2]
        for t in range(2):
            pt = psb[(2 * tt_ + t) % 4]
            P.tr(pt[:, 0:128], yf[:, t * 128:(t + 1) * 128], ident[:])
            P.cp(o[:, t, (tt_ % 4) * 128:(tt_ % 4 + 1) * 128], pt[:, 0:128], eng=ev_eng())
        if tt_ % 4 == 3:
            tb = tt_ // 4
            P.dma(R(yT[256:512, tb * 512:(tb + 1) * 512].rearrange("(k p) t -> p k t", p=128), "rw_%d" % tb), o[:])
    P.pop()


def kernel(**inputs):
    dbg = inputs.pop("_dbg", None)
    ncores = inputs.pop("_ncores", 8)
    nlayers = inputs.pop("_nlayers", DEPTH)
    mixers = inputs.pop("_mixers", ("s5", "rw", "sb", "ml"))
    nc = build_program(dbg, nlayers, mixers)
    x = np.ascontiguousarray(inputs["x"], dtype=np.float32)
    shared = {n: np.ascontiguousarray(inputs[n], dtype=np.float32) for n, _ in PARAMS}
    in_maps = []
    for c in range(ncores):
        m = {"x": x[c]}
        m.update(shared)
        in_maps.append(m)
    res = run_bass_kernel_spmd(nc, in_maps, core_ids=list(range(ncores)))
    return np.stack([r["out"] for r in res.results], axis=0)
```

```python
import math
from contextlib import ExitStack
import numpy as np
import concourse.bass as bass
import concourse.mybir as mybir
from concourse.bass_utils import run_bass_kernel_spmd

F32 = mybir.dt.float32
BF16 = mybir.dt.bfloat16
AF = mybir.ActivationFunctionType
ALU = mybir.AluOpType
AX = mybir.AxisListType

S = 4096
D = 1024
NIN = 3080
DFF = 2816
DEPTH = 2
LN_EPS = 1e-5
DN_ALPHA = (2 * DEPTH) ** 0.25


class R:
    def __init__(self, ap, tag):
        self.ap = ap
        self.tag = tag


def _u(x):
    if isinstance(x, R):
        return x.ap, "%s#%s" % (x.ap.name, x.tag)
    return x, x.name


class Prog:
    NQ = 8

    def __init__(self, nc):
        self.nc = nc
        self.es = ExitStack()
        self.ops = {k: [] for k in ("pe", "dve", "act", "pool", "sp")}
        self.sem = {k: self.es.enter_context(nc.semaphore("s_" + k)) for k in ("pe", "dve", "act", "pool")}
        self.cnt = {k: 0 for k in ("pe", "dve", "act", "pool")}
        self.qsem = [self.es.enter_context(nc.semaphore("q%d" % i)) for i in range(self.NQ)]
        self.ndma = 0
        self.waited = {k: {} for k in self.ops}
        self.lastw = {}
        self.readers = {}
        self.stack = [self.es]
        self.pend = {}
        self.nuniq = 0

    def push(self):
        self.stack.append(ExitStack())

    def pop(self):
        self.barrier()
        self.stack.pop().close()

    def barrier(self):
        toks = [(self.sem[k], self.cnt[k]) for k in self.cnt if self.cnt[k] > 0]
        for j in range(min(self.NQ, self.ndma)):
            n_on = (self.ndma - j + self.NQ - 1) // self.NQ
            toks.append((self.qsem[j], 16 * n_on))
        for e in self.ops:
            self.pend[e] = list(toks)

    def sb(self, name, shape, dtype):
        self.nuniq += 1
        return self.stack[-1].enter_context(self.nc.sbuf_tensor("%s_u%d" % (name, self.nuniq), list(shape), dtype))

    def ps(self, name, shape, dtype=F32):
        return self.es.enter_context(self.nc.psum_tensor(name, list(shape), dtype))

    def dram(self, name, shape, dtype, kind="Internal"):
        return self.nc.dram_tensor(name, list(shape), dtype, kind=kind)

    def _deps(self, eng, reads, writes):
        deps = {}

        def add(tok):
            key = id(tok[0])
            if key not in deps or deps[key][1] < tok[1]:
                deps[key] = tok
        for tok in self.pend.pop(eng, []):
            add(tok)
        for b in reads:
            for tok in self.lastw.get(b, {}).values():
                add(tok)
        for b in writes:
            for tok in self.lastw.get(b, {}).values():
                add(tok)
            for tok in self.readers.get(b, {}).values():
                add(tok)
        waits = []
        w = self.waited[eng]
        for key, tok in deps.items():
            if eng == "pe" and tok[0] is self.sem["pe"]:
                continue
            if w.get(key, 0) >= tok[1]:
                continue
            w[key] = tok[1]
            waits.append(tok)
        return waits

    def _commit(self, tok, reads, writes):
        key = id(tok[0])
        for b in writes:
            self.lastw[b] = {key: tok}
            self.readers[b] = {}
        for b in reads:
            self.readers.setdefault(b, {})[key] = tok

    def op(self, eng, fn, reads, writes):
        reads = [_u(x)[1] for x in reads]
        writes = [_u(x)[1] for x in writes]
        waits = self._deps(eng, reads, writes)
        self.cnt[eng] += 1
        tok = (self.sem[eng], self.cnt[eng])
        self.ops[eng].append((fn, waits, tok, 1))
        self._commit(tok, reads, writes)

    def dma(self, out, in_, **kw):
        oap, ob = _u(out)
        iap, ib = _u(in_)
        kw.setdefault("allow_slow_non_contiguous", True)
        waits = self._deps("sp", [ib], [ob])
        i = self.ndma
        self.ndma += 1
        sem = self.qsem[i % self.NQ]
        if i >= self.NQ:
            prev = (sem, 16 * (i // self.NQ))
            key = id(sem)
            if self.waited["sp"].get(key, 0) < prev[1]:
                self.waited["sp"][key] = prev[1]
                waits.append(prev)
        tok = (sem, 16 * (i // self.NQ + 1))
        self.ops["sp"].append((lambda e: e.dma_start(out=oap, in_=iap, **kw), waits, tok, 16))
        self._commit(tok, [ib], [ob])

    def mm(self, out, lhsT, rhs, start=True, stop=True):
        o, l, r = _u(out)[0], _u(lhsT)[0], _u(rhs)[0]
        self.op("pe", lambda e: e.matmul(o, l, r, start=start, stop=stop), [lhsT, rhs], [out])

    def tr(self, out, in_, ident):
        o, i, d = _u(out)[0], _u(in_)[0], _u(ident)[0]
        self.op("pe", lambda e: e.transpose(o, i, d), [in_, ident], [out])

    def act(self, out, in_, func, bias=None, scale=None, accum_out=None, eng="act"):
        o, i = _u(out)[0], _u(in_)[0]
        kw = {}
        rd = [in_]
        if bias is not None:
            kw["bias"] = bias if isinstance(bias, (int, float)) else _u(bias)[0]
            if not isinstance(bias, (int, float)):
                rd.append(bias)
        if scale is not None:
            kw["scale"] = scale if isinstance(scale, (int, float)) else _u(scale)[0]
            if not isinstance(scale, (int, float)):
                rd.append(scale)
        wr = [out]
        if accum_out is not None:
            kw["accum_out"] = _u(accum_out)[0]
            wr.append(accum_out)
        self.op("act", lambda e: e.activation(o, i, func, **kw), rd, wr)

    def tt(self, out, in0, in1, op, eng="dve"):
        o, a, b = _u(out)[0], _u(in0)[0], _u(in1)[0]
        self.op(eng, lambda e: e.tensor_tensor(o, a, b, op), [in0, in1], [out])

    def ts(self, out, in0, s1, s2, op0, op1=None, eng="dve", accum_out=None):
        o, a = _u(out)[0], _u(in0)[0]
        rd = [in0]
        v1 = s1
        if not isinstance(s1, (int, float)):
            v1 = _u(s1)[0]
            rd.append(s1)
        v2 = s2
        if s2 is not None and not isinstance(s2, (int, float)):
            v2 = _u(s2)[0]
            rd.append(s2)
        wr = [out]
        kw = {}
        if accum_out is not None:
            kw["accum_out"] = _u(accum_out)[0]
            wr.append(accum_out)
        if op1 is None:
            self.op(eng, lambda e: e.tensor_scalar(o, a, v1, None, op0, **kw), rd, wr)
        else:
            self.op(eng, lambda e: e.tensor_scalar(o, a, v1, v2, op0, op1, **kw), rd, wr)

    def stt(self, out, in0, scalar, in1, op0, op1, eng="dve"):
        o, a, b = _u(out)[0], _u(in0)[0], _u(in1)[0]
        rd = [in0, in1]
        sv = scalar
        if not isinstance(scalar, (int, float)):
            sv = _u(scalar)[0]
            rd.append(scalar)
        self.op("dve", lambda e: e.scalar_tensor_tensor(o, a, sv, b, op0, op1), rd, [out])

    def cp(self, out, in_, eng="dve"):
        o, i = _u(out)[0], _u(in_)[0]
        if eng == "act":
            self.op("act", lambda e: e.copy(o, i), [in_], [out])
        else:
            self.op(eng, lambda e: e.tensor_copy(o, i), [in_], [out])

    def memset(self, out, val, eng="pool"):
        o = _u(out)[0]
        self.op(eng, lambda e: e.memset(o, val), [], [out])

    def scan(self, out, d0, d1, init, op0, op1):
        o, a, b = _u(out)[0], _u(d0)[0], _u(d1)[0]
        rd = [d0, d1]
        iv = init
        if not isinstance(init, (int, float)):
            iv = _u(init)[0]
            rd.append(init)
        self.op("dve", lambda e: e.tensor_tensor_scan(o, a, b, iv, op0, op1), rd, [out])

    def recip(self, out, in_):
        o, i = _u(out)[0], _u(in_)[0]
        self.op("dve", lambda e: e.reciprocal(o, i), [in_], [out])

    def asel(self, out, in_, pattern, cmp, fill, base, cm):
        o, i = _u(out)[0], _u(in_)[0]
        self.op("pool", lambda e: e.affine_select(o, i, pattern, cmp, fill, base=base, channel_multiplier=cm),
                [in_], [out])

    def emit(self):
        nc = self.nc
        fin = []
        for j in range(min(self.NQ, self.ndma)):
            n_on = (self.ndma - j + self.NQ - 1) // self.NQ
            fin.append((self.qsem[j], 16 * n_on))
        ops = self.ops
        sems = self.sem

        def run(e, lst):
            for fn, waits, tok, inc in lst:
                for (s, v) in waits:
                    e.wait_ge(s, v)
                fn(e).then_inc(tok[0], inc)

        with nc.Block() as block:
            @block.sync
            def _(e):
                run(e, ops["sp"])
                for (s, v) in fin:
                    e.wait_ge(s, v)

            @block.tensor
            def _(e):
                run(e, ops["pe"])

            @block.vector
            def _(e):
                run(e, ops["dve"])

            @block.scalar
            def _(e):
                run(e, ops["act"])

            @block.gpsimd
            def _(e):
                run(e, ops["pool"])
        self.es.close()


PI = math.pi
PARAMS = [
    ("ln_in_g", [D]), ("ln_in_b", [D]), ("w_in", [DEPTH, D, NIN]),
    ("s5_lambda_re", [DEPTH, 16, 64]), ("s5_lambda_im", [DEPTH, 16, 64]), ("s5_log_dt", [DEPTH, 16]),
    ("s5_b_re", [DEPTH, 16, 64, 16]), ("s5_b_im", [DEPTH, 16, 64, 16]),
    ("s5_c_re", [DEPTH, 16, 16, 64]), ("s5_c_im", [DEPTH, 16, 16, 64]), ("s5_d", [DEPTH, 256]),
    ("s5_w_glu", [DEPTH, 256, 256]), ("s5_b_glu", [DEPTH, 256]),
    ("rw_mu", [DEPTH, 1024]), ("rw_w0", [DEPTH, 256]), ("rw_w2", [DEPTH, 64, 256]), ("rw_a0", [DEPTH, 256]),
    ("rw_a2", [DEPTH, 64, 256]), ("rw_g2", [DEPTH, 128, 256]), ("rw_k_k", [DEPTH, 256]), ("rw_k_a", [DEPTH, 256]),
    ("rw_r_k", [DEPTH, 4, 64]), ("rw_ln_g", [DEPTH, 256]), ("rw_ln_b", [DEPTH, 256]),
    ("ml_conv_w", [DEPTH, 4, 512]), ("ml_conv_b", [DEPTH, 512]), ("ml_b_i", [DEPTH, 4]), ("ml_b_f", [DEPTH, 4]),
    ("ml_ln_g", [DEPTH, 256]), ("w_out", [DEPTH, D, D]), ("ln1_g", [DEPTH, D]), ("ln1_b", [DEPTH, D]),
    ("ffn_w_up", [DEPTH, D, 2 * DFF]), ("ffn_conv_w", [DEPTH, 3, DFF]), ("ffn_conv_b", [DEPTH, DFF]),
    ("ffn_w_down", [DEPTH, DFF, D]), ("ln2_g", [DEPTH, D]), ("ln2_b", [DEPTH, D]),
]


nc_out_handle = [None]


def build_program(dbg=None, nlayers=DEPTH, mixers=("s5", "rw", "sb", "ml")):
    nc = bass.Bass("TRN2", target_bir_lowering=False)
    P = Prog(nc)
    x_in = nc.dram_tensor("x", [S, D], F32, kind="ExternalInput")
    prm = {n: nc.dram_tensor(n, shp, F32, kind="ExternalInput") for n, shp in PARAMS}
    out = nc.dram_tensor("out", [S, D], F32, kind="ExternalOutput")
    nc_out_handle[0] = out

    hT = P.dram("hT", [D, S], F32)
    h1T = P.dram("h1T", [D, S], F32)
    zT = P.dram("zT", [25 * 128, S], F32)
    yT = P.dram("yT", [D, S], BF16)
    vTM = P.dram("vTM", [S, 512], BF16)
    vrw = P.dram("vrw", [S, 256], F32)
    grw = P.dram("grw", [S, 256], F32)
    yrw = P.dram("yrw", [S, 256], F32)

    ident = P.sb("ident", [128, 128], F32)
    P.memset(ident[:], 1.0)
    P.asel(ident[:], ident[:], [[-1, 128]], ALU.is_equal, 0.0, 0, 1)
    identb = P.sb("identb", [128, 128], BF16)
    P.cp(identb[:], ident[:])
    onesm = P.sb("onesm", [128, 128], F32)
    P.memset(onesm[:], 1.0 / D)
    ones1 = P.sb("ones1", [128, 128], F32)
    P.memset(ones1[:], 1.0)
    psb = [P.ps("psb%d" % i, [128, 512], F32) for i in range(8)]
    rr = [0]

    def ev_eng():
        rr[0] += 1
        return "act" if rr[0] % 2 else "dve"

    def ln_block(L, src, dst_dram, tb, g, b):
        pm = psb[6]
        for k in range(8):
            P.mm(pm[:], onesm[:], src[:, k, :], start=(k == 0), stop=(k == 7))
        P.cp(L["mean"][:], pm[:], eng="act")
        for k in range(8):
            P.tt(src[:, k, :], src[:, k, :], L["mean"][:], ALU.subtract, eng=("dve" if k % 2 else "pool"))
        pv = psb[7]
        for k in range(8):
            sq = L["sq"][k % 2]
            P.act(sq[:], src[:, k, :], AF.Square)
            P.mm(pv[:], onesm[:], sq[:], start=(k == 0), stop=(k == 7))
        P.ts(L["rstd"][:], pv[:], LN_EPS, None, ALU.add)
        P.act(L["rstd"][:], L["rstd"][:], AF.Sqrt)
        P.recip(L["rstd"][:], L["rstd"][:])
        for k in range(8):
            P.tt(src[:, k, :], src[:, k, :], L["rstd"][:], ALU.mult, eng=("dve" if k % 2 else "pool"))
            P.ts(L["ho"][:, k, :], src[:, k, :], g[:, k:k + 1], b[:, k:k + 1], ALU.mult, ALU.add)
        P.dma(R(dst_dram[:, tb * 512:(tb + 1) * 512].rearrange("(k p) t -> p k t", p=128), tb), L["ho"][:])

    def ln_alloc():
        return {"mean": P.sb("ln_mean", [128, 512], F32), "rstd": P.sb("ln_rstd", [128, 512], F32),
                "sq": [P.sb("ln_sq%d" % i, [128, 512], F32) for i in range(2)],
                "ho": P.sb("ln_out", [128, 8, 512], F32)}

    def load_cols(dst, vec_ap, n):
        P.dma(dst, vec_ap.rearrange("(k p) -> p k", p=128), allow_slow_non_contiguous=True)

    P.push()
    xt = [P.sb("xt%d" % i, [128, D], F32) for i in range(8)]
    xT2 = [P.sb("xTblk%d" % i, [128, 8, 512], F32) for i in range(2)]
    gcol = P.sb("gcol", [128, 8], F32)
    bcol = P.sb("bcol", [128, 8], F32)
    load_cols(gcol[:], prm["ln_in_g"][:], 8)
    load_cols(bcol[:], prm["ln_in_b"][:], 8)
    L = ln_alloc()
    def ld_xt(tb):
        for j in range(4):
            tt_ = tb * 4 + j
            P.dma(xt[tt_ % 8][:], x_in[tt_ * 128:(tt_ + 1) * 128, :])
    ld_xt(0)
    for tb in range(9):
        if tb + 1 < 8:
            ld_xt(tb + 1)
        if tb < 8:
            xT = xT2[tb % 2]
            for j in range(4):
                tt_ = tb * 4 + j
                xb = xt[tt_ % 8]
                for k in range(8):
                    pt = psb[k % 4]
                    P.tr(pt[:, 0:128], xb[:, k * 128:(k + 1) * 128], ident[:])
                    P.cp(xT[:, k, j * 128:(j + 1) * 128], pt[:, 0:128], eng=ev_eng())
        if tb >= 1:
            ln_block(L, xT2[(tb - 1) % 2], hT, tb - 1, gcol, bcol)
    P.pop()

    for l in range(nlayers):
        phase_A(P, nc, prm, l, hT, zT, vTM, vrw, psb, ev_eng)
        if "s5" in mixers:
            phase_s5(P, nc, prm, l, zT, yT, psb, ident, ev_eng)
        if "sb" in mixers:
            phase_sb(P, nc, prm, l, zT, yT, vTM, psb, ones1, ev_eng)
        if "ml" in mixers:
            phase_ml(P, nc, prm, l, zT, yT, vTM, psb, ident, ev_eng)
        if "rw" in mixers:
            phase_rw(P, nc, prm, l, zT, yT, vrw, grw, yrw, psb, ident, identb, ev_eng)
        if dbg == "mix":
            break
        phase_proj_ln(P, nc, l, prm["w_out"][l], 8, yT, True, hT, h1T, prm["ln1_g"][l], prm["ln1_b"][l],
                      psb, ln_alloc, ln_block, load_cols, ev_eng)
        phase_ffn(P, nc, prm, l, h1T, hT, psb, ln_alloc, ln_block, load_cols, ev_eng)

    P.push()
    import os
    if os.environ.get("DUMPT"):
        pass
    elif dbg == "mix" and os.environ.get("DUMP"):
        src = {"yrw": yrw, "vrw": vrw, "grw": grw}[os.environ["DUMP"]]
        stf = P.sb("stgf2", [128, 32, 256], F32)
        P.dma(stf[:], src[:, :].rearrange("(n p) c -> p n c", p=128))
        P.dma(out[:, 0:256].rearrange("(n p) c -> p n c", p=128), stf[:])
    elif dbg == "mix":
        stg = P.sb("stgb", [128, 8, 512], BF16)
        stf = P.sb("stgf", [128, 8, 512], F32)
        ov = out[:, :].rearrange("(a b) d -> a (b d)", a=D)
        for tb in range(8):
            P.dma(stg[:], yT[:, tb * 512:(tb + 1) * 512].rearrange("(k p) t -> p k t", p=128))
            P.cp(stf[:], stg[:])
            P.dma(R(ov[:, tb * 512:(tb + 1) * 512].rearrange("(k p) t -> p k t", p=128), tb), stf[:])
    else:
        hb = [P.sb("fin_h%d" % i, [128, 8, 512], F32) for i in range(2)]
        ob = [P.sb("fin_o%d" % i, [128, D], F32) for i in range(2)]
        n = 0
        for tb in range(8):
            hbb = hb[tb % 2]
            P.dma(hbb[:], hT[:, tb * 512:(tb + 1) * 512].rearrange("(k p) t -> p k t", p=128))
            for j in range(4):
                o = ob[n % 2]
                n += 1
                for k in range(8):
                    pt = psb[k % 4]
                    P.tr(pt[:, 0:128], hbb[:, k, j * 128:(j + 1) * 128], ident[:])
                    P.cp(o[:, k * 128:(k + 1) * 128], pt[:, 0:128], eng=ev_eng())
                tt_ = tb * 4 + j
                P.dma(R(out[tt_ * 128:(tt_ + 1) * 128, :], tt_), o[:])
    P.pop()
    P.emit()
    return nc


def load_cast(P, dst_bf, src_dram_ap, stage, eng):
    P.dma(stage, src_dram_ap)
    P.cp(dst_bf, stage, eng=eng)


def phase_A(P, nc, prm, l, hT, zT, vTM, vrw, psb, ev_eng):
    P.push()
    wA = P.sb("wA", [128, 8, NIN], BF16)
    hb = P.sb("hTb", [128, 8, S + 1], BF16)
    wst = [P.sb("wAst%d" % i, [128, NIN], F32) for i in range(2)]
    hst = [P.sb("hst%d" % i, [128, 2048], F32) for i in range(2)]
    zo = [P.sb("zo%d" % i, [128, 512], F32) for i in range(3)]
    vo = [P.sb("vo%d" % i, [128, 512], BF16) for i in range(2)]
    vo2 = [P.sb("vo2%d" % i, [128, 256], F32) for i in range(2)]
    wv1 = P.sb("wv1", [128, 8, 256], BF16)
    wv2 = P.sb("wv2", [128, 8, 256], BF16)
    mur = P.sb("mur", [128, 256], F32)
    tmpf = P.sb("wvtmp", [128, 256], F32)
    tmpg = P.sb("wvtmp2", [128, 256], F32)
    P.dma(mur[:], prm["rw_mu"][l, 512:768].partition_broadcast(128))
    P.memset(hb[:, :, 0:1], 0.0)
    n = 0
    for k in range(8):
        st = wst[k % 2]
        P.dma(st[:], prm["w_in"][l, k * 128:(k + 1) * 128, :])
        P.cp(wA[:, k, :], st[:], eng=("act" if k % 2 else "dve"))
        P.tt(tmpf[:], st[:, 768:1024], mur[:], ALU.mult)
        P.cp(wv2[:, k, :], tmpf[:], eng="pool")
        P.tt(tmpg[:], st[:, 768:1024], tmpf[:], ALU.subtract)
        P.cp(wv1[:, k, :], tmpg[:], eng="pool")
        for hh in range(2):
            s2 = hst[n % 2]
            n += 1
            P.dma(s2[:], hT[k * 128:(k + 1) * 128, hh * 2048:(hh + 1) * 2048])
            P.cp(hb[:, k, 1 + hh * 2048:1 + (hh + 1) * 2048], s2[:], eng=("act" if n % 2 else "dve"))
    n = 0
    for m in range(25):
        msz = min(128, NIN - m * 128)
        for tb in range(8):
            ps = psb[n % 4]
            for k in range(8):
                P.mm(ps[0:msz, :], wA[:, k, m * 128:m * 128 + msz], hb[:, k, 1 + tb * 512:1 + (tb + 1) * 512],
                     start=(k == 0), stop=(k == 7))
            o = zo[n % 3]
            P.cp(o[0:msz, :], ps[0:msz, :], eng=ev_eng())
            P.dma(R(zT[m * 128:m * 128 + msz, tb * 512:(tb + 1) * 512], "%d_%d" % (m, tb)), o[0:msz, :])
            n += 1
    for tt_ in range(32):
        ps = psb[4 + tt_ % 2]
        lo = 1 + tt_ * 128
        for (c0, o0) in ((1792, 0), (2560, 256)):
            for k in range(8):
                P.mm(ps[:, o0:o0 + 256], hb[:, k, lo:lo + 128], wA[:, k, c0:c0 + 256], start=(k == 0), stop=(k == 7))
        o = vo[tt_ % 2]
        P.cp(o[:], ps[:], eng=ev_eng())
        P.dma(R(vTM[tt_ * 128:(tt_ + 1) * 128, :], tt_), o[:])
        ps2 = psb[6 + tt_ % 2]
        for k in range(8):
            P.mm(ps2[:, 0:256], hb[:, k, lo:lo + 128], wv1[:, k, :], start=(k == 0), stop=False)
            P.mm(ps2[:, 0:256], hb[:, k, lo - 1:lo + 127], wv2[:, k, :], start=False, stop=(k == 7))
        o2 = vo2[tt_ % 2]
        P.cp(o2[:], ps2[:, 0:256], eng=ev_eng())
        P.dma(R(vrw[tt_ * 128:(tt_ + 1) * 128, :], tt_), o2[:])
    P.pop()


def phase_proj_ln(P, nc, l, w_dram, nk, src, src_is_bf, resid, dst, g_ap, b_ap, psb, ln_alloc, ln_block, load_cols, ev_eng):
    P.push()
    w = P.sb("pw", [128, nk, D], BF16)
    st = [P.sb("pwst%d" % i, [128, D], F32) for i in range(4)]
    for k in range(nk):
        P.dma(st[k % 4][:], w_dram[k * 128:(k + 1) * 128, :])
        P.cp(w[:, k, :], st[k % 4][:], eng=("act" if k % 2 else "dve"))
    gcol = P.sb("pg", [128, 8], F32)
    bcol = P.sb("pb", [128, 8], F32)
    load_cols(gcol[:], g_ap, 8)
    load_cols(bcol[:], b_ap, 8)
    sb_ = [P.sb("psrc%d" % i, [128, nk, 512], BF16) for i in range(2)]
    res2 = [P.sb("pres%d" % i, [128, 8, 512], F32) for i in range(3)]
    L = ln_alloc()

    def loads(tb):
        P.dma(sb_[tb % 2][:], src[:, tb * 512:(tb + 1) * 512].rearrange("(k p) t -> p k t", p=128))
        P.dma(res2[tb % 3][:], resid[:, tb * 512:(tb + 1) * 512].rearrange("(k p) t -> p k t", p=128))
    loads(0)
    for tb in range(9):
        if tb < 8:
            res = res2[tb % 3]
            s_ = sb_[tb % 2]
            for m in range(8):
                ps = psb[m % 4]
                for k in range(nk):
                    P.mm(ps[:], w[:, k, m * 128:(m + 1) * 128], s_[:, k, :], start=(k == 0), stop=(k == nk - 1))
                P.stt(res[:, m, :], res[:, m, :], DN_ALPHA, ps[:], ALU.mult, ALU.add)
                if m == 3 and tb + 1 < 8:
                    loads(tb + 1)
        if tb >= 1:
            ln_block(L, res2[(tb - 1) % 3], dst, tb - 1, gcol, bcol)
    P.pop()


def phase_ffn(P, nc, prm, l, h1T, hT, psb, ln_alloc, ln_block, load_cols, ev_eng):
    aT = P.dram("aT%d" % l, [DFF, S], BF16)
    NF = DFF // 128
    P.push()
    wup = P.sb("wup", [128, 8, 2 * DFF], BF16)
    st = [P.sb("wupst%d" % i, [128, 1408], F32) for i in range(4)]
    n = 0
    for k in range(8):
        for q in range(4):
            s_ = st[n % 4]
            P.dma(s_[:], prm["ffn_w_up"][l, k * 128:(k + 1) * 128, q * 1408:(q + 1) * 1408])
            P.cp(wup[:, k, q * 1408:(q + 1) * 1408], s_[:], eng=("act" if n % 2 else "dve"))
            n += 1
    cw = P.sb("fcw", [128, NF, 3], F32)
    cb = P.sb("fcb", [128, NF], F32)
    for i in range(3):
        P.dma(cw[:, :, i], prm["ffn_conv_w"][l, i].rearrange("(f p) -> p f", p=128))
    P.dma(cb[:], prm["ffn_conv_b"][l].rearrange("(f p) -> p f", p=128), allow_slow_non_contiguous=True)
    uprev = P.sb("uprev", [128, NF, 2], F32)
    P.memset(uprev[:], 0.0)
    hst = [P.sb("fhst%d" % i, [128, 8, 512], F32) for i in range(2)]
    hb = [P.sb("fhb%d" % i, [128, 8, 512], BF16) for i in range(2)]
    ub = [P.sb("fub%d" % i, [128, 514], F32) for i in range(4)]
    acc = [P.sb("facc%d" % i, [128, 512], F32) for i in range(4)]
    t1 = [P.sb("ft1%d" % i, [128, 512], F32) for i in range(4)]
    ao = [P.sb("fao%d" % i, [128, 512], BF16) for i in range(4)]
    n = 0
    def load_h(tb):
        P.dma(hst[tb % 2][:], h1T[:, tb * 512:(tb + 1) * 512].rearrange("(k p) t -> p k t", p=128))
        P.cp(hb[tb % 2][:], hst[tb % 2][:], eng="dve")
    load_h(0)
    for tb in range(8):
        hs, hbb = hst[tb % 2], hb[tb % 2]
        for f in range(NF):
            if f == 2 and tb + 1 < 8:
                load_h(tb + 1)
            pu, pg = psb[(2 * n) % 8], psb[(2 * n + 1) % 8]
            for k in range(8):
                P.mm(pu[:], wup[:, k, f * 128:(f + 1) * 128], hbb[:, k, :], start=(k == 0), stop=(k == 7))
            for k in range(8):
                P.mm(pg[:], wup[:, k, DFF + f * 128:DFF + (f + 1) * 128], hbb[:, k, :], start=(k == 0), stop=(k == 7))
            u, a, t = ub[n % 4], acc[n % 4], t1[n % 4]
            P.cp(u[:, 0:2], uprev[:, f, :], eng="pool")
            P.cp(u[:, 2:514], pu[:], eng="act")
            P.cp(uprev[:, f, :], u[:, 512:514], eng="pool")
            P.ts(a[:], u[:, 2:514], cw[:, f, 2:3], cb[:, f:f + 1], ALU.mult, ALU.add)
            P.stt(a[:], u[:, 1:513], cw[:, f, 1:2], a[:], ALU.mult, ALU.add)
            P.stt(a[:], u[:, 0:512], cw[:, f, 0:1], a[:], ALU.mult, ALU.add)
            gelu(P, t[:], a[:], u[:, 0:512])
            o = ao[n % 4]
            P.tt(o[:], t[:], pg[:], ALU.mult)
            P.dma(R(aT[f * 128:(f + 1) * 128, tb * 512:(tb + 1) * 512], "%d_%d" % (f, tb)), o[:])
            n += 1
    P.pop()
    P.push()
    wdn = P.sb("wdn", [128, NF, D], BF16)
    st = [P.sb("wdnst%d" % i, [128, D], F32) for i in range(4)]
    for k in range(NF):
        P.dma(st[k % 4][:], prm["ffn_w_down"][l, k * 128:(k + 1) * 128, :])
        P.cp(wdn[:, k, :], st[k % 4][:], eng=("act" if k % 2 else "dve"))
    gcol = P.sb("fg", [128, 8], F32)
    bcol = P.sb("fb", [128, 8], F32)
    load_cols(gcol[:], prm["ln2_g"][l], 8)
    load_cols(bcol[:], prm["ln2_b"][l], 8)
    ab = [P.sb("fab%d" % i, [128, NF, 512], BF16) for i in range(2)]
    res2 = [P.sb("fres%d" % i, [128, 8, 512], F32) for i in range(3)]
    L = ln_alloc()

    def loads(tb):
        P.dma(ab[tb % 2][:], aT[:, tb * 512:(tb + 1) * 512].rearrange("(k p) t -> p k t", p=128))
        P.dma(res2[tb % 3][:], h1T[:, tb * 512:(tb + 1) * 512].rearrange("(k p) t -> p k t", p=128))
    loads(0)
    for tb in range(9):
        if tb < 8:
            res = res2[tb % 3]
            a_ = ab[tb % 2]
            for m in range(8):
                ps = psb[m % 4]
                for k in range(NF):
                    P.mm(ps[:], wdn[:, k, m * 128:(m + 1) * 128], a_[:, k, :], start=(k == 0), stop=(k == NF - 1))
                P.stt(res[:, m, :], res[:, m, :], DN_ALPHA, ps[:], ALU.mult, ALU.add)
                if m == 3 and tb + 1 < 8:
                    loads(tb + 1)
        if tb >= 1:
            ln_block(L, res2[(tb - 1) % 3], hT, tb - 1, gcol, bcol)
    P.pop()


def gelu(P, out, x, tmp, eng="dve"):
    P.tt(tmp, x, x, ALU.mult, eng="pool")
    P.ts(tmp, tmp, 0.044715, 1.0, ALU.mult, ALU.add, eng="pool")
    P.tt(tmp, tmp, x, ALU.mult, eng="pool")
    P.act(tmp, tmp, AF.Sigmoid, scale=1.5957691216057308)
    P.tt(out, tmp, x, ALU.mult, eng=eng)


def sincos(P, pre, ang, shape):
    I32 = mybir.dt.int32
    outs = []
    npi = P.sb(pre + "_npi", [shape[0], 1], F32)
    P.memset(npi[:], -PI)
    for nm, off in (("c", PI / 2), ("s", 0.0)):
        t = P.sb(pre + "_t" + nm, shape, F32)
        tf = P.sb(pre + "_f" + nm, shape, F32)
        ti = P.sb(pre + "_i" + nm, shape, I32)
        o = P.sb(pre + "_o" + nm, shape, F32)
        P.ts(t[:], ang, 64 * PI + PI + off, None, ALU.add)
        P.ts(tf[:], t[:], 1.0 / (2 * PI), None, ALU.mult)
        P.cp(ti[:], tf[:])
        P.cp(tf[:], ti[:])
        P.stt(t[:], tf[:], -2 * PI, t[:], ALU.mult, ALU.add)
        P.ts(tf[:], t[:], 0.0, 2 * PI, ALU.is_lt, ALU.mult)
        P.tt(t[:], t[:], tf[:], ALU.add)
        P.ts(tf[:], t[:], 2 * PI, -2 * PI, ALU.is_ge, ALU.mult)
        P.tt(t[:], t[:], tf[:], ALU.add)
        P.act(o[:], t[:], AF.Sin, bias=npi[:])
        outs.append(o)
    return outs[0], outs[1]


def phase_s5(P, nc, prm, l, zT, yT, psb, ident, ev_eng):
    P.push()
    lr = P.sb("s5lr", [128, 8], F32)
    li = P.sb("s5li", [128, 8], F32)
    dt = P.sb("s5dt", [128, 8], F32)
    br = P.sb("s5br", [128, 8, 16], F32)
    bi = P.sb("s5bi", [128, 8, 16], F32)
    for gs in range(2):
        sl = slice(gs * 64, (gs + 1) * 64)
        P.dma(lr[sl, :], prm["s5_lambda_re"][l, gs::2, :].rearrange("j p -> p j"), allow_slow_non_contiguous=True)
        P.dma(li[sl, :], prm["s5_lambda_im"][l, gs::2, :].rearrange("j p -> p j"), allow_slow_non_contiguous=True)
        P.dma(dt[sl, :], prm["s5_log_dt"][l, gs::2].partition_broadcast(64))
        P.dma(br[sl, :, :], prm["s5_b_re"][l, gs::2].rearrange("j p c -> p j c"))
        P.dma(bi[sl, :, :], prm["s5_b_im"][l, gs::2].rearrange("j p c -> p j c"))
    P.act(dt[:], dt[:], AF.Exp)
    mag = P.sb("s5mag", [128, 8], F32)
    ang = P.sb("s5ang", [128, 8], F32)
    P.tt(mag[:], lr[:], dt[:], ALU.mult)
    P.act(mag[:], mag[:], AF.Exp)
    P.tt(ang[:], li[:], dt[:], ALU.mult)
    cs, sn = sincos(P, "s5sc", ang[:], [128, 8])
    abr = P.sb("s5abr", [128, 8], F32)
    abi = P.sb("s5abi", [128, 8], F32)
    P.tt(abr[:], mag[:], cs[:], ALU.mult)
    P.tt(abi[:], mag[:], sn[:], ALU.mult)
    den = P.sb("s5den", [128, 8], F32)
    t0 = P.sb("s5t0", [128, 8], F32)
    t1 = P.sb("s5t1", [128, 8], F32)
    zr = P.sb("s5zr", [128, 8], F32)
    zi = P.sb("s5zi", [128, 8], F32)
    nzi = P.sb("s5nzi", [128, 8], F32)
    P.tt(den[:], lr[:], lr[:], ALU.mult)
    P.tt(t0[:], li[:], li[:], ALU.mult)
    P.tt(den[:], den[:], t0[:], ALU.add)
    P.recip(den[:], den[:])
    P.ts(t0[:], abr[:], -1.0, None, ALU.add)
    P.tt(zr[:], t0[:], lr[:], ALU.mult)
    P.tt(t1[:], abi[:], li[:], ALU.mult)
    P.tt(zr[:], zr[:], t1[:], ALU.add)
    P.tt(zr[:], zr[:], den[:], ALU.mult)
    P.tt(zi[:], abi[:], lr[:], ALU.mult)
    P.tt(t1[:], t0[:], li[:], ALU.mult)
    P.tt(zi[:], zi[:], t1[:], ALU.subtract)
    P.tt(zi[:], zi[:], den[:], ALU.mult)
    P.ts(nzi[:], zi[:], -1.0, None, ALU.mult)
    bbr = P.sb("s5bbr", [128, 8, 16], F32)
    bbi = P.sb("s5bbi", [128, 8, 16], F32)
    tb_ = P.sb("s5tb", [128, 8, 16], F32)
    for j in range(8):
        P.ts(tb_[:, j, :], bi[:, j, :], nzi[:, j:j + 1], None, ALU.mult)
        P.stt(bbr[:, j, :], br[:, j, :], zr[:, j:j + 1], tb_[:, j, :], ALU.mult, ALU.add)
        P.ts(tb_[:, j, :], br[:, j, :], zi[:, j:j + 1], None, ALU.mult)
        P.stt(bbi[:, j, :], bi[:, j, :], zr[:, j:j + 1], tb_[:, j, :], ALU.mult, ALU.add)
    LB = P.sb("s5LB", [128, 2, 8, 128], BF16)
    zz = [P.sb("s5zz%d" % i, [128, 128], F32) for i in range(2)]
    n = 0
    for comp, bb in enumerate((bbr, bbi)):
        for j in range(8):
            z_ = zz[n % 2]
            P.memset(z_[:], 0.0)
            for gs in range(2):
                g = 2 * j + gs
                c0 = 16 * (g % 8)
                P.cp(z_[gs * 64:(gs + 1) * 64, c0:c0 + 16], bb[gs * 64:(gs + 1) * 64, j, :], eng="pool")
            pt = psb[n % 2]
            P.tr(pt[:, 0:128], z_[:], ident[:])
            P.cp(LB[:, comp, j, :], pt[:, 0:128], eng=ev_eng())
            n += 1
    LC = P.sb("s5LC", [128, 2, 8, 128], BF16)
    P.memset(LC[:], 0.0)
    cc = P.sb("s5cc", [128, 128], F32)
    tt_ = P.sb("s5ctt", [128, 128], F32)
    for comp, nm in enumerate(("s5_c_re", "s5_c_im")):
        for hh in range(2):
            src = prm[nm][l, hh * 8:(hh + 1) * 8].rearrange("g c p -> (g c) p")
            P.dma(cc[:, 0:64], src)
            P.dma(cc[:, 64:128], src)
            pt = psb[2 + (comp * 2 + hh) % 2]
            P.tr(pt[:, 0:128], cc[:], ident[:])
            P.ts(tt_[:], pt[:, 0:128], (1.0 if comp == 0 else -1.0), None, ALU.mult)
            for jj in range(4):
                j = hh * 4 + jj
                for gs in range(2):
                    g = 2 * j + gs
                    c0 = 16 * (g % 8)
                    P.cp(LC[gs * 64:(gs + 1) * 64, comp, j, c0:c0 + 16], tt_[gs * 64:(gs + 1) * 64, c0:c0 + 16], eng="pool")
    wc = P.sb("s5wc", [128, 12, 8], F32)
    ws = P.sb("s5ws", [128, 12, 8], F32)
    P.cp(wc[:, 0, :], cs[:])
    P.cp(ws[:, 0, :], sn[:])
    for k in range(1, 12):
        P.tt(t0[:], wc[:, k - 1, :], wc[:, k - 1, :], ALU.mult)
        P.tt(t1[:], ws[:, k - 1, :], ws[:, k - 1, :], ALU.mult)
        P.tt(wc[:, k, :], t0[:], t1[:], ALU.subtract)
        P.tt(t0[:], wc[:, k - 1, :], ws[:, k - 1, :], ALU.mult)
        P.ts(ws[:, k, :], t0[:], 2.0, None, ALU.mult)
    ub = P.sb("s5ub", [128, 2, S], BF16)
    ust = [P.sb("s5ust%d" % i, [128, 2048], F32) for i in range(2)]
    n = 0
    for hh in range(2):
        for q in range(2):
            P.dma(ust[n % 2][:], zT[hh * 128:(hh + 1) * 128, q * 2048:(q + 1) * 2048])
            P.cp(ub[:, hh, q * 2048:(q + 1) * 2048], ust[n % 2][:], eng=("act" if n % 2 else "pool"))
            n += 1
    TC = P.sb("s5TC", [128, S], F32)
    TS = P.sb("s5TS", [128, S], F32)
    RHO = P.sb("s5RHO", [128, S], F32)
    Xr = P.sb("s5Xr", [128, S], F32)
    Xi = P.sb("s5Xi", [128, S], F32)
    yacc = P.sb("s5yacc", [128, 2, S], F32)
    tmp = [P.sb("s5tmp%d" % i, [128, 2048], F32) for i in range(2)]
    sbr = [P.sb("s5sbr%d" % i, [128, 512], BF16) for i in range(2)]
    sbi = [P.sb("s5sbi%d" % i, [128, 512], BF16) for i in range(2)]
    for j in range(8):
        hh = j // 4
        P.memset(TC[:, 0:1], 1.0, eng="dve")
        P.memset(TS[:, 0:1], 0.0, eng="dve")
        for k in range(12):
            n_ = 1 << k
            c_, s_ = wc[:, k, j:j + 1], ws[:, k, j:j + 1]
            tA, tB = tmp[0][:, 0:n_], tmp[1][:, 0:n_]
            P.act(tA, TS[:, 0:n_], AF.Copy, scale=s_)
            P.act(tB, TC[:, 0:n_], AF.Copy, scale=s_)
            P.stt(TC[:, n_:2 * n_], TC[:, 0:n_], c_, tA, ALU.mult, ALU.subtract)
            P.stt(TS[:, n_:2 * n_], TS[:, 0:n_], c_, tB, ALU.mult, ALU.add)
        P.ts(RHO[:], TC[:], 0.0, mag[:, j:j + 1], ALU.mult, ALU.add, eng="pool")
        for tb in range(8):
            sl = slice(tb * 512, (tb + 1) * 512)
            pr, pi_ = psb[(2 * tb) % 4], psb[(2 * tb + 1) % 4]
            P.mm(pr[:], LB[:, 0, j, :], ub[:, hh, sl])
            P.mm(pi_[:], LB[:, 1, j, :], ub[:, hh, sl])
            a_, b_ = tmp[0][:, (tb % 2) * 512:(tb % 2) * 512 + 512], tmp[1][:, (tb % 2) * 512:(tb % 2) * 512 + 512]
            P.tt(a_, pi_[:], TS[:, sl], ALU.mult)
            P.tt(Xr[:, sl], pr[:], TC[:, sl], ALU.mult)
            P.tt(Xr[:, sl], Xr[:, sl], a_, ALU.add, eng="pool")
            P.tt(b_, pr[:], TS[:, sl], ALU.mult)
            P.tt(Xi[:, sl], pi_[:], TC[:, sl], ALU.mult)
            P.tt(Xi[:, sl], Xi[:, sl], b_, ALU.subtract, eng="pool")
        P.scan(Xr[:], RHO[:], Xr[:], 0.0, ALU.mult, ALU.add)
        P.scan(Xi[:], RHO[:], Xi[:], 0.0, ALU.mult, ALU.add)
        for tb in range(8):
            sl = slice(tb * 512, (tb + 1) * 512)
            a_, b_ = tmp[0][:, (tb % 2) * 512:(tb % 2) * 512 + 512], tmp[1][:, (tb % 2) * 512:(tb % 2) * 512 + 512]
            sr_, si_ = sbr[tb % 2], sbi[tb % 2]
            P.tt(a_, Xi[:, sl], TS[:, sl], ALU.mult, eng="pool")
            P.tt(b_, Xr[:, sl], TC[:, sl], ALU.mult)
            P.tt(sr_[:], b_, a_, ALU.subtract)
            P.tt(a_, Xr[:, sl], TS[:, sl], ALU.mult, eng="pool")
            P.tt(b_, Xi[:, sl], TC[:, sl], ALU.mult)
            P.tt(si_[:], b_, a_, ALU.add)
            py = psb[4 + tb % 2]
            P.mm(py[:], LC[:, 0, j, :], sr_[:], start=True, stop=False)
            P.mm(py[:], LC[:, 1, j, :], si_[:], start=False, stop=True)
            if j % 4 == 0:
                P.cp(yacc[:, hh, sl], py[:], eng="act")
            else:
                P.tt(yacc[:, hh, sl], yacc[:, hh, sl], py[:], ALU.add)
    dcol = P.sb("s5d", [128, 2], F32)
    bg = P.sb("s5bg", [128, 2], F32)
    P.dma(dcol[:], prm["s5_d"][l].rearrange("(k p) -> p k", p=128), allow_slow_non_contiguous=True)
    P.dma(bg[:], prm["s5_b_glu"][l].rearrange("(k p) -> p k", p=128), allow_slow_non_contiguous=True)
    wg = P.sb("s5wg", [128, 2, 256], BF16)
    for k in range(2):
        P.dma(tmp[k][:, 0:256], prm["s5_w_glu"][l, k * 128:(k + 1) * 128, :])
        P.cp(wg[:, k, :], tmp[k][:, 0:256])
    yg = P.sb("s5yg", [128, 2, 512], F32)
    ygb = P.sb("s5ygb", [128, 2, 512], BF16)
    gt = P.sb("s5gt", [128, 512], F32)
    sg = P.sb("s5sg", [128, 512], F32)
    yo = [P.sb("s5yo%d" % i, [128, 512], BF16) for i in range(2)]
    u32 = [P.sb("s5u32%d" % i, [128, 512], F32) for i in range(2)]
    for tb in range(8):
        sl = slice(tb * 512, (tb + 1) * 512)
        for hh in range(2):
            P.dma(u32[hh][:], zT[hh * 128:(hh + 1) * 128, sl])
            P.stt(yacc[:, hh, sl], u32[hh][:], dcol[:, hh:hh + 1], yacc[:, hh, sl], ALU.mult, ALU.add)
            gelu(P, yg[:, hh, :], yacc[:, hh, sl], gt[:])
            P.cp(ygb[:, hh, :], yg[:, hh, :], eng="act")
        for h2 in range(2):
            ps = psb[6 + h2]
            for k in range(2):
                P.mm(ps[:], wg[:, k, h2 * 128:(h2 + 1) * 128], ygb[:, k, :], start=(k == 0), stop=(k == 1))
            P.act(sg[:], ps[:], AF.Sigmoid, bias=bg[:, h2:h2 + 1])
            o = yo[h2]
            P.tt(o[:], yg[:, h2, :], sg[:], ALU.mult)
            P.dma(R(yT[h2 * 128:(h2 + 1) * 128, sl], "s5_%d_%d" % (h2, tb)), o[:])
    P.pop()


def make_masks(P, pre, strict):
    ms = []
    for o in range(4):
        m = P.sb("%s_m%d" % (pre, o), [128, 512], F32)
        P.memset(m[:], 1.0)
        P.asel(m[:], m[:], [[1, 512]], (ALU.is_gt if strict else ALU.is_ge), 0.0, -128 * o, -1)
        ms.append(m)
    return ms


def load_heads_bf(P, pre, dst, zT, row0, scale, stages):
    n = 0
    for t in range(2):
        for q in range(2):
            st = stages[n % 2]
            P.dma(st[:], zT[row0 + t * 128:row0 + (t + 1) * 128, q * 2048:(q + 1) * 2048])
            if scale == 1.0:
                P.cp(dst[:, t, q * 2048:(q + 1) * 2048], st[:], eng=("act" if n % 2 else "pool"))
            else:
                P.act(dst[:, t, q * 2048:(q + 1) * 2048], st[:], AF.Copy, scale=scale)
            n += 1


def load_heads_zpad(P, dst, zT, row0, scale, stages):
    P.memset(dst[:], 0.0)
    n = 0
    for t in range(2):
        for q in range(2):
            st = stages[n % 2]
            n += 1
            P.dma(st[:], zT[row0 + t * 128:row0 + (t + 1) * 128, q * 2048:(q + 1) * 2048])
            for gs in range(2):
                pb = gs * 64
                P.act(dst[pb:pb + 64, 2 * t + gs, q * 2048:(q + 1) * 2048], st[pb:pb + 64, :], AF.Copy, scale=scale)


def phase_sb(P, nc, prm, l, zT, yT, vTM, psb, ones1, ev_eng):
    P.push()
    qb = P.sb("sbq", [128, 4, S], BF16)
    kb_ = P.sb("sbk", [128, 2, S], BF16)
    stg = [P.sb("sbst%d" % i, [128, 2048], F32) for i in range(2)]
    load_heads_zpad(P, qb, zT, 1280, 0.125, stg)
    load_heads_bf(P, "sbk", kb_, zT, 1536, 1.0, stg)
    vb = P.sb("sbv", [128, 32, 256], BF16)
    P.dma(vb[:], vTM[:, 0:256].rearrange("(n p) c -> p n c", p=128))
    masks = make_masks(P, "sbm", True)
    tri = P.sb("sbtri", [128, 128], F32)
    P.memset(tri[:], 1.0)
    P.asel(tri[:], tri[:], [[-1, 128]], ALU.is_ge, 0.0, 0, 1)
    trib = P.sb("sbtrib", [128, 128], BF16)
    P.cp(trib[:], tri[:])
    onesb = P.sb("sbonesb", [128, 128], BF16)
    P.memset(onesb[:], 1.0)
    hi_ = [P.sb("sbhi%d" % i, [128, 512], BF16) for i in range(2)]
    lo_ = [P.sb("sblo%d" % i, [128, 512], BF16) for i in range(2)]
    carry = P.sb("sbcarry", [128, 512], F32)
    e_ = [P.sb("sbe%d" % i, [128, 512], F32) for i in range(2)]
    sp_ = [P.sb("sbsp%d" % i, [128, 512], F32) for i in range(2)]
    tm_ = [P.sb("sbtm%d" % i, [128, 512], F32) for i in range(2)]
    w_ = [P.sb("sbw%d" % i, [128, 512], BF16) for i in range(2)]
    yo = [P.sb("sbyo%d" % i, [128, 512], BF16) for i in range(2)]
    steps = []
    n = 0
    for h in range(4):
        for sb in range(8):
            kbs = list(range(4 * sb + 3, -1, -1))
            for idx, kb in enumerate(kbs):
                steps.append(dict(h=h, sb=sb, kb=kb, idx=idx, nk=len(kbs), n=n))
                n += 1

    def banks(n):
        return psb[n % 3], psb[3 + n % 2], psb[5 + n % 2]

    def cols(st):
        o = st["kb"] - 4 * st["sb"]
        q0 = 128 * o if o > 0 else 0
        return slice(q0, 512), slice(st["sb"] * 512 + q0, (st["sb"] + 1) * 512)

    def stage_a(st):
        h, sb, kb, n = st["h"], st["sb"], st["kb"], st["n"]
        t = h // 2
        cq, qs = cols(st)
        diag = kb >= 4 * sb
        pl, pc, pt = banks(n)
        e, sp, hi, lo = e_[n % 2], sp_[n % 2], hi_[n % 2], lo_[n % 2]
        P.mm(pl[:, cq], kb_[:, t, kb * 128:(kb + 1) * 128], qb[:, h, qs])
        P.act(e[:, cq], pl[:, cq], AF.Exp)
        P.act(sp[:, cq], e[:, cq], AF.Ln, bias=1.0)
        if diag:
            P.tt(sp[:, cq], sp[:, cq], masks[kb - 4 * sb][:, cq], ALU.mult, eng="pool")
        P.cp(hi[:, cq], sp[:, cq], eng="act")
        P.tt(lo[:, cq], sp[:, cq], hi[:, cq], ALU.subtract, eng="pool")

    def stage_a2(st):
        n = st["n"]
        cq, qs = cols(st)
        pl, pc, pt = banks(n)
        hi, lo = hi_[n % 2], lo_[n % 2]
        P.mm(pc[:, cq], trib[:], hi[:, cq], start=True, stop=False)
        P.mm(pc[:, cq], trib[:], lo[:, cq], start=False, stop=True)
        if st["idx"] < st["nk"] - 1:
            P.mm(pt[:, cq], onesb[:], hi[:, cq], start=True, stop=False)
            P.mm(pt[:, cq], onesb[:], lo[:, cq], start=False, stop=True)

    def stage_b1(st):
        h, sb, kb, n, idx, nk = st["h"], st["sb"], st["kb"], st["n"], st["idx"], st["nk"]
        diag = kb >= 4 * sb
        cq, qs = cols(st)
        pl, pc, pt = banks(n)
        tm, w = tm_[n % 2], w_[n % 2]
        if idx == 0:
            P.memset(carry[:], 0.0)
            P.cp(tm[:, cq], pc[:, cq])
        else:
            P.tt(tm[:, cq], pc[:, cq], carry[:, cq], ALU.add)
        P.tt(tm[:, cq], pl[:, cq], tm[:, cq], ALU.subtract)
        if diag:
            P.act(tm[:, cq], tm[:, cq], AF.Exp)
            P.tt(w[:, cq], tm[:, cq], masks[kb - 4 * sb][:, cq], ALU.mult, eng="pool")
        else:
            P.act(w[:, cq], tm[:, cq], AF.Exp)

    def stage_b2(st):
        h, sb, kb, n, idx, nk = st["h"], st["sb"], st["kb"], st["n"], st["idx"], st["nk"]
        t, pb = h // 2, (h % 2) * 64
        cq, _ = cols(st)
        qs = slice(sb * 512, (sb + 1) * 512)
        pl, pc, pt = banks(n)
        w = w_[n % 2]
        po = psb[7]
        P.mm(po[:, cq], vb[:, kb, t * 128:(t + 1) * 128], w[:, cq], start=(idx == 0), stop=(idx == nk - 1))
        if idx < nk - 1:
            P.tt(carry[:, cq], carry[:, cq], pt[:, cq], ALU.add)
        else:
            o = yo[(h * 8 + sb) % 2]
            P.cp(o[pb:pb + 64, :], po[pb:pb + 64, :], eng="act")
            P.dma(R(yT[512 + h * 64:512 + (h + 1) * 64, qs], "sb_%d_%d" % (h, sb)), o[pb:pb + 64, :])

    ns = len(steps)
    for i in range(ns + 2):
        if i >= 2:
            stage_b1(steps[i - 2])
        if i < ns:
            stage_a(steps[i])
        if 1 <= i <= ns:
            stage_a2(steps[i - 1])
        if i >= 2:
            stage_b2(steps[i - 2])
    P.pop()


def phase_ml(P, nc, prm, l, zT, yT, vTM, psb, ident, ev_eng):
    P.push()
    cw = P.sb("mlcw", [128, 4, 4], F32)
    cb = P.sb("mlcb", [128, 4], F32)
    for i in range(4):
        P.dma(cw[:, :, i], prm["ml_conv_w"][l, i].rearrange("(t p) -> p t", p=128))
    P.dma(cb[:], prm["ml_conv_b"][l].rearrange("(t p) -> p t", p=128), allow_slow_non_contiguous=True)
    qz = P.sb("mlqz", [128, 4, S], BF16)
    kk_ = P.sb("mlkk", [128, 2, S], BF16)
    P.memset(qz[:], 0.0)
    va = P.sb("mlva", [128, 32, 4, 128], BF16)
    P.memset(va[:], 0.0)
    P.memset(va[:, :, :, 64:65], 1.0)
    P.push()
    xin2 = [P.sb("mlxin%d" % i, [128, S + 3], F32) for i in range(2)]
    acc = P.sb("mlacc", [128, S], F32)
    vst = P.sb("mlvst", [128, 32, 256], BF16)
    P.dma(vst[:], vTM[:, 256:512].rearrange("(n p) c -> p n c", p=128))
    for h in range(4):
        P.cp(va[:, :, h, 0:64], vst[:, :, h * 64:(h + 1) * 64], eng=("pool" if h % 2 else "dve"))
    for i in range(2):
        P.memset(xin2[i][:, 0:3], 0.0)
    def ld_x(t):
        row0 = 2048 + t * 128
        for q in range(2):
            P.dma(xin2[t % 2][:, 3 + q * 2048:3 + (q + 1) * 2048], zT[row0:row0 + 128, q * 2048:(q + 1) * 2048])
    ld_x(0)
    ld_x(1)
    for t in range(4):
        xin = xin2[t % 2]
        P.ts(acc[:], xin[:, 3:S + 3], cw[:, t, 3:4], cb[:, t:t + 1], ALU.mult, ALU.add)
        for i in range(3):
            P.stt(acc[:], xin[:, i:S + i], cw[:, t, i:i + 1], acc[:], ALU.mult, ALU.add)
        P.act(acc[:], acc[:], AF.Silu)
        if t < 2:
            for gs in range(2):
                pb = gs * 64
                P.cp(qz[pb:pb + 64, 2 * t + gs, :], acc[pb:pb + 64, :], eng=("act" if gs else "pool"))
        else:
            P.act(kk_[:, t - 2, :], acc[:], AF.Copy, scale=0.125)
        if t + 2 < 4:
            ld_x(t + 2)
    P.pop()
    gi = P.sb("mlgi", [4, S], F32)
    gf = P.sb("mlgf", [4, S], F32)
    bi_ = P.sb("mlbi", [4, 1], F32)
    bf_ = P.sb("mlbf", [4, 1], F32)
    P.dma(gi[:], zT[3072:3076, :])
    P.dma(gf[:], zT[3076:3080, :])
    P.dma(bi_[:], prm["ml_b_i"][l].rearrange("(p o) -> p o", o=1))
    P.dma(bf_[:], prm["ml_b_f"][l].rearrange("(p o) -> p o", o=1))
    P.ts(bf_[:], bf_[:], -1.0, None, ALU.mult)
    P.act(gf[:], gf[:], AF.Exp, bias=bf_[:], scale=-1.0)
    P.act(gf[:], gf[:], AF.Ln, bias=1.0)
    P.ts(gf[:], gf[:], -1.0, None, ALU.mult)
    Fc = gf
    zer = P.sb("mlzer", [4, 512], F32)
    P.memset(zer[:], 0.0)
    for q in range(8):
        sl = slice(q * 512, (q + 1) * 512)
        if q == 0:
            P.scan(Fc[:, sl], gf[:, sl], zer[:], 0.0, ALU.add, ALU.add)
        else:
            P.scan(Fc[:, sl], gf[:, sl], zer[:], Fc[:, q * 512 - 1:q * 512], ALU.add, ALU.add)
    P.ts(gi[:], gi[:], bi_[:], None, ALU.add)
    P.tt(gi[:], gi[:], Fc[:], ALU.subtract)
    colb = P.sb("mlcolb", [128, 32, 4], F32)
    for kb in range(32):
        pt = psb[kb % 2]
        P.tr(pt[:, 0:4], gi[0:4, kb * 128:(kb + 1) * 128], ident[0:4, 0:4])
        P.cp(colb[:, kb, :], pt[:, 0:4], eng=ev_eng())
    sel = P.sb("mlsel", [4, 4, 128], F32)
    P.memset(sel[:], 1.0)
    P.asel(sel[:], sel[:], [[-1, 4], [0, 128]], ALU.is_equal, 0.0, 0, 1)
    masks = make_masks(P, "mlm", False)
    selden = P.sb("mlselden", [128, 64], F32)
    P.memset(selden[:], 0.0)
    P.memset(selden[64:65, :], 1.0)
    j64 = P.sb("mlj64", [64, 64], F32)
    P.memset(j64[:], 1.0 / 64)
    lng = P.sb("mllng", [64, 4], F32)
    P.dma(lng[:], prm["ml_ln_g"][l].rearrange("(h p) -> p h", p=64), allow_slow_non_contiguous=True)
    frow = [P.sb("mlfrow%d" % i, [128, 512], F32) for i in range(2)]
    da = [P.sb("mlda%d" % i, [128, 512], F32) for i in range(4)]
    pp = [P.sb("mlpp%d" % i, [128, 512], BF16) for i in range(4)]
    o65 = P.sb("mlo65", [128, 512], F32)
    hh_ = P.sb("mlhh", [64, 512], F32)
    t1 = P.sb("mlt1", [64, 512], F32)
    t2 = P.sb("mlt2", [64, 512], F32)
    og = P.sb("mlog", [64, 512], F32)
    yo = [P.sb("mlyo%d" % i, [64, 512], BF16) for i in range(2)]
    steps = []
    n = 0
    for h in range(4):
        for sb in range(8):
            nkb = 4 * sb + 4
            for kb in range(nkb):
                steps.append(dict(h=h, sb=sb, kb=kb, nkb=nkb, n=n))
                n += 1

    def stage_a(st):
        h, sb, kb, n = st["h"], st["sb"], st["kb"], st["n"]
        t = h // 2
        qs = slice(sb * 512, (sb + 1) * 512)
        diag = kb >= 4 * sb
        if kb == 0:
            pf = psb[4]
            P.mm(pf[:], sel[:, h, :], Fc[:, qs])
            P.cp(frow[(h * 8 + sb) % 2][:], pf[:], eng="act")
        fr = frow[(h * 8 + sb) % 2]
        pl = psb[n % 4]
        d_ = da[n % 4]
        o_ = kb - 4 * sb
        q0 = 128 * o_ if o_ > 0 else 0
        cq = slice(q0, 512)
        P.mm(pl[:, cq], kk_[:, t, kb * 128:(kb + 1) * 128], qz[:, h, sb * 512 + q0:(sb + 1) * 512])
        if diag:
            P.ts(d_[:, cq], fr[:, cq], colb[:, kb, h:h + 1], 30.0, ALU.add, ALU.min)
            P.act(d_[:, cq], d_[:, cq], AF.Exp)
            P.tt(d_[:, cq], d_[:, cq], masks[kb - 4 * sb][:, cq], ALU.mult, eng="pool")
        else:
            P.act(d_[:], fr[:], AF.Exp, bias=colb[:, kb, h:h + 1])

    def stage_b(st):
        h, sb, kb, n, nkb = st["h"], st["sb"], st["kb"], st["n"], st["nkb"]
        qs = slice(sb * 512, (sb + 1) * 512)
        pl = psb[n % 4]
        d_, p_ = da[n % 4], pp[n % 4]
        po = psb[5]
        o_ = kb - 4 * sb
        q0 = 128 * o_ if o_ > 0 else 0
        cq = slice(q0, 512)
        P.tt(p_[:, cq], pl[:, cq], d_[:, cq], ALU.mult)
        P.mm(po[:, cq], va[:, kb, h, :], p_[:, cq], start=(kb == 0), stop=(kb == nkb - 1))
        if kb == nkb - 1:
            P.cp(o65[:], po[:], eng="act")
            pending.extend(epilogue(h, sb))
        for _ in range(2):
            if pending:
                pending.pop(0)()

    def epilogue(h, sb):
        qs = slice(sb * 512, (sb + 1) * 512)
        pd, pm = psb[6], psb[7]
        o = yo[(h * 8 + sb) % 2]
        return [
            lambda: P.dma(og[:], zT[2816 + h * 64:2816 + (h + 1) * 64, qs]),
            lambda: P.mm(pd[0:64, :], selden[:], o65[:]),
            lambda: P.act(t1[:], pd[0:64, :], AF.Abs),
            lambda: P.ts(t1[:], t1[:], 1.0, None, ALU.max),
            lambda: P.recip(t1[:], t1[:]),
            lambda: P.tt(hh_[:], o65[0:64, :], t1[:], ALU.mult),
            lambda: P.mm(pm[0:64, :], j64[:], hh_[:]),
            lambda: P.tt(hh_[:], hh_[:], pm[0:64, :], ALU.subtract),
            lambda: P.tt(t1[:], hh_[:], hh_[:], ALU.mult, eng="pool"),
            lambda: P.mm(pd[0:64, :], j64[:], t1[:]),
            lambda: P.ts(t2[:], pd[0:64, :], LN_EPS, None, ALU.add),
            lambda: P.act(t2[:], t2[:], AF.Sqrt),
            lambda: P.recip(t2[:], t2[:]),
            lambda: P.tt(hh_[:], hh_[:], t2[:], ALU.mult),
            lambda: P.act(og[:], og[:], AF.Sigmoid),
            lambda: P.stt(o[:], hh_[:], lng[:, h:h + 1], og[:], ALU.mult, ALU.mult),
            lambda: P.dma(R(yT[768 + h * 64:768 + (h + 1) * 64, qs], "ml_%d_%d" % (h, sb)), o[:]),
        ]

    pending = []
    for i, st in enumerate(steps):
        stage_a(st)
        if i >= 2:
            stage_b(steps[i - 2])
    stage_b(steps[-2])
    stage_b(steps[-1])
    while pending:
        pending.pop(0)()
    P.pop()


def phase_rw(P, nc, prm, l, zT, yT, vrw, grw, yrw, psb, ident, identb, ev_eng):
    RW0 = 256
    H = 2048
    P.push()
    def col2(name, src):
        t = P.sb(name, [128, 2], F32)
        P.dma(t[:], src.rearrange("(k p) -> p k", p=128), allow_slow_non_contiguous=True)
        return t
    mu = P.sb("rwmu", [128, 8], F32)
    P.dma(mu[:], prm["rw_mu"][l].rearrange("(k p) -> p k", p=128), allow_slow_non_contiguous=True)
    w0 = col2("rww0", prm["rw_w0"][l])
    a0 = col2("rwa0", prm["rw_a0"][l])
    kkc = col2("rwkk", prm["rw_k_k"][l])
    kac = col2("rwka", prm["rw_k_a"][l])
    rkc = col2("rwrk", prm["rw_r_k"][l].rearrange("h d -> (h d)"))
    omka = P.sb("rwomka", [128, 2], F32)
    P.ts(omka[:], kac[:], -1.0, 1.0, ALU.mult, ALU.add)
    stw = P.sb("rwstw", [128, 256], F32)
    w2b = P.sb("rww2b", [128, 256], BF16)
    P.dma(stw[0:64, :], prm["rw_w2"][l])
    P.dma(stw[64:128, :], prm["rw_a2"][l])
    P.cp(w2b[:], stw[:])
    stg2 = P.sb("rwstg2", [128, 256], F32)
    g2b = P.sb("rwg2b", [128, 256], BF16)
    P.dma(stg2[:], prm["rw_g2"][l])
    P.cp(g2b[:], stg2[:])
    lngr = P.sb("rwlng", [128, 256], F32)
    lnbr = P.sb("rwlnb", [128, 256], F32)
    P.dma(lngr[:], prm["rw_ln_g"][l].partition_broadcast(128))
    P.dma(lnbr[:], prm["rw_ln_b"][l].partition_broadcast(128))
    bo = P.sb("rwbo", [128, 128], F32)
    P.memset(bo[:], 0.0)
    P.memset(bo[0:64, 0:64], 1.0)
    P.memset(bo[64:128, 64:128], 1.0)
    hsel = P.sb("rwhsel", [128, 2], F32)
    P.memset(hsel[:], 0.0)
    P.memset(hsel[0:64, 0:1], 1.0)
    P.memset(hsel[64:128, 1:2], 1.0)
    cmask = P.sb("rwcmask", [128, H], F32)
    P.memset(cmask[:], 1.0)
    P.memset(cmask[:].rearrange("p (c t) -> p c t", t=64)[:, :, 0:1], 0.0)
    rt = P.sb("rwrt", [128, 2, S], BF16)
    kt = P.sb("rwkt", [128, 2, S], BF16)
    bt = P.sb("rwbt", [128, 2, S], BF16)
    at = P.sb("rwat", [128, 2, S], BF16)
    gend = P.sb("rwgend", [128, 2, 64], F32)
    rk = P.sb("rwrk_tm", [128, 32, 4], F32)
    lora = P.sb("rwlora", [128, S], BF16)
    sgb = P.sb("rwsgb", [128, S], BF16)
    P.push()
    buf = P.sb("rwbuf", [128, H + 1], F32)
    T = [P.sb("rwT%d" % i, [128, H], F32) for i in range(8)]

    def load_shift(dst, tile_idx, hf):
        r0 = RW0 + tile_idx * 128
        if hf == 0:
            P.memset(buf[:, 0:1], 0.0)
            P.dma(buf[:, 1:H + 1], zT[r0:r0 + 128, 0:H])
        else:
            P.dma(buf[:, 0:H + 1], zT[r0:r0 + 128, H - 1:2 * H])
        P.tt(dst, buf[:, 0:H], buf[:, 1:H + 1], ALU.subtract, eng="pool")
        P.stt(dst, dst, mu[:, tile_idx:tile_idx + 1], buf[:, 1:H + 1], ALU.mult, ALU.add)

    for hf in range(2):
        hs = slice(hf * H, (hf + 1) * H)
        load_shift(T[0][:], 6, hf)
        P.act(lora[0:64, hs], T[0][0:64, :], AF.Tanh)
        P.cp(lora[64:128, hs], T[0][64:128, :])
        load_shift(T[0][:], 7, hf)
        P.act(sgb[:, hs], T[0][:], AF.Sigmoid)
    n = 0
    import os
    RWSUB = int(os.environ.get("RWSUB", "9"))
    for hp in range(2 if RWSUB > 0 else 0):
        for hf in range(2):
            Tr, Tk, Ta, Tlw, Tkk, TG, Tt, Te = [t[:] for t in T]
            hs = slice(hf * H, (hf + 1) * H)
            load_shift(Tr, hp, hf)
            load_shift(Tk, 2 + hp, hf)
            for tb in range(4):
                bs = slice(tb * 512, (tb + 1) * 512)
                gs_ = slice(hf * H + tb * 512, hf * H + (tb + 1) * 512)
                pw, pa = psb[(2 * n) % 4], psb[(2 * n + 1) % 4]
                n += 1
                P.mm(pw[:], w2b[0:64, hp * 128:(hp + 1) * 128], lora[0:64, gs_])
                P.mm(pa[:], w2b[64:128, hp * 128:(hp + 1) * 128], lora[64:128, gs_])
                P.act(Tlw[:, bs], pw[:], AF.Sigmoid, bias=w0[:, hp:hp + 1])
                P.act(Ta[:, bs], pa[:], AF.Sigmoid, bias=a0[:, hp:hp + 1])
            if RWSUB <= 1:
                continue
            P.ts(Tlw, Tlw, -0.6065306597126334, None, ALU.mult)
            P.act(Tkk, Tk, AF.Copy, scale=kkc[:, hp:hp + 1])
            P.tt(Tt, Tkk, Tkk, ALU.mult, eng="pool")
            for tb in range(4):
                bs = slice(tb * 512, (tb + 1) * 512)
                ps = psb[4 + tb % 2]
                P.mm(ps[:], bo[:], Tt[:, bs])
                P.act(Te[:, bs], ps[:], AF.Sqrt)
            P.ts(Te, Te, 1e-12, None, ALU.max)
            P.recip(Te, Te)
            P.tt(Tkk, Tkk, Te, ALU.mult)
            if RWSUB <= 2:
                continue
            P.ts(Tt, Ta, kac[:, hp:hp + 1], omka[:, hp:hp + 1], ALU.mult, ALU.add)
            P.tt(Tk, Tk, Tt, ALU.mult)
            P.stt(Tt, Tr, rkc[:, hp:hp + 1], Tk, ALU.mult, ALU.mult)
            for t16 in range(16):
                tt_ = hf * 16 + t16
                ps = psb[6 + t16 % 2]
                P.mm(ps[:, 0:2], Tt[:, t16 * 128:(t16 + 1) * 128], hsel[:])
                P.cp(rk[:, tt_, 2 * hp:2 * hp + 2], ps[:, 0:2], eng=ev_eng())
            if RWSUB <= 3:
                continue
            P.scan(TG, cmask[:], Tlw, 0.0, ALU.mult, ALU.add)
            if RWSUB <= 4:
                continue
            P.act(Te, TG, AF.Exp)
            P.tt(rt[:, hp, hs], Tr, Te, ALU.mult)
            if RWSUB <= 5:
                continue
            P.cp(gend[:, hp, hf * 32:(hf + 1) * 32], Te.rearrange("p (c t) -> p c t", t=64)[:, :, 63], eng="pool")
            if RWSUB <= 6:
                continue
            P.tt(Tt, TG, Tlw, ALU.subtract, eng="pool")
            P.act(Tt, Tt, AF.Exp)
            P.stt(at[:, hp, hs], Tkk, -1.0, Tt, ALU.mult, ALU.mult)
            P.act(Te, TG, AF.Exp, scale=-1.0)
            P.tt(kt[:, hp, hs], Tk, Te, ALU.mult)
            P.tt(Tt, Tkk, Ta, ALU.mult, eng="pool")
            P.tt(bt[:, hp, hs], Tt, Te, ALU.mult)
    P.pop()
    import os
    if os.environ.get("DUMPT"):
        outd = nc_out_handle[0][:, :].rearrange("(p a) d -> p (a d)", p=128)
        dst_ = [P.sb("dst%d" % i, [128, 2048], F32) for i in range(2)]
        n_ = 0
        for ai, arr in enumerate((rt, kt, bt, at)):
            for hp in range(2):
                for q in range(2):
                    d_ = dst_[n_ % 2]
                    P.cp(d_[:], arr[:, hp, q * 2048:(q + 1) * 2048])
                    off = ai * 8192 + hp * 4096 + q * 2048
                    P.dma(R(outd[:, off:off + 2048], n_), d_[:])
                    n_ += 1
        P.pop()
        return
    RWSTOP = int(os.environ.get("RWSTOP", "9"))
    if RWSTOP <= 1:
        P.pop()
        return
    gst = [P.sb("rwgst%d" % i, [128, 256], F32) for i in range(2)]
    for tt_ in range(32):
        ps = psb[tt_ % 2]
        P.mm(ps[:, 0:256], sgb[:, tt_ * 128:(tt_ + 1) * 128], g2b[:])
        P.cp(gst[tt_ % 2][:], ps[:, 0:256], eng=ev_eng())
        P.dma(R(grw[tt_ * 128:(tt_ + 1) * 128, :], tt_), gst[tt_ % 2][:])
    odd = {}
    for nm, arr in (("r", rt), ("k", kt), ("b", bt), ("a", at)):
        o_ = P.sb("rwodd_" + nm, [64, 2, S], BF16)
        P.dma(o_[:], arr[64:128, :, :])
        odd[nm] = o_
    gall = P.sb("rwgall", [64, 4, 64], F32)
    for hp in range(2):
        P.cp(gall[:, 2 * hp, :], gend[0:64, hp, :], eng="pool")
        P.dma(gall[:, 2 * hp + 1, :], gend[64:128, hp, :])

    def fm(nm, arr, h, cs):
        return arr[0:64, h // 2, cs] if h % 2 == 0 else odd[nm][:, h // 2, cs]

    def mask_n(name, specs):
        n_ = len(specs)
        m = P.sb(name, [64, n_ * 4, 64], F32)
        P.memset(m[:], 1.0)
        for i, sp_ in enumerate(specs):
            if sp_ is None:
                continue
            cm, step, cmp = sp_
            P.asel(m[:, 4 * i:4 * i + 4, :], m[:, 4 * i:4 * i + 4, :], [[0, 4], [step, 64]], cmp, 0.0, 0, cm)
        return m
    S_MU = (-1, 1, ALU.is_gt)
    S_MUI = (-1, 1, ALU.is_ge)
    S_ML = (1, -1, ALU.is_gt)
    M0 = mask_n("rwM0", [S_MU, S_ML])
    M1 = mask_n("rwM1", [S_MU, S_MUI])
    M2 = mask_n("rwM2", [S_MUI, None])
    I4 = mask_n("rwI4", [(1, -1, ALU.is_equal)])
    M32 = P.sb("rwM32", [64, 4, 64], F32)
    Mb = P.sb("rwMb", [64, 4, 64], BF16)
    P.memset(M32[:], 0.0)
    P.memset(Mb[:], 0.0)
    NN = [P.sb("rwNN%d" % i, [64, 8, 64], BF16) for i in range(2)]
    AR = [P.sb("rwAR%d" % i, [64, 8, 64], BF16) for i in range(2)]
    RB = [P.sb("rwRB%d" % i, [64, 8, 64], BF16) for i in range(2)]
    KT = [P.sb("rwKT%d" % i, [64, 4, 64], BF16) for i in range(2)]
    Nk = [P.sb("rwNk%d" % i, [64, 4, 64], BF16) for i in range(2)]
    NkT = [P.sb("rwNkT%d" % i, [64, 4, 64], BF16) for i in range(2)]
    P32 = P.sb("rwP32", [64, 4, 64], F32)
    Pb = [P.sb("rwPb%d" % i, [64, 4, 64], BF16) for i in range(2)]
    V32 = [P.sb("rwV32%d" % i, [64, 4, 64], F32) for i in range(2)]
    Vb = [P.sb("rwVb%d" % i, [64, 4, 64], BF16) for i in range(2)]
    Xb = P.sb("rwXb", [64, 4, 64], BF16)
    Ub = P.sb("rwUb", [64, 4, 64], BF16)
    Yc = [P.sb("rwYc%d" % i, [64, 4, 64], F32) for i in range(2)]
    B0, B1, B2, B3, B4, B5, B6 = psb[0], psb[1], psb[2], psb[3], psb[4], psb[5], psb[6]

    def pvn(bank, lo, n_):
        return bank[0:64, lo:lo + 64 * n_].rearrange("p (a t) -> p a t", a=n_)

    def hc(h, lo=0):
        return slice(lo + h * 64, lo + (h + 1) * 64)

    def rw_pre(c):
        cs = slice(c * 64, (c + 1) * 64)
        nn, ar, rb, kt_ = NN[c % 2], AR[c % 2], RB[c % 2], KT[c % 2]
        pb_ = Pb[c % 2]

        def init():
            for h in range(4):
                r_, k_, b_, a_ = fm("r", rt, h, cs), fm("k", kt, h, cs), fm("b", bt, h, cs), fm("a", at, h, cs)
                idn = identb[0:64, 0:64]
                P.mm(B0[0:64, hc(h)], b_, a_)
                P.mm(B0[0:64, hc(h, 256)], a_, b_)
                P.mm(B1[0:64, hc(h)], k_, a_)
                P.mm(B1[0:64, hc(h, 256)], b_, r_)
                P.mm(B2[0:64, hc(h)], k_, r_)
                P.mm(B2[0:64, hc(h, 256)], b_, idn)
                P.mm(B3[0:64, hc(h)], k_, idn)
            P.tt(nn[:], pvn(B0, 0, 8), M0[:], ALU.mult)
            P.tt(P32[:], nn[:, 0:4, :], I4[:], ALU.add)
            P.cp(pb_[:], P32[:], eng="dve")
            P.tt(ar[:], pvn(B1, 0, 8), M1[:], ALU.mult)
            P.tt(rb[:], pvn(B2, 0, 8), M2[:], ALU.mult)
            P.cp(kt_[:], pvn(B3, 0, 4), eng="dve")

        def stage(i):
            def f():
                last = (i == 5)
                nxt = i % 2
                curN, curT = (nn[:, 0:4, :], nn[:, 4:8, :]) if i == 1 else (Nk[1 - nxt][:], NkT[1 - nxt][:])
                for h in range(4):
                    if not last:
                        P.mm(B4[0:64, hc(h)], curT[:, h, :], curN[:, h, :])
                    P.mm(B4[0:64, hc(h, 256)], curN[:, h, :], curT[:, h, :])
                P.cp(NkT[nxt][:], pvn(B4, 256, 4), eng="dve")
                if not last:
                    P.cp(Nk[nxt][:], pvn(B4, 0, 4), eng="dve")
                for h in range(4):
                    P.mm(B5[0:64, hc(h)], NkT[nxt][:, h, :], pb_[:, h, :])
                P.tt(P32[:], P32[:], pvn(B5, 0, 4), ALU.add)
                P.cp(pb_[:], P32[:], eng="dve")
            return f
        return [init] + [stage(i) for i in range(1, 6)]

    def rw_post(c):
        cs = slice(c * 64, (c + 1) * 64)
        ar, rb, kt_, pb_ = AR[c % 2], RB[c % 2], KT[c % 2], Pb[c % 2]
        v32, vb_ = V32[c % 2], Vb[c % 2]
        yc = Yc[c % 2]

        def fx():
            P.dma(v32[:], vrw[c * 64:(c + 1) * 64, :].rearrange("t (h v) -> t h v", h=4))
            P.cp(vb_[:], v32[:], eng="pool")
            for h in range(4):
                P.mm(B6[0:64, hc(h)], fm("a", at, h, cs), Mb[:, h, :], start=True, stop=False)
                P.mm(B6[0:64, hc(h)], ar[:, h, :], vb_[:, h, :], start=False, stop=True)
            P.cp(Xb[:], pvn(B6, 0, 4), eng="dve")

        def fu():
            for h in range(4):
                P.mm(B7[0:64, hc(h)], pb_[:, h, :], Xb[:, h, :])
            P.cp(Ub[:], pvn(B7, 0, 4), eng="dve")

        def fm_():
            for h in range(4):
                P.mm(B3[0:64, hc(h, 256)], rb[:, 4 + h, :], Ub[:, h, :], start=True, stop=False)
                P.mm(B3[0:64, hc(h, 256)], kt_[:, h, :], vb_[:, h, :], start=False, stop=True)
            P.tt(M32[:], M32[:], pvn(B3, 256, 4), ALU.add)
            for h in range(4):
                P.ts(M32[:, h, :], M32[:, h, :], gall[:, h, c:c + 1], None, ALU.mult)
            P.cp(Mb[:], M32[:], eng="dve")

        def fy():
            for h in range(4):
                P.mm(B6[0:64, hc(h, 256)], fm("r", rt, h, cs), Mbp[:, h, :], start=True, stop=False)
                P.mm(B6[0:64, hc(h, 256)], ar[:, 4 + h, :], Ub[:, h, :], start=False, stop=False)
                P.mm(B6[0:64, hc(h, 256)], rb[:, h, :], vb_[:, h, :], start=False, stop=True)
            P.cp(yc[:], pvn(B6, 256, 4), eng="pool" if False else "dve")
            P.dma(R(yrw[c * 64:(c + 1) * 64, :].rearrange("t (h v) -> t h v", h=4), c), yc[:])
        return [fx, fu, fy, fm_]

    B7 = psb[7]
    Mbp = Mb
    NCH = 64
    for f in rw_pre(0):
        f()
    for c in range(NCH):
        pre = rw_pre(c + 1) if c + 1 < NCH else []
        post = rw_post(c)
        order = []
        for i in range(max(len(pre), len(post))):
            if i < len(post):
                order.append(post[i])
            if i < len(pre):
                order.append(pre[i])
        for f in order:
            f()
    P.barrier()
    if RWSTOP <= 2:
        P.pop()
        return
    yin = [P.sb("rwyin%d" % i, [128, 4, 64], F32) for i in range(2)]
    vin = [P.sb("rwvin%d" % i, [128, 4, 64], F32) for i in range(2)]
    gin = [P.sb("rwgin%d" % i, [128, 256], F32) for i in range(2)]
    sq = P.sb("rwsq", [128, 4, 64], F32)
    st4 = P.sb("rwst4", [128, 4], F32)
    st5 = P.sb("rwst5", [128, 4], F32)
    yo = [P.sb("rwyo%d" % i, [128, 2, 512], BF16) for i in range(2)]
    for tt_ in range(32):
        y_, v_, g_ = yin[tt_ % 2], vin[tt_ % 2], gin[tt_ % 2]
        ts_ = slice(tt_ * 128, (tt_ + 1) * 128)
        P.dma(y_[:], yrw[ts_, :].rearrange("t (h d) -> t h d", h=4))
        P.dma(v_[:], vrw[ts_, :].rearrange("t (h d) -> t h d", h=4))
        P.dma(g_[:], grw[ts_, :])
        P.op("dve", lambda e, o=st4[:], i=y_[:]: e.reduce_sum(o, i, AX.X), [y_[:]], [st4[:]])
        P.ts(st4[:], st4[:], -1.0 / 64, None, ALU.mult)
        for h in range(4):
            P.ts(y_[:, h, :], y_[:, h, :], st4[:, h:h + 1], None, ALU.add, eng=("pool" if h % 2 else "dve"))
        P.tt(sq[:], y_[:], y_[:], ALU.mult, eng="pool")
        P.op("dve", lambda e, o=st5[:], i=sq[:]: e.reduce_sum(o, i, AX.X), [sq[:]], [st5[:]])
        P.ts(st5[:], st5[:], 1.0 / 64, 64e-5, ALU.mult, ALU.add)
        P.act(st5[:], st5[:], AF.Sqrt)
        P.recip(st5[:], st5[:])
        for h in range(4):
            P.ts(y_[:, h, :], y_[:, h, :], st5[:, h:h + 1], None, ALU.mult, eng=("pool" if h % 2 else "dve"))
        yf = y_[:].rearrange("p h d -> p (h d)")
        P.tt(yf, yf, lngr[:], ALU.mult)
        P.tt(yf, yf, lnbr[:], ALU.add, eng="pool")
        for h in range(4):
            P.stt(y_[:, h, :], v_[:, h, :], rk[:, tt_, h:h + 1], y_[:, h, :], ALU.mult, ALU.add)
        P.tt(yf, yf, g_[:], ALU.mult)
        o = yo[(tt_ // 4) % 2]
        for t in range(2):
            pt = psb[(2 * tt_ + t) % 4]
            P.tr(pt[:, 0:128], yf[:, t * 128:(t + 1) * 128], ident[:])
            P.cp(o[:, t, (tt_ % 4) * 128:(tt_ % 4 + 1) * 128], pt[:, 0:128], eng=ev_eng())
        if tt_ % 4 == 3:
            tb = tt_ // 4
            P.dma(R(yT[256:512, tb * 512:(tb + 1) * 512].rearrange("(k p) t -> p k t", p=128), "rw_%d" % tb), o[:])
    P.pop()


def kernel(**inputs):
    dbg = inputs.pop("_dbg", None)
    ncores = inputs.pop("_ncores", 8)
    nlayers = inputs.pop("_nlayers", DEPTH)
    mixers = inputs.pop("_mixers", ("s5", "rw", "sb", "ml"))
    nc = build_program(dbg, nlayers, mixers)
    x = np.ascontiguousarray(inputs["x"], dtype=np.float32)
    shared = {n: np.ascontiguousarray(inputs[n], dtype=np.float32) for n, _ in PARAMS}
    in_maps = []
    for c in range(ncores):
        m = {"x": x[c]}
        m.update(shared)
        in_maps.append(m)
    res = run_bass_kernel_spmd(nc, in_maps, core_ids=list(range(ncores)))
    return np.stack([r["out"] for r in res.results], axis=0)
```
